# Optimizing a Trainium2 kernel written in Bass

```python
import math
import jax, jax.numpy as jnp
from jax import lax
import numpy as np

D_MODEL = 2048
BATCH = 16
SEQ = 2048
DEPTH = 2
DEC_BATCH = 32
DEC_SEQ = 32
PAST_LEN = 2048

CHUNK = 64
Q_BLOCK = 128
HEAD_DIM = 128
SELF_WIDTH = 1536
DIFF_HEADS = SELF_WIDTH // (2 * HEAD_DIM)
DIFF_VDIM = 2 * HEAD_DIM
SB_HEADS = SELF_WIDTH // HEAD_DIM
MEM_HEADS = 4
MEM_WIDTH = MEM_HEADS * HEAD_DIM
MIX_WIDTH = SELF_WIDTH + MEM_WIDTH
N_MEM = 256
IN_WIDTH = 3 * SELF_WIDTH + MEM_WIDTH
N_BUCKETS = 32
MAX_DISTANCE = 128
PEER_HEADS = 8
PEER_QDIM = 256
N_KEYS = 128
N_EXPERTS = N_KEYS * N_KEYS
PEER_TOPK = 16
PEER_BLOCK = 256
N_MIXERS = 2
N_DIFF = (DEPTH + 1) // 2
ALPHA = (2 * DEPTH) ** 0.25
BETA = (8 * DEPTH) ** -0.25
LN_EPS = 1e-5

kernel_name = "hybrid_diffattn_stickbreak_peer_stream_step"


def layer_norm(x, g, b):
    xf = x.astype(jnp.float32)
    mu = jnp.mean(xf, axis=-1, keepdims=True)
    var = jnp.mean(jnp.square(xf - mu), axis=-1, keepdims=True)
    return ((xf - mu) * lax.rsqrt(var + LN_EPS) * g.astype(jnp.float32) + b.astype(jnp.float32)).astype(x.dtype)


def head_rms(o, g):
    of = o.astype(jnp.float32)
    return (of * lax.rsqrt(jnp.mean(jnp.square(of), axis=-1, keepdims=True) + LN_EPS) * g.astype(jnp.float32)).astype(o.dtype)


def t5_bucket(rel):
    nb = N_BUCKETS // 2
    max_exact = nb // 2
    ret = jnp.where(rel > 0, nb, 0)
    n = jnp.abs(rel)
    nf = jnp.maximum(n, 1).astype(jnp.float32)
    large = max_exact + (jnp.log(nf / max_exact) / math.log(MAX_DISTANCE / max_exact) * (nb - max_exact)).astype(jnp.int32)
    large = jnp.minimum(large, nb - 1)
    return ret + jnp.where(n < max_exact, n, large)


def diff_attend_block(q, k, v, q_pos, k_pos, rel_table, lam):
    logits = jnp.einsum('bqhcd,bkhcd->bchqk', q, k, preferred_element_type=jnp.float32) * (HEAD_DIM ** -0.5)
    bias = jnp.transpose(rel_table.astype(jnp.float32)[t5_bucket(k_pos[None, :] - q_pos[:, None])], (2, 0, 1))
    visible = (k_pos[None, :] // CHUNK) <= (q_pos[:, None] // CHUNK)
    logits = jnp.where(visible, logits + bias[None, None], -jnp.inf)
    p = jax.nn.softmax(logits, axis=-1)
    w = p[:, 0] - lam * p[:, 1]
    return jnp.einsum('bhqk,bkhe->bqhe', w.astype(v.dtype), v)


def sb_attend_block(q, k, v, q_pos, k_pos):
    z = jnp.einsum('bqhd,bkhd->bhqk', q, k, preferred_element_type=jnp.float32) * (HEAD_DIM ** -0.5)
    before = k_pos[None, :] < q_pos[:, None]
    log_keep = jnp.where(before, jax.nn.log_sigmoid(-z), 0.0)
    between = lax.cumsum(log_keep, axis=3, reverse=True) - log_keep
    a = jnp.where(before, jnp.exp(jax.nn.log_sigmoid(z) + between), 0.0)
    return jnp.einsum('bhqk,bkhd->bqhd', a.astype(v.dtype), v)


def sweep_queries(fn, q, q_pos):
    B, T = q.shape[:2]
    if T <= Q_BLOCK:
        return fn(q, q_pos)
    nb = T // Q_BLOCK
    qb = jnp.moveaxis(q.reshape((B, nb, Q_BLOCK) + q.shape[2:]), 1, 0)
    pb = q_pos.reshape(nb, Q_BLOCK)
    out = lax.map(lambda a: fn(a[0], a[1]), (qb, pb))
    out = jnp.moveaxis(out, 0, 1)
    return out.reshape((B, T) + out.shape[3:])


def self_mixer(kind, layer, q, k, v, q_pos, k_pos, rel_table, lam_vec, subln_g):
    B, Tq = q.shape[:2]
    Tk = k.shape[1]
    if kind == 0:
        lam_init = 0.8 - 0.6 * math.exp(-0.3 * layer)
        lp = lam_vec.astype(jnp.float32)
        lam = jnp.exp(jnp.sum(lp[0] * lp[1])) - jnp.exp(jnp.sum(lp[2] * lp[3])) + lam_init
        qh = q.reshape(B, Tq, DIFF_HEADS, 2, HEAD_DIM)
        kh = k.reshape(B, Tk, DIFF_HEADS, 2, HEAD_DIM)
        vh = v.reshape(B, Tk, DIFF_HEADS, DIFF_VDIM)
        o = sweep_queries(lambda qb, pb: diff_attend_block(qb, kh, vh, pb, k_pos, rel_table, lam), qh, q_pos)
        o = head_rms(o, subln_g) * (1.0 - lam_init)
    else:
        qh = q.reshape(B, Tq, SB_HEADS, HEAD_DIM)
        kh = k.reshape(B, Tk, SB_HEADS, HEAD_DIM)
        vh = v.reshape(B, Tk, SB_HEADS, HEAD_DIM)
        o = sweep_queries(lambda qb, pb: sb_attend_block(qb, kh, vh, pb, k_pos), qh, q_pos)
    return o.reshape(B, Tq, SELF_WIDTH)


def mem_attend(mq, mk, mv):
    B, T = mq.shape[:2]
    qh = mq.reshape(B, T, MEM_HEADS, HEAD_DIM)
    logits = jnp.einsum('bqhd,bmhd->bhqm', qh, mk, preferred_element_type=jnp.float32) * (HEAD_DIM ** -0.5)
    p = jax.nn.softmax(logits, axis=-1)
    return jnp.einsum('bhqm,bmhd->bqhd', p.astype(mv.dtype), mv).reshape(B, T, MEM_WIDTH)


def peer_ffn(x, w_q, sub_keys, u_tab, v_tab):
    shp = x.shape
    xt = x.reshape(-1, D_MODEL)
    T = xt.shape[0]
    q = (xt @ w_q).reshape(T, PEER_HEADS, 2, PEER_QDIM // 2)
    s = jnp.einsum('thcd,hcnd->thcn', q, sub_keys, preferred_element_type=jnp.float32)
    sv, si = lax.top_k(s, PEER_TOPK)
    cand = (sv[:, :, 0, :, None] + sv[:, :, 1, None, :]).reshape(T, PEER_HEADS, PEER_TOPK * PEER_TOPK)
    cidx = (si[:, :, 0, :, None] * N_KEYS + si[:, :, 1, None, :]).reshape(T, PEER_HEADS, PEER_TOPK * PEER_TOPK)
    best, pos = lax.top_k(cand, PEER_TOPK)
    experts = jnp.take_along_axis(cidx, pos, axis=-1).reshape(T, PEER_HEADS * PEER_TOPK)
    gates = jax.nn.softmax(best, axis=-1).reshape(T, PEER_HEADS * PEER_TOPK)
    n_blk = -(-T // PEER_BLOCK)
    pad = n_blk * PEER_BLOCK - T
    xb = jnp.pad(xt, ((0, pad), (0, 0))).reshape(n_blk, PEER_BLOCK, D_MODEL)
    eb = jnp.pad(experts, ((0, pad), (0, 0))).reshape(n_blk, PEER_BLOCK, PEER_HEADS * PEER_TOPK)
    gb = jnp.pad(gates, ((0, pad), (0, 0))).reshape(n_blk, PEER_BLOCK, PEER_HEADS * PEER_TOPK)

    def block(args):
        xs, es, gs = args
        act = jnp.einsum('td,tkd->tk', xs, u_tab[es], preferred_element_type=jnp.float32)
        h = (jax.nn.gelu(act, approximate=False) * gs).astype(xs.dtype)
        return jnp.einsum('tk,tkd->td', h, v_tab[es])

    out = lax.map(block, (xb, eb, gb)).reshape(n_blk * PEER_BLOCK, D_MODEL)[:T]
    return out.reshape(shp)


def split_proj(proj):
    s = SELF_WIDTH
    return proj[..., :s], proj[..., s:2 * s], proj[..., 2 * s:3 * s], proj[..., 3 * s:]


def setup_inputs(seed: int = 0) -> dict:
    key = jax.random.key(seed)
    ks = jax.random.split(key, 20)

    def nrm(k, shape, scale):
        return jax.random.normal(k, shape, jnp.float32) * scale

    return {
        "x_prompt": nrm(ks[0], (BATCH, SEQ, D_MODEL), 1.0),
        "x_sample": nrm(ks[1], (DEC_BATCH, DEC_SEQ, D_MODEL), 1.0),
        "cache_self_k": nrm(ks[2], (DEPTH, DEC_BATCH, PAST_LEN, SELF_WIDTH), 1.0),
        "cache_self_v": nrm(ks[3], (DEPTH, DEC_BATCH, PAST_LEN, SELF_WIDTH), 1.0),
        "cache_mem_k": nrm(ks[4], (DEPTH, DEC_BATCH, N_MEM, MEM_HEADS, HEAD_DIM), 1.0),
        "cache_mem_v": nrm(ks[5], (DEPTH, DEC_BATCH, N_MEM, MEM_HEADS, HEAD_DIM), 1.0),
        "mem_prompt": nrm(ks[6], (BATCH, N_MEM, D_MODEL), 1.0),
        "w_in": nrm(ks[7], (DEPTH, D_MODEL, IN_WIDTH), D_MODEL ** -0.5),
        "w_o": nrm(ks[8], (DEPTH, MIX_WIDTH, D_MODEL), BETA * MIX_WIDTH ** -0.5),
        "w_mem_k": nrm(ks[9], (DEPTH, D_MODEL, MEM_WIDTH), D_MODEL ** -0.5),
        "w_mem_v": nrm(ks[10], (DEPTH, D_MODEL, MEM_WIDTH), D_MODEL ** -0.5),
        "rel_bias_table": nrm(ks[11], (N_BUCKETS, DIFF_HEADS), 0.5),
        "diff_lambda": nrm(ks[12], (N_DIFF, 4, HEAD_DIM), 0.1),
        "diff_subln_g": 1.0 + nrm(ks[13], (N_DIFF, DIFF_VDIM), 0.02),
        "ln_g": 1.0 + nrm(ks[14], (DEPTH, 2, D_MODEL), 0.02),
        "ln_b": nrm(ks[15], (DEPTH, 2, D_MODEL), 0.02),
        "peer_w_q": nrm(ks[16], (DEPTH, D_MODEL, PEER_HEADS * PEER_QDIM), D_MODEL ** -0.5),
        "peer_sub_keys": nrm(ks[17], (DEPTH, PEER_HEADS, 2, N_KEYS, PEER_QDIM // 2), (PEER_QDIM // 2) ** -0.5),
        "peer_u": nrm(ks[18], (DEPTH, N_EXPERTS, D_MODEL), D_MODEL ** -0.5),
        "peer_v": nrm(ks[19], (DEPTH, N_EXPERTS, D_MODEL), BETA * PEER_HEADS ** -0.5),
    }


def reference(x_prompt, x_sample, cache_self_k, cache_self_v, cache_mem_k, cache_mem_v, mem_prompt,
              w_in, w_o, w_mem_k, w_mem_v, rel_bias_table, diff_lambda, diff_subln_g,
              ln_g, ln_b, peer_w_q, peer_sub_keys, peer_u, peer_v):
    bp, n_prompt = x_prompt.shape[:2]
    n_new = x_sample.shape[1]
    n_past = cache_self_k.shape[2]
    pos_p = jnp.arange(n_prompt, dtype=jnp.int32)
    pos_s = n_past + jnp.arange(n_new, dtype=jnp.int32)
    pos_ks = jnp.arange(n_past + n_new, dtype=jnp.int32)

    hp, hs = x_prompt, x_sample
    new_k_p, new_v_p, new_mk_p, new_mv_p, new_k_s, new_v_s = [], [], [], [], [], []
    for i in range(DEPTH):
        kind = i % N_MIXERS
        j = i // N_MIXERS
        lam_vec = diff_lambda[j] if kind == 0 else None
        subln_g = diff_subln_g[j] if kind == 0 else None

        qp, kp, vp, mqp = split_proj(hp @ w_in[i])
        qs, ks_, vs, mqs = split_proj(hs @ w_in[i])
        mkp = (mem_prompt @ w_mem_k[i]).reshape(bp, N_MEM, MEM_HEADS, HEAD_DIM)
        mvp = (mem_prompt @ w_mem_v[i]).reshape(bp, N_MEM, MEM_HEADS, HEAD_DIM)

        op = self_mixer(kind, i, qp, kp, vp, pos_p, pos_p, rel_bias_table, lam_vec, subln_g)
        k_all = jnp.concatenate([cache_self_k[i], ks_], axis=1)
        v_all = jnp.concatenate([cache_self_v[i], vs], axis=1)
        os_ = self_mixer(kind, i, qs, k_all, v_all, pos_s, pos_ks, rel_bias_table, lam_vec, subln_g)

        mop = mem_attend(mqp, mkp, mvp)
        mos = mem_attend(mqs, cache_mem_k[i], cache_mem_v[i])

        hp = layer_norm(ALPHA * hp + jnp.concatenate([op, mop], axis=-1) @ w_o[i], ln_g[i, 0], ln_b[i, 0])
        hs = layer_norm(ALPHA * hs + jnp.concatenate([os_, mos], axis=-1) @ w_o[i], ln_g[i, 0], ln_b[i, 0])

        hp = layer_norm(ALPHA * hp + peer_ffn(hp, peer_w_q[i], peer_sub_keys[i], peer_u[i], peer_v[i]), ln_g[i, 1], ln_b[i, 1])
        hs = layer_norm(ALPHA * hs + peer_ffn(hs, peer_w_q[i], peer_sub_keys[i], peer_u[i], peer_v[i]), ln_g[i, 1], ln_b[i, 1])

        new_k_p.append(kp)
        new_v_p.append(vp)
        new_mk_p.append(mkp)
        new_mv_p.append(mvp)
        new_k_s.append(ks_)
        new_v_s.append(vs)

    return (hp, hs, jnp.stack(new_k_p), jnp.stack(new_v_p), jnp.stack(new_mk_p), jnp.stack(new_mv_p),
            jnp.stack(new_k_s), jnp.stack(new_v_s))
```

```python
import math
import os
from contextlib import ExitStack
import numpy as np
import concourse.bass as bass
import concourse.mybir as mybir
from concourse.bass_utils import run_bass_kernel_spmd

F32 = mybir.dt.float32
BF16 = mybir.dt.bfloat16
U32 = mybir.dt.uint32
AF = mybir.ActivationFunctionType
ALU = mybir.AluOpType
AX = mybir.AxisListType

NEG = -30000.0
LN_EPS = 1e-5
SW = 1536
NMEM = 256
DEC_SEQ = 32
NEXP = 16384


class Cfg:
    def __init__(self, D=2048, SEQ=2048, NSEQ=2, NSMP=4, PAST=2048, DEPTH=2, debug=False, stop=None):
        self.D, self.SEQ, self.NSEQ, self.NSMP, self.PAST, self.DEPTH = D, SEQ, NSEQ, NSMP, PAST, DEPTH
        self.KD = D // 128
        self.NP = NSEQ * SEQ
        self.NTOK = self.NP + NSMP * DEC_SEQ
        assert self.NTOK % 128 == 0 and SEQ % 256 == 0 and PAST % 128 == 0
        self.NT = self.NTOK // 128
        self.NMT = NSEQ * NMEM
        self.debug = debug
        self.stop = stop
        self.alpha = (2 * DEPTH) ** 0.25


class Sem:
    def __init__(self, h):
        self.h = h
        self.count = 0


class Buf:
    def __init__(self, t, name=""):
        self.t = t
        self.name = name
        self.w = None
        self.r = {}
        self.dsem = None
        self.dsem_sw = None
        self.excl = False

    def __getitem__(self, k):
        return self.t[k]


class Eng:
    def __init__(self, h, sem, name):
        self.h = h
        self.sem = sem
        self.name = name
        self.seen = {}
        self.outst = []


class KB:
    def __init__(self, nc, st):
        self.nc = nc
        self.st = st
        mk = lambda n: Sem(st.enter_context(nc.semaphore(n)))
        self.pe = Eng(nc.tensor, mk("s_pe"), "pe")
        self.act = Eng(nc.scalar, mk("s_act"), "act")
        self.dve = Eng(nc.vector, mk("s_dve"), "dve")
        self.pool = Eng(nc.gpsimd, mk("s_pool"), "pool")
        self.sp = Eng(nc.sync, mk("s_sp"), "sp")
        self.engs = [self.pe, self.act, self.dve, self.pool, self.sp]
        self.dsems = [mk(f"s_d{i}") for i in range(48)]
        self.free_dsems = list(self.dsems)
        self.dsems_sw = [mk(f"s_w{i}") for i in range(28)]
        self.free_dsems_sw = list(self.dsems_sw)
        self.psum = []
        for i in range(8):
            t = st.enter_context(nc.psum_tensor(f"ps{i}", [128, 512], F32))
            self.psum.append(Buf(t, f"ps{i}"))
            self.psum[-1].excl = True
        self.ps_rr = 0
        self.reserved = set()

    def sb(self, ph, name, shape, dt):
        self.uid = getattr(self, "uid", 0) + 1
        name = f"{name}_{self.uid}"
        return Buf(ph.enter_context(self.nc.sbuf_tensor(name, list(shape), dt)), name)

    def release(self, bufs):
        for b in bufs:
            if b.dsem is not None:
                self.free_dsems.append(b.dsem)
                b.dsem = None
            if b.dsem_sw is not None:
                self.free_dsems_sw.append(b.dsem_sw)
                b.dsem_sw = None

    def ps(self, reserve=False):
        while True:
            b = self.psum[self.ps_rr % 8]
            self.ps_rr += 1
            if id(b) not in self.reserved:
                break
        if reserve:
            self.reserved.add(id(b))
        return b

    def unreserve(self, b):
        self.reserved.discard(id(b))

    def _wait(self, e, evs):
        best = {}
        for (s, v) in evs:
            if v > best.get(id(s), (s, 0))[1]:
                best[id(s)] = (s, v)
        for (s, v) in best.values():
            if e is self.pe and s is e.sem:
                continue
            if e.seen.get(id(s), 0) < v:
                e.h.wait_ge(s.h, v)
                e.seen[id(s)] = v

    def _deps(self, r, w):
        evs = []
        for b in r:
            if b.w is not None:
                evs.append(b.w)
            if b.excl:
                evs.extend(b.r.values())
        for b in w:
            if b.w is not None:
                evs.append(b.w)
            evs.extend(b.r.values())
        return evs

    def _record(self, ev, r, w):
        for b in r:
            b.r[id(ev[0])] = ev
        for b in w:
            b.w = ev
            b.r = {}

    def op(self, e, fn, r=(), w=()):
        self._wait(e, self._deps(r, w))
        ins = fn(e.h)
        e.sem.count += 1
        ins.then_inc(e.sem.h, 1)
        ev = (e.sem, e.sem.count)
        self._record(ev, r, w)
        return ev

    def mm(self, out_buf, mms, r=(), extra_w=()):
        e = self.pe
        self._wait(e, self._deps(r, [out_buf] + list(extra_w)))
        n = len(mms)
        ins = None
        for i, (o, l, rh) in enumerate(mms):
            ins = e.h.matmul(o, l, rh, start=(i == 0), stop=(i == n - 1))
        e.sem.count += 1
        ins.then_inc(e.sem.h, 1)
        ev = (e.sem, e.sem.count)
        self._record(ev, r, [out_buf] + list(extra_w))
        return ev

    def mm_multi(self, out_bufs, groups, r=()):
        e = self.pe
        self._wait(e, self._deps(r, out_bufs))
        ins = None
        for g in groups:
            n = len(g)
            for i, (o, l, rh) in enumerate(g):
                ins = e.h.matmul(o, l, rh, start=(i == 0), stop=(i == n - 1))
        e.sem.count += 1
        ins.then_inc(e.sem.h, 1)
        ev = (e.sem, e.sem.count)
        self._record(ev, r, out_bufs)
        return ev

    def transpose(self, out_buf, out_ap, in_ap, ident_ap, r=()):
        e = self.pe
        self._wait(e, self._deps(r, [out_buf]))
        ins = e.h.transpose(out_ap, in_ap, ident_ap)
        e.sem.count += 1
        ins.then_inc(e.sem.h, 1)
        ev = (e.sem, e.sem.count)
        self._record(ev, r, [out_buf])
        return ev

    def dma(self, e, out, in_, sbuf, r=(), w=(), ndesc=256):
        if e is self.pool:
            if sbuf.dsem_sw is None:
                sbuf.dsem_sw = self.free_dsems_sw.pop(0)
            ds = sbuf.dsem_sw
        else:
            if sbuf.dsem is None:
                sbuf.dsem = self.free_dsems.pop(0)
            ds = sbuf.dsem
        lim = 1536 if e is self.pool else 6144
        tot = sum(n for _, n in e.outst) + ndesc
        evs = self._deps(r, w)
        while e.outst and tot > lim:
            ev0, n0 = e.outst.pop(0)
            evs.append((ev0[0], ev0[0].count))
            tot -= n0
        self._wait(e, evs)
        ins = e.h.dma_start(out=out, in_=in_)
        ds.count += 16
        ins.then_inc(ds.h, 16)
        ev = (ds, ds.count)
        e.outst.append((ev, ndesc))
        self._record(ev, r, w)
        return ev

    def barrier(self):
        evs = [(e.sem, e.sem.count) for e in self.engs if e.sem.count > 0]
        evs += [(d, d.count) for d in self.dsems + self.dsems_sw if d.count > 0]
        for e in self.engs:
            self._wait(e, evs)
            e.outst = []


def t5_bucket_np(rel):
    nb = 16
    max_exact = 8
    ret = np.where(rel > 0, nb, 0)
    n = np.abs(rel)
    nf = np.maximum(n, 1).astype(np.float32)
    large = max_exact + (np.log(nf / max_exact) / math.log(128 / max_exact) * (nb - max_exact)).astype(np.int32)
    large = np.minimum(large, nb - 1)
    return ret + np.where(n < max_exact, n, large)


def make_consts():
    c = {}
    c["ident"] = np.eye(128, dtype=np.float32)
    c["iota"] = np.tile(np.arange(128, dtype=np.float32)[None, :], (128, 1))
    k = np.arange(128)[:, None]
    q = np.arange(128)[None, :]
    c["tri"] = (k >= q).astype(np.float32)
    c["ones"] = np.ones((128, 128), np.float32)
    c["jx"] = np.eye(128, dtype=np.float32)[::-1].copy()
    c["maskd"] = np.where((k // 64) <= (q // 64), 0.0, NEG).astype(np.float32)
    c["masksb"] = (k < q).astype(np.float32)
    rel = np.arange(384) - 255
    bk = t5_bucket_np(rel.astype(np.int32))
    oh = np.zeros((32, 384), np.float32)
    oh[bk, np.arange(384)] = 1.0
    c["ohr"] = oh[:, ::-1].copy()
    return c


CONST_SHAPES = {"ident": [128, 128], "iota": [128, 128], "tri": [128, 128], "ones": [128, 128],
                "jx": [128, 128], "maskd": [128, 128], "masksb": [128, 128], "ohr": [32, 384]}


def build(cfg):
    nc = bass.Bass("TRN2", target_bir_lowering=False)
    D, KD, NTOK, NT, L = cfg.D, cfg.KD, cfg.NTOK, cfg.NT, cfg.DEPTH
    NSEQ, NSMP, SEQ, PAST, NMT = cfg.NSEQ, cfg.NSMP, cfg.SEQ, cfg.PAST, cfg.NMT

    def din(name, shape, dt=F32):
        return nc.dram_tensor(name, list(shape), dt, kind="ExternalInput").ap()

    def dout(name, shape, dt=F32):
        return nc.dram_tensor(name, list(shape), dt, kind="ExternalOutput").ap()

    def dscr(name, shape, dt):
        return nc.dram_tensor(name, list(shape), dt, kind="ExternalOutput" if cfg.debug else "Internal").ap()

    xin = din("xin", [NTOK, D])
    w_in = din("w_in", [L, D, 5120])
    w_o = din("w_o", [L, 2048, D])
    w_mkv = din("w_mkv", [L, D, 1024])
    w_q = din("w_q", [L, D, 2048])
    subk = din("subk", [L, 16, 128, 128])
    ut = din("ut", [L, D, NEXP])
    vv = din("vv", [L, NEXP, D])
    ck = din("ck", [L, NSMP, PAST, SW])
    cv = din("cv", [L, NSMP, PAST, SW])
    cmk = din("cmk", [L, NSMP, NMEM, 512])
    cmv = din("cmv", [L, NSMP, NMEM, 512])
    memp = din("memp", [NMT, D])
    relb = din("relb", [32, 6])
    lamv = din("lamv", [1, 512])
    subg = din("subg", [1, 256])
    lng = din("lng", [L * 2, D])
    lnb = din("lnb", [L * 2, D])
    cst = {k: din("c_" + k, s) for k, s in CONST_SHAPES.items()}

    y = dout("y", [NTOK, D])
    nk = dout("nk", [L, NTOK, SW])
    nv = dout("nv", [L, NTOK, SW])
    nmk = dout("nmk", [L, NMT, 512])
    nmv = dout("nmv", [L, NMT, 512])

    QT = dscr("QT", [SW, NTOK], BF16)
    KT = dscr("KT", [SW, NTOK], BF16)
    MQT = dscr("MQT", [512, NTOK], BF16)
    VB = dscr("VB", [NTOK, SW], BF16)
    MKT = dscr("MKT", [512, NMT], BF16)
    MVB = dscr("MVB", [NMT, 512], BF16)
    KTS = dscr("KTS", [NSMP, SW, PAST], BF16)
    MKTS = dscr("MKTS", [NSMP, 512, NMEM], BF16)
    MIXT = dscr("MIXT", [2048, NTOK], BF16)
    H1 = dscr("H1", [NTOK, D], F32)
    H1T = dscr("H1T", [D, NTOK], BF16)
    H2 = dscr("H2", [NTOK, D], F32)
    WT = dscr("WT", [NT, 128, 128, 128], BF16)
    BIASV = dscr("BIASV", [6, 384], F32)

    with ExitStack() as st:
        kb = KB(nc, st)
        pe, act, dve, pool, sp = kb.pe, kb.act, kb.dve, kb.pool, kb.sp

        identf = kb.sb(st, "identf", [128, 128], F32)
        identb = kb.sb(st, "identb", [128, 128], BF16)
        iota = kb.sb(st, "iota", [128, 128], F32)
        iotab = kb.sb(st, "iotab", [128, 128], BF16)
        trib = kb.sb(st, "trib", [128, 128], F32)
        onesf = kb.sb(st, "onesf", [128, 128], F32)
        jx = kb.sb(st, "jx", [128, 128], F32)
        maskd = kb.sb(st, "maskd", [128, 128], F32)
        masksb = kb.sb(st, "masksb", [128, 128], F32)
        lam_t = kb.sb(st, "lam_t", [128, 4], F32)
        subg_bc = kb.sb(st, "subg_bc", [128, 256], F32)
        cfar = kb.sb(st, "cfar", [128, 6], F32)
        zero_c = kb.sb(st, "zero_c", [128, 1], F32)
        bias_t = kb.sb(st, "bias_t", [128, 6, 2, 128], F32)

        kb.dma(sp, identf[:], cst["ident"], identf, w=[identf])
        kb.dma(pool, identb[:], cst["ident"], identb, w=[identb])
        kb.dma(sp, iota[:], cst["iota"], iota, w=[iota])
        kb.dma(pool, iotab[:], cst["iota"], iotab, w=[iotab])
        kb.dma(sp, trib[:], cst["tri"], trib, w=[trib])
        kb.dma(sp, onesf[:], cst["ones"], onesf, w=[onesf])
        kb.dma(sp, jx[:], cst["jx"], jx, w=[jx])
        kb.dma(sp, maskd[:], cst["maskd"], maskd, w=[maskd])
        kb.dma(sp, masksb[:], cst["masksb"], masksb, w=[masksb])
        kb.dma(sp, subg_bc[:], subg.partition_broadcast(128), subg_bc, w=[subg_bc])
        kb.dma(sp, cfar[:], relb[15:16, :].partition_broadcast(128), cfar, w=[cfar])
        kb.op(dve, lambda h: h.memset(zero_c[:], 0.0), w=[zero_c])

        with ExitStack() as ph:
            lbc = kb.sb(ph, "lbc", [128, 512], F32)
            lpr = kb.sb(ph, "lpr", [128, 256], F32)
            lsm = kb.sb(ph, "lsm", [128, 2], F32)
            kb.dma(sp, lbc[:], lamv.partition_broadcast(128), lbc, w=[lbc])
            l4 = lbc[:].rearrange("p (a b) -> p a b", a=4)
            kb.op(dve, lambda h: h.tensor_tensor(out=lpr[:].rearrange("p (a b) -> p a b", a=2), in0=l4[:, 0:4:2, :],
                                                 in1=l4[:, 1:4:2, :], op=ALU.mult), r=[lbc], w=[lpr])
            kb.op(dve, lambda h: h.reduce_sum(out=lsm[:], in_=lpr[:].rearrange("p (a b) -> p a b", a=2), axis=AX.X),
                  r=[lpr], w=[lsm])
            kb.op(act, lambda h: h.activation(out=lsm[:], in_=lsm[:], func=AF.Exp), r=[lsm], w=[lsm])
            lam_init0 = 0.8 - 0.6 * math.exp(-0.3 * 0)
            kb.op(dve, lambda h: h.tensor_tensor(out=lam_t[:, 0:1], in0=lsm[:, 0:1], in1=lsm[:, 1:2], op=ALU.subtract),
                  r=[lsm], w=[lam_t])
            kb.op(dve, lambda h: h.tensor_scalar(out=lam_t[:, 0:1], in0=lam_t[:, 0:1], scalar1=lam_init0, scalar2=None,
                                                 op0=ALU.add), r=[lam_t], w=[lam_t])
            kb.op(dve, lambda h: h.tensor_scalar(out=lam_t[:, 1:2], in0=lam_t[:, 0:1], scalar1=-1.0, scalar2=None,
                                                 op0=ALU.mult), r=[lam_t], w=[lam_t])
            kb.op(dve, lambda h: h.tensor_scalar(out=subg_bc[:], in0=subg_bc[:], scalar1=(1.0 - lam_init0), scalar2=None,
                                                 op0=ALU.mult), r=[subg_bc], w=[subg_bc])
            relb_sb = kb.sb(ph, "relb_sb", [32, 6], F32)
            ohr_sb = kb.sb(ph, "ohr_sb", [32, 384], F32)
            fv_sb = kb.sb(ph, "fv_sb", [6, 384], F32)
            bp = kb.sb(ph, "bp", [128, 6, 2, 128], F32)
            kb.dma(sp, relb_sb[:], relb, relb_sb, w=[relb_sb])
            kb.dma(sp, ohr_sb[:], cst["ohr"], ohr_sb, w=[ohr_sb])
            p0 = kb.ps()
            kb.mm(p0, [(p0[0:6, 0:384], relb_sb[:], ohr_sb[:])], r=[relb_sb, ohr_sb])
            kb.op(dve, lambda h: h.tensor_copy(out=fv_sb[:], in_=p0[0:6, 0:384]), r=[p0], w=[fv_sb])
            kb.dma(sp, BIASV, fv_sb[:], fv_sb, r=[fv_sb])
            kb.barrier()
            for hh in range(6):
                for ti, off in enumerate((1, 129)):
                    src = bass.AP(tensor=BIASV.tensor, offset=hh * 384 + off, ap=[[1, 128], [1, 128]])
                    kb.dma(sp, bp[:, hh, ti, :], src, bp, w=[bp])
            for hh in range(6):
                pp = kb.ps()
                kb.mm(pp, [(pp[:, 0:256], jx[:], bp[:, hh, :, :].rearrange("p a b -> p (a b)"))], r=[jx, bp])
                kb.op(dve, lambda h: h.tensor_copy(out=bias_t[:, hh, :, :].rearrange("p a b -> p (a b)"), in_=pp[:, 0:256]),
                      r=[pp], w=[bias_t])
            for hh in range(6):
                kb.op(dve, lambda h: h.tensor_tensor(out=bias_t[:, hh, 0, :], in0=bias_t[:, hh, 0, :], in1=maskd[:],
                                                     op=ALU.add), r=[bias_t, maskd], w=[bias_t])
            kb.barrier()
            kb.release([lbc, relb_sb, ohr_sb, fv_sb, bp, identf, identb, iota, iotab, trib, onesf, jx, maskd, masksb, subg_bc, cfar])

        def load_ln_params(ph, idx):
            g = kb.sb(ph, "ln_g", [128, D], F32)
            b = kb.sb(ph, "ln_b", [128, D], F32)
            kb.dma(sp, g[:], lng[idx:idx + 1, :].partition_broadcast(128), g, w=[g])
            kb.dma(sp, b[:], lnb[idx:idx + 1, :].partition_broadcast(128), b, w=[b])
            return g, b

        def layer_norm(r_buf, out_buf, g, b, stat):
            junk = out_buf
            kb.op(act, lambda h: h.activation(out=junk[:], in_=r_buf[:], func=AF.Copy, accum_out=stat[:, 0:1]),
                  r=[r_buf], w=[junk, stat])
            kb.op(dve, lambda h: h.tensor_scalar(out=stat[:, 1:2], in0=stat[:, 0:1], scalar1=-1.0 / D, scalar2=None,
                                                 op0=ALU.mult), r=[stat], w=[stat])
            kb.op(act, lambda h: h.activation(out=junk[:], in_=r_buf[:], func=AF.Square, bias=stat[:, 1:2], scale=1.0,
                                              accum_out=stat[:, 2:3]), r=[r_buf, stat], w=[junk, stat])
            kb.op(dve, lambda h: h.tensor_scalar(out=stat[:, 3:4], in0=stat[:, 2:3], scalar1=1.0 / D, scalar2=LN_EPS,
                                                 op0=ALU.mult, op1=ALU.add), r=[stat], w=[stat])
            kb.op(act, lambda h: h.activation(out=stat[:, 3:4], in_=stat[:, 3:4], func=AF.Sqrt), r=[stat], w=[stat])
            kb.op(dve, lambda h: h.reciprocal(out=stat[:, 4:5], in_=stat[:, 3:4]), r=[stat], w=[stat])
            kb.op(dve, lambda h: h.tensor_tensor(out=stat[:, 5:6], in0=stat[:, 1:2], in1=stat[:, 4:5], op=ALU.mult),
                  r=[stat], w=[stat])
            kb.op(act, lambda h: h.activation(out=junk[:], in_=r_buf[:], func=AF.Identity, bias=stat[:, 5:6],
                                              scale=stat[:, 4:5]), r=[r_buf, stat], w=[junk])
            kb.op(dve, lambda h: h.tensor_tensor(out=junk[:], in0=junk[:], in1=g[:], op=ALU.mult), r=[junk, g], w=[junk])
            kb.op(dve, lambda h: h.tensor_tensor(out=out_buf[:], in0=junk[:], in1=b[:], op=ALU.add), r=[junk, b],
                  w=[out_buf])

        def transpose_to_T(src_bf, dstT, col0, ncol_chunks, evict_rr):
            for c0 in range(0, ncol_chunks, 4):
                n = min(4, ncol_chunks - c0)
                pp = kb.ps()
                ppb = pp[:, 0:256].bitcast(BF16)
                for c in range(n):
                    kb.transpose(pp, ppb[:, c * 128:(c + 1) * 128], src_bf[:, (c0 + c) * 128:(c0 + c + 1) * 128],
                                 identb[:], r=[src_bf, identb])
                e = act if (evict_rr[0] % 2 == 0) else dve
                evict_rr[0] += 1
                if e is act:
                    kb.op(act, lambda h: h.activation(out=dstT[:, c0:c0 + n, col0:col0 + 128],
                                                      in_=ppb[:, 0:n * 128].rearrange("p (c t) -> p c t", c=n),
                                                      func=AF.Copy), r=[pp], w=[dstT])
                else:
                    kb.op(dve, lambda h: h.tensor_copy(out=dstT[:, c0:c0 + n, col0:col0 + 128],
                                                       in_=ppb[:, 0:n * 128].rearrange("p (c t) -> p c t", c=n)),
                          r=[pp], w=[dstT])

        def project(src_tok, ntiles, w_dram, ncols, fm_outs, tm_outs, kd, tag):
            nblk = ncols // 512
            with ExitStack() as ph:
                rr = [0]
                xts = [kb.sb(ph, f"{tag}_xT{i}", [128, kd, 512], BF16) for i in range(2)]
                xbs = [kb.sb(ph, f"{tag}_xb{i}", [128, kd * 128], BF16) for i in range(2)]
                wbs = [kb.sb(ph, f"{tag}_wb{i}", [128, kd, 512], BF16) for i in range(3)]
                ofm = [kb.sb(ph, f"{tag}_ofm{i}", [128, 512], BF16) for i in range(3)]
                otf = [kb.sb(ph, f"{tag}_otf{i}", [128, 512], F32) for i in range(3)]
                otb = [kb.sb(ph, f"{tag}_otb{i}", [128, 512], BF16) for i in range(3)]
                wv = w_dram.rearrange("(k p) f -> p k f", p=128)
                gi = 0
                xi = 0
                wi = 0
                oi = 0
                for t0 in range(0, ntiles, 4):
                    nt_g = min(4, ntiles - t0)
                    ntk = nt_g * 128
                    xT = xts[gi % 2]
                    gi += 1
                    for tt in range(nt_g):
                        xb = xbs[xi % 2]
                        xi += 1
                        kb.dma(pool, xb[:], src_tok[(t0 + tt) * 128:(t0 + tt + 1) * 128, :], xb, w=[xb], ndesc=128)
                        transpose_to_T(xb, xT, tt * 128, kd, rr)
                    CUT = 0
                    for b in range(nblk if CUT != 1 else 0):
                        wb = wbs[wi % 3]
                        wi += 1
                        kq = max(1, kd // 4)
                        for k0 in range(0, kd, kq):
                            kb.dma(pool, wb[:, k0:k0 + kq, :], wv[:, k0:k0 + kq, b * 512:(b + 1) * 512], wb, w=[wb],
                                   ndesc=128 * kq)
                        for (c0, nch, dstT) in (fm_outs if CUT != 2 else []):
                            for ch in range(nch):
                                col = c0 + ch * 128
                                if col // 512 != b:
                                    continue
                                lo = col - b * 512
                                pp = kb.ps()
                                kb.mm(pp, [(pp[:, 0:ntk], wb[:, k, lo:lo + 128], xT[:, k, 0:ntk]) for k in range(kd)],
                                      r=[wb, xT])
                                ob = ofm[oi % 3]
                                oi += 1
                                e = act if oi % 2 == 0 else dve
                                if e is act:
                                    kb.op(act, lambda h: h.activation(out=ob[:, 0:ntk], in_=pp[:, 0:ntk], func=AF.Copy),
                                          r=[pp], w=[ob])
                                else:
                                    kb.op(dve, lambda h: h.tensor_copy(out=ob[:, 0:ntk], in_=pp[:, 0:ntk]), r=[pp], w=[ob])
                                kb.dma(sp, dstT[ch * 128:(ch + 1) * 128, t0 * 128:t0 * 128 + ntk], ob[:, 0:ntk], ob, r=[ob],
                                       ndesc=128)
                        for (c0, ncl, dst32, dst16) in (tm_outs if CUT not in (2, 3) else []):
                            if not (c0 <= b * 512 < c0 + ncl):
                                continue
                            dc = b * 512 - c0
                            for tt in range(nt_g):
                                pp = kb.ps()
                                kb.mm(pp, [(pp[:, :], xT[:, k, tt * 128:(tt + 1) * 128], wb[:, k, :]) for k in range(kd)],
                                      r=[wb, xT])
                                rows = slice((t0 + tt) * 128, (t0 + tt + 1) * 128)
                                if dst32 is not None:
                                    o32 = otf[oi % 3]
                                    kb.op(act, lambda h: h.activation(out=o32[:], in_=pp[:, :], func=AF.Copy), r=[pp],
                                          w=[o32])
                                    kb.dma(sp, dst32[rows, dc:dc + 512], o32[:], o32, r=[o32], ndesc=128)
                                if dst16 is not None:
                                    o16 = otb[oi % 3]
                                    kb.op(dve, lambda h: h.tensor_copy(out=o16[:], in_=pp[:, :]), r=[pp], w=[o16])
                                    kb.dma(sp, dst16[rows, dc:dc + 512], o16[:], o16, r=[o16], ndesc=128)
                                oi += 1
                kb.barrier()
                kb.release(xts + xbs + wbs + ofm + otf + otb)

        def cache_transposes(l):
            with ExitStack() as ph:
                rr = [0]
                kin = [kb.sb(ph, f"ckin{i}", [128, SW], BF16) for i in range(2)]
                kout = [kb.sb(ph, f"ckout{i}", [128, 12, 128], BF16) for i in range(2)]
                i = 0
                for s in range(NSMP):
                    for blk in range(PAST // 128):
                        a = kin[i % 2]
                        o = kout[i % 2]
                        i += 1
                        kb.dma(pool, a[:], ck[l, s, blk * 128:(blk + 1) * 128, :], a, w=[a], ndesc=128)
                        transpose_to_T(a, o, 0, 12, rr)
                        kb.dma(sp, KTS[s, :, blk * 128:(blk + 1) * 128].rearrange("(c p) t -> p c t", p=128), o[:], o,
                               r=[o], ndesc=128 * 12)
                    for blk in range(NMEM // 128):
                        a = kin[i % 2]
                        o = kout[i % 2]
                        i += 1
                        kb.dma(pool, a[:, 0:512], cmk[l, s, blk * 128:(blk + 1) * 128, :], a, w=[a], ndesc=128)
                        transpose_to_T(a, o, 0, 4, rr)
                        kb.dma(sp, MKTS[s, :, blk * 128:(blk + 1) * 128].rearrange("(c p) t -> p c t", p=128),
                               o[:, 0:4, :], o, r=[o], ndesc=128 * 4)
                kb.barrier()
                kb.release(kin + kout)

        def attention(l):
            kind = l % 2
            scale = 128 ** -0.5
            with ExitStack() as ph:
                rr = [0]
                NKMAX = max(SEQ, PAST + 128) // 128
                qTs = [kb.sb(ph, f"a_qT{i}", [128, 2, SEQ], BF16) for i in range(2)]
                kTs = [kb.sb(ph, f"a_kT{i}", [128, 2, NKMAX * 128], BF16) for i in range(2)]
                vss = [kb.sb(ph, f"a_v{i}", [128, NKMAX, 260], BF16) for i in range(2)]
                for vsb in vss:
                    kb.op(dve, lambda h: h.memset(vsb[:], 1.0), w=[vsb])
                NR = 5
                tf = [kb.sb(ph, f"a_tf{i}", [128, 512], F32) for i in range(NR)]
                tg = [kb.sb(ph, f"a_tg{i}", [128, 512], F32) for i in range(NR)]
                tx = [kb.sb(ph, f"a_tx{i}", [128, 256], F32) for i in range(NR)]
                pb = [kb.sb(ph, f"a_pb{i}", [128, 512], BF16) for i in range(NR)]
                raccs = [kb.sb(ph, f"a_racc{i}", [128, 256], F32) for i in range(4)]
                osb = [kb.sb(ph, f"a_o{i}", [128, 256], F32) for i in range(2)]
                osq = kb.sb(ph, "a_osq", [128, 256], F32)
                obf = [kb.sb(ph, f"a_ob{i}", [128, 256], BF16) for i in range(2)]
                oT = [kb.sb(ph, f"a_oT{i}", [128, 2, 128], BF16) for i in range(2)]
                sm = [kb.sb(ph, f"a_sm{i}", [128, 8], F32) for i in range(2)]
                cnt = {"h": 0, "w": 0, "o": 0}

                def run_head(ncomp, dv, q_src, nq_tot, qchunk, kblocks, cls, mode, bias_h, mix_row0, tok0):
                    hi = cnt["h"]
                    cnt["h"] += 1
                    qT, kT, vs = qTs[hi % 2], kTs[hi % 2], vss[hi % 2]
                    for c in range(ncomp):
                        kb.dma(sp, qT[:, c, 0:nq_tot], q_src[c], qT, w=[qT], ndesc=128)
                    for j, kbk in enumerate(kblocks):
                        ks = kbk["ks"]
                        for c in range(ncomp):
                            kb.dma(sp, kT[:, c, j * 128:j * 128 + ks], kbk["kT"][c], kT, w=[kT], ndesc=128)
                        kb.dma(pool if kbk["cast"] else sp, vs[0:ks, j, 256 - dv:256], kbk["v"], vs, w=[vs], ndesc=128)
                    nkb = len(kblocks)
                    for q0 in range(0, nq_tot, qchunk):
                        nq = min(qchunk, nq_tot - q0)
                        nsub = (nq + 127) // 128
                        subs = [(q0 // 128 + i, min(128, nq - i * 128)) for i in range(nsub)]
                        accs = {}
                        for c in range(ncomp):
                            for si in range(nsub):
                                accs[(c, si)] = kb.ps(reserve=True)
                        order = list(range(nkb))
                        if mode == "sb":
                            order = order[::-1]
                        started = set()
                        live = [j for j in order if any(cls(j, qb) != "skip" for qb, _ in subs)]
                        lastj = {}
                        for si, (qb, _) in enumerate(subs):
                            for j in live:
                                if cls(j, qb) != "skip":
                                    lastj[si] = j
                        npairs = len(live)
                        st_ = {}

                        def stageA(j):
                            ks = kblocks[j]["ks"]
                            types = [cls(j, qb) for qb, _ in subs]
                            sp_ps = kb.ps()
                            kb.mm_multi([sp_ps], [[(sp_ps[0:ks, c * 256:c * 256 + nq], kT[:, c, j * 128:j * 128 + ks],
                                                   qT[:, c, q0:q0 + nq])] for c in range(ncomp)], r=[kT, qT])
                            wi = cnt["w"]
                            cnt["w"] += 1
                            P = pb[wi % NR]
                            T = tf[wi % NR]
                            G = tg[wi % NR]
                            st_[j] = dict(ks=ks, types=types, P=P, T=T, G=G, wi=wi)
                            if mode in ("diff", "mem"):
                                if all(t == "far" for t in types):
                                    bias_ap = cfar[0:ks, bias_h:bias_h + 1] if mode == "diff" else zero_c[0:ks, :]
                                    kb.op(act, lambda h: h.activation(
                                        out=P[0:ks, :].rearrange("p (c q) -> p c q", c=2)[:, 0:ncomp, 0:nq],
                                        in_=sp_ps[0:ks, :].rearrange("p (c q) -> p c q", c=2)[:, 0:ncomp, 0:nq],
                                        func=AF.Exp, bias=bias_ap, scale=scale), r=[sp_ps, cfar, zero_c], w=[P])
                                else:
                                    for si, (qb, nqs) in enumerate(subs):
                                        t = types[si]
                                        if t == "skip":
                                            continue
                                        for c in range(ncomp):
                                            src = sp_ps[0:ks, c * 256 + si * 128:c * 256 + si * 128 + nqs]
                                            dst = P[0:ks, c * 256 + si * 128:c * 256 + si * 128 + nqs]
                                            if t == "far":
                                                kb.op(act, lambda h: h.activation(out=dst, in_=src, func=AF.Exp,
                                                                                  bias=cfar[0:ks, bias_h:bias_h + 1],
                                                                                  scale=scale), r=[sp_ps, cfar], w=[P])
                                            else:
                                                bt = bias_t[0:ks, bias_h, 0 if t == "diag" else 1, 0:nqs]
                                                tmp = T[0:ks, c * 256 + si * 128:c * 256 + si * 128 + nqs]
                                                kb.op(dve, lambda h: h.scalar_tensor_tensor(out=tmp, in0=src, scalar=scale,
                                                                                            in1=bt, op0=ALU.mult,
                                                                                            op1=ALU.add),
                                                      r=[sp_ps, bias_t], w=[T])
                                                kb.op(act, lambda h: h.activation(out=dst, in_=tmp, func=AF.Exp), r=[T],
                                                      w=[P])
                            else:
                                E = T
                                SPt = G
                                kb.op(act, lambda h: h.activation(out=E[0:ks, 0:nq], in_=sp_ps[0:ks, 0:nq], func=AF.Exp,
                                                                  scale=scale), r=[sp_ps], w=[E])
                                kb.op(act, lambda h: h.activation(out=SPt[0:ks, 0:nq], in_=E[0:ks, 0:nq], func=AF.Ln,
                                                                  bias=1.0, scale=1.0), r=[E], w=[SPt])
                                for si, (qb, nqs) in enumerate(subs):
                                    t = types[si]
                                    sl = slice(si * 128, si * 128 + nqs)
                                    if t == "diag":
                                        kb.op(dve, lambda h: h.tensor_tensor(out=SPt[0:ks, sl], in0=SPt[0:ks, sl],
                                                                             in1=masksb[0:ks, 0:nqs], op=ALU.mult),
                                              r=[SPt, masksb], w=[SPt])
                                        kb.op(dve, lambda h: h.tensor_tensor(out=E[0:ks, sl], in0=E[0:ks, sl],
                                                                             in1=masksb[0:ks, 0:nqs], op=ALU.mult),
                                              r=[E, masksb], w=[E])
                                    elif t == "skip":
                                        kb.op(dve, lambda h: h.memset(SPt[0:ks, sl], 0.0), w=[SPt])
                                        kb.op(dve, lambda h: h.memset(E[0:ks, sl], 0.0), w=[E])

                        rstate = {"prev": None, "n": 0}

                        def stageB(j):
                            if mode != "sb":
                                return
                            d_ = st_[j]
                            ks, P, E, SPt, wi = d_["ks"], d_["P"], d_["T"], d_["G"], d_["wi"]
                            cp = kb.ps()
                            mmsl = [(cp[0:ks, 0:nq], trib[0:ks, 0:ks], SPt[0:ks, 0:nq])]
                            rprev = rstate["prev"]
                            rd = [trib, onesf, SPt]
                            if rprev is not None:
                                mmsl.append((cp[0:ks, 0:nq], onesf[:, 0:ks], rprev[:, 0:nq]))
                                rd.append(rprev)
                            kb.mm(cp, mmsl, r=rd)
                            X = tx[wi % NR]
                            kb.op(act, lambda h: h.activation(out=X[0:ks, 0:nq], in_=cp[0:ks, 0:nq], func=AF.Exp,
                                                              scale=-1.0), r=[cp], w=[X])
                            kb.op(dve, lambda h: h.tensor_tensor(out=P[0:ks, 0:nq], in0=E[0:ks, 0:nq], in1=X[0:ks, 0:nq],
                                                                 op=ALU.mult), r=[E, X], w=[P])
                            rn = raccs[rstate["n"] % len(raccs)]
                            rstate["n"] += 1
                            if rprev is None:
                                if ks < 128:
                                    kb.op(dve, lambda h: h.memset(rn[:], 0.0), w=[rn])
                                kb.op(dve, lambda h: h.tensor_copy(out=rn[0:ks, 0:nq], in_=SPt[0:ks, 0:nq]), r=[SPt], w=[rn])
                            else:
                                kb.op(dve, lambda h: h.tensor_tensor(out=rn[:, 0:nq], in0=rprev[:, 0:nq], in1=SPt[:, 0:nq],
                                                                     op=ALU.add), r=[rprev, SPt], w=[rn])
                            rstate["prev"] = rn

                        def stageC(j):
                            d_ = st_[j]
                            ks, types, P = d_["ks"], d_["types"], d_["P"]
                            ncol = dv + (0 if mode == "sb" else 1)
                            for si, (qb, nqs) in enumerate(subs):
                                if types[si] == "skip":
                                    continue
                                for c in range(ncomp):
                                    a = accs[(c, si)]
                                    first = (c, si) not in started
                                    started.add((c, si))
                                    last = (lastj[si] == j)
                                    kb._wait(pe, kb._deps([P, vs], [a] if first else []))
                                    ins = pe.h.matmul(a[0:nqs, 0:ncol], P[0:ks, c * 256 + si * 128:c * 256 + si * 128 + nqs],
                                                      vs[0:ks, j, 256 - dv:256 - dv + ncol], start=first, stop=last)
                                    pe.sem.count += 1
                                    ins.then_inc(pe.sem.h, 1)
                                    ev = (pe.sem, pe.sem.count)
                                    kb._record(ev, [P, vs], [a])

                        for idx in range(npairs + 2):
                            if idx < npairs:
                                stageA(live[idx])
                            if 0 <= idx - 1 < npairs:
                                stageB(live[idx - 1])
                            if 0 <= idx - 2 < npairs:
                                stageC(live[idx - 2])
                        for si, (qb, nqs) in enumerate(subs):
                            oi = cnt["o"]
                            cnt["o"] += 1
                            o = osb[oi % 2]
                            s_ = sm[oi % 2]
                            ob = obf[oi % 2]
                            ot = oT[oi % 2]
                            if mode == "sb":
                                kb.op(act, lambda h: h.activation(out=ob[0:nqs, 0:dv], in_=accs[(0, si)][0:nqs, 0:dv],
                                                                  func=AF.Copy), r=[accs[(0, si)]], w=[ob])
                            elif mode == "mem":
                                a0 = accs[(0, si)]
                                kb.op(dve, lambda h: h.reciprocal(out=s_[0:nqs, 0:1], in_=a0[0:nqs, dv:dv + 1]), r=[a0],
                                      w=[s_])
                                kb.op(act, lambda h: h.activation(out=ob[0:nqs, 0:dv], in_=a0[0:nqs, 0:dv], func=AF.Copy,
                                                                  scale=s_[0:nqs, 0:1]), r=[a0, s_], w=[ob])
                            else:
                                a0, a1 = accs[(0, si)], accs[(1, si)]
                                kb.op(dve, lambda h: h.reciprocal(out=s_[0:nqs, 0:1], in_=a0[0:nqs, dv:dv + 1]), r=[a0],
                                      w=[s_])
                                kb.op(dve, lambda h: h.reciprocal(out=s_[0:nqs, 1:2], in_=a1[0:nqs, dv:dv + 1]), r=[a1],
                                      w=[s_])
                                kb.op(dve, lambda h: h.tensor_tensor(out=s_[0:nqs, 1:2], in0=s_[0:nqs, 1:2],
                                                                     in1=lam_t[0:nqs, 1:2], op=ALU.mult), r=[s_, lam_t],
                                      w=[s_])
                                kb.op(act, lambda h: h.activation(out=o[0:nqs, 0:dv], in_=a0[0:nqs, 0:dv], func=AF.Copy,
                                                                  scale=s_[0:nqs, 0:1]), r=[a0, s_], w=[o])
                                kb.op(dve, lambda h: h.scalar_tensor_tensor(out=o[0:nqs, 0:dv], in0=a1[0:nqs, 0:dv],
                                                                            scalar=s_[0:nqs, 1:2], in1=o[0:nqs, 0:dv],
                                                                            op0=ALU.mult, op1=ALU.add), r=[a1, s_, o], w=[o])
                                kb.op(act, lambda h: h.activation(out=osq[0:nqs, 0:dv], in_=o[0:nqs, 0:dv], func=AF.Square,
                                                                  accum_out=s_[0:nqs, 2:3]), r=[o], w=[osq, s_])
                                kb.op(dve, lambda h: h.tensor_scalar(out=s_[0:nqs, 3:4], in0=s_[0:nqs, 2:3],
                                                                     scalar1=1.0 / dv, scalar2=LN_EPS, op0=ALU.mult,
                                                                     op1=ALU.add), r=[s_], w=[s_])
                                kb.op(act, lambda h: h.activation(out=s_[0:nqs, 3:4], in_=s_[0:nqs, 3:4], func=AF.Sqrt),
                                      r=[s_], w=[s_])
                                kb.op(dve, lambda h: h.reciprocal(out=s_[0:nqs, 4:5], in_=s_[0:nqs, 3:4]), r=[s_], w=[s_])
                                kb.op(dve, lambda h: h.scalar_tensor_tensor(out=ob[0:nqs, 0:dv], in0=o[0:nqs, 0:dv],
                                                                            scalar=s_[0:nqs, 4:5], in1=subg_bc[0:nqs, 0:dv],
                                                                            op0=ALU.mult, op1=ALU.mult),
                                      r=[o, s_, subg_bc], w=[ob])
                            nch = dv // 128
                            pp = kb.ps()
                            ppb = pp[:, 0:256].bitcast(BF16)
                            for c in range(nch):
                                kb.transpose(pp, ppb[:, c * 128:c * 128 + nqs], ob[0:nqs, c * 128:(c + 1) * 128],
                                             identb[0:nqs, 0:nqs], r=[ob, identb])
                            kb.op(dve, lambda h: h.tensor_copy(out=ot[:, 0:nch, 0:nqs],
                                                               in_=ppb[:, 0:nch * 128].rearrange("p (c t) -> p c t", c=nch)[:, :, 0:nqs]),
                                  r=[pp], w=[ot])
                            tq = tok0 + qb * 128 if nq_tot > 128 else tok0
                            kb.dma(sp, MIXT[mix_row0:mix_row0 + dv, tq:tq + nqs].rearrange("(c p) t -> p c t", p=128),
                                   ot[:, 0:nch, 0:nqs], ot, r=[ot], ndesc=128 * nch)
                        for a_ in accs.values():
                            kb.unreserve(a_)

                nblk = SEQ // 128
                for s in range(NSEQ):
                    tok0 = s * SEQ
                    if kind == 0:
                        for hh in range(6):
                            rows = [hh * 256 + c * 128 for c in range(2)]
                            q_src = [QT[r0:r0 + 128, tok0:tok0 + SEQ] for r0 in rows]
                            kbl = [{"ks": 128, "kT": [KT[r0:r0 + 128, tok0 + j * 128:tok0 + (j + 1) * 128] for r0 in rows],
                                    "v": VB[tok0 + j * 128:tok0 + (j + 1) * 128, hh * 256:(hh + 1) * 256], "cast": False}
                                   for j in range(nblk)]

                            def cls(j, qb):
                                return "skip" if j > qb else ("diag" if j == qb else ("prev" if j == qb - 1 else "far"))
                            run_head(2, 256, q_src, SEQ, 256, kbl, cls, "diff", hh, hh * 256, tok0)
                    else:
                        for hh in range(12):
                            r0 = hh * 128
                            q_src = [QT[r0:r0 + 128, tok0:tok0 + SEQ]]
                            kbl = [{"ks": 128, "kT": [KT[r0:r0 + 128, tok0 + j * 128:tok0 + (j + 1) * 128]],
                                    "v": VB[tok0 + j * 128:tok0 + (j + 1) * 128, r0:r0 + 128], "cast": False}
                                   for j in range(nblk)]

                            def cls(j, qb):
                                return "skip" if j > qb else ("diag" if j == qb else "far")
                            run_head(1, 128, q_src, SEQ, 256, kbl, cls, "sb", 0, r0, tok0)
                    for hh in range(4):
                        r0 = hh * 128
                        q_src = [MQT[r0:r0 + 128, tok0:tok0 + SEQ]]
                        kbl = [{"ks": 128, "kT": [MKT[r0:r0 + 128, s * NMEM + j * 128:s * NMEM + (j + 1) * 128]],
                                "v": MVB[s * NMEM + j * 128:s * NMEM + (j + 1) * 128, r0:r0 + 128], "cast": False}
                               for j in range(NMEM // 128)]
                        run_head(1, 128, q_src, SEQ, 256, kbl, lambda j, qb: "far", "mem", 0, SW + r0, tok0)
                npb = PAST // 128
                for s in range(NSMP):
                    tok0 = cfg.NP + s * DEC_SEQ
                    if kind == 0:
                        for hh in range(6):
                            rows = [hh * 256 + c * 128 for c in range(2)]
                            q_src = [QT[r0:r0 + 128, tok0:tok0 + DEC_SEQ] for r0 in rows]
                            kbl = [{"ks": 128, "kT": [KTS[s, r0:r0 + 128, j * 128:(j + 1) * 128] for r0 in rows],
                                    "v": cv[l, s, j * 128:(j + 1) * 128, hh * 256:(hh + 1) * 256], "cast": True}
                                   for j in range(npb)]
                            kbl.append({"ks": DEC_SEQ, "kT": [KT[r0:r0 + 128, tok0:tok0 + DEC_SEQ] for r0 in rows],
                                        "v": VB[tok0:tok0 + DEC_SEQ, hh * 256:(hh + 1) * 256], "cast": False})

                            def cls(j, qb):
                                return "diag" if j == npb else ("prev" if j == npb - 1 else "far")
                            run_head(2, 256, q_src, DEC_SEQ, 256, kbl, cls, "diff", hh, hh * 256, tok0)
                    else:
                        for hh in range(12):
                            r0 = hh * 128
                            q_src = [QT[r0:r0 + 128, tok0:tok0 + DEC_SEQ]]
                            kbl = [{"ks": 128, "kT": [KTS[s, r0:r0 + 128, j * 128:(j + 1) * 128]],
                                    "v": cv[l, s, j * 128:(j + 1) * 128, r0:r0 + 128], "cast": True} for j in range(npb)]
                            kbl.append({"ks": DEC_SEQ, "kT": [KT[r0:r0 + 128, tok0:tok0 + DEC_SEQ]],
                                        "v": VB[tok0:tok0 + DEC_SEQ, r0:r0 + 128], "cast": False})

                            def cls(j, qb):
                                return "diag" if j == npb else "far"
                            run_head(1, 128, q_src, DEC_SEQ, 256, kbl, cls, "sb", 0, r0, tok0)
                    for hh in range(4):
                        r0 = hh * 128
                        q_src = [MQT[r0:r0 + 128, tok0:tok0 + DEC_SEQ]]
                        kbl = [{"ks": 128, "kT": [MKTS[s, r0:r0 + 128, j * 128:(j + 1) * 128]],
                                "v": cmv[l, s, j * 128:(j + 1) * 128, r0:r0 + 128], "cast": True}
                               for j in range(NMEM // 128)]
                        run_head(1, 128, q_src, DEC_SEQ, 256, kbl, lambda j, qb: "far", "mem", 0, SW + r0, tok0)
                kb.barrier()
                kb.release(qTs + kTs + vss + tf + tg + tx + pb + osb + obf + oT + sm + raccs + [osq])

        def out_proj(l, src_tok):
            with ExitStack() as ph:
                rr = [0]
                wo = kb.sb(ph, "wo", [128, 16, D], BF16)
                wov = w_o[l].rearrange("(k p) f -> p k f", p=128)
                for k0 in range(0, 16, 2):
                    kb.dma(pool, wo[:, k0:k0 + 2, :], wov[:, k0:k0 + 2, :], wo, w=[wo], ndesc=256)
                g, b = load_ln_params(ph, l * 2 + 0)
                mts = [kb.sb(ph, f"c_mT{i}", [128, 16, 128], BF16) for i in range(2)]
                xs = [kb.sb(ph, f"c_x{i}", [128, D], F32) for i in range(2)]
                rs = [kb.sb(ph, f"c_r{i}", [128, D], F32) for i in range(2)]
                hs = [kb.sb(ph, f"c_h{i}", [128, D], F32) for i in range(2)]
                hb = [kb.sb(ph, f"c_hb{i}", [128, D], BF16) for i in range(2)]
                hT = [kb.sb(ph, f"c_hT{i}", [128, KD, 128], BF16) for i in range(2)]
                stat = [kb.sb(ph, f"c_st{i}", [128, 8], F32) for i in range(2)]
                for t in range(NT):
                    mT, x, r, hsb, hbb, hTt, stt = mts[t % 2], xs[t % 2], rs[t % 2], hs[t % 2], hb[t % 2], hT[t % 2], stat[t % 2]
                    rows = slice(t * 128, (t + 1) * 128)
                    kb.dma(sp, mT[:], MIXT[:, rows].rearrange("(c p) t -> p c t", p=128), mT, w=[mT], ndesc=2048)
                    kb.dma(sp, x[:], src_tok[rows, :], x, w=[x], ndesc=128)
                    for f0 in range(0, D, 512):
                        pp = kb.ps()
                        kb.mm(pp, [(pp[:, :], mT[:, k, :], wo[:, k, f0:f0 + 512]) for k in range(16)], r=[mT, wo])
                        kb.op(dve, lambda h: h.scalar_tensor_tensor(out=r[:, f0:f0 + 512], in0=x[:, f0:f0 + 512],
                                                                    scalar=cfg.alpha, in1=pp[:, :], op0=ALU.mult,
                                                                    op1=ALU.add), r=[x, pp], w=[r])
                    layer_norm(r, hsb, g, b, stt)
                    kb.dma(sp, H1[rows, :], hsb[:], hsb, r=[hsb], ndesc=128)
                    kb.op(act, lambda h: h.activation(out=hbb[:], in_=hsb[:], func=AF.Copy), r=[hsb], w=[hbb])
                    transpose_to_T(hbb, hTt, 0, KD, rr)
                    kb.dma(sp, H1T[:, rows].rearrange("(c p) t -> p c t", p=128), hTt[:], hTt, r=[hTt], ndesc=128 * KD)
                kb.barrier()
                kb.release([wo, g, b] + mts + xs + rs + hs + hb + hT + stat)

        def peer_retrieve(l, IDX1T, IDX2T, GT):
            with ExitStack() as ph:
                wq = kb.sb(ph, "wq", [128, KD, 2048], BF16)
                wqv = w_q[l].rearrange("(k p) f -> p k f", p=128)
                for k0 in range(0, KD, 2):
                    kb.dma(pool, wq[:, k0:k0 + 2, :], wqv[:, k0:k0 + 2, :], wq, w=[wq], ndesc=256)
                sk = kb.sb(ph, "sk", [128, 16, 128], BF16)
                kb.dma(pool, sk[:], subk[l].rearrange("g d n -> d g n"), sk, w=[sk], ndesc=2048)
                hTs = [kb.sb(ph, f"d_hT{i}", [128, KD, 128], BF16) for i in range(2)]
                qp = [kb.sb(ph, f"d_qp{i}", [128, 16, 128], BF16) for i in range(2)]
                sc = [kb.sb(ph, f"d_sc{i}", [128, 2048], F32) for i in range(2)]
                sc2 = kb.sb(ph, "d_sc2", [128, 2048], F32)
                sv = kb.sb(ph, "d_sv", [128, 16, 16], F32)
                si_u = kb.sb(ph, "d_siu", [128, 16, 16], U32)
                si_f = kb.sb(ph, "d_sif", [128, 16, 16], F32)
                cand = kb.sb(ph, "d_cand", [128, 8, 256], F32)
                cand2 = kb.sb(ph, "d_cand2", [128, 8, 256], F32)
                best = kb.sb(ph, "d_best", [128, 8, 16], F32)
                pos_u = kb.sb(ph, "d_posu", [128, 8, 16], U32)
                pa_u = kb.sb(ph, "d_pau", [128, 8, 16], U32)
                pb_u = kb.sb(ph, "d_pbu", [128, 8, 16], U32)
                pa_f = kb.sb(ph, "d_paf", [128, 8, 16], F32)
                pb_f = kb.sb(ph, "d_pbf", [128, 8, 16], F32)
                oh = kb.sb(ph, "d_oh", [128, 8, 16, 16], F32)
                outs = [kb.sb(ph, f"d_out{i}", [128, 3, 128], F32) for i in range(2)]
                gs = kb.sb(ph, "d_gs", [128, 8], F32)
                v_sv = [Buf(None, f"vsv{i}") for i in range(16)]
                v_si = [Buf(None, f"vsi{i}") for i in range(16)]
                v_sc2 = [Buf(None, f"vsc2{i}") for i in range(16)]
                v_best = [Buf(None, f"vb{i}") for i in range(8)]
                v_pos = [Buf(None, f"vp{i}") for i in range(8)]
                v_c2 = [Buf(None, f"vc2{i}") for i in range(8)]
                for t in range(NT):
                    hTt, qpt, sct, ot = hTs[t % 2], qp[t % 2], sc[t % 2], outs[t % 2]
                    rows = slice(t * 128, (t + 1) * 128)
                    kb.dma(sp, hTt[:], H1T[:, rows].rearrange("(c p) t -> p c t", p=128), hTt, w=[hTt], ndesc=128 * KD)
                    for g0 in range(0, 16, 4):
                        pp = kb.ps()
                        kb.mm_multi([pp], [[(pp[:, gg * 128:(gg + 1) * 128], wq[:, k, (g0 + gg) * 128:(g0 + gg + 1) * 128],
                                             hTt[:, k, :]) for k in range(KD)] for gg in range(4)], r=[wq, hTt])
                        kb.op(act, lambda h: h.activation(out=qpt[:, g0:g0 + 4, :],
                                                          in_=pp[:, :].rearrange("p (g t) -> p g t", g=4), func=AF.Copy),
                              r=[pp], w=[qpt])
                    for g0 in range(0, 16, 4):
                        pp = kb.ps()
                        kb.mm_multi([pp], [[(pp[:, gg * 128:(gg + 1) * 128], qpt[:, g0 + gg, :], sk[:, g0 + gg, :])]
                                           for gg in range(4)], r=[qpt, sk])
                        kb.op(act, lambda h: h.activation(out=sct[:, g0 * 128:(g0 + 4) * 128], in_=pp[:, :], func=AF.Copy),
                              r=[pp], w=[sct])
                    for stg in range(5):
                        for gI in range(16):
                            seg = sct[:, gI * 128:(gI + 1) * 128]
                            seg2 = sc2[:, gI * 128:(gI + 1) * 128]
                            vsv, vsi, vs2 = v_sv[gI], v_si[gI], v_sc2[gI]
                            if stg == 0:
                                kb.op(dve, lambda h: h.max(out=sv[:, gI, 0:8], in_=seg), r=[sct], w=[vsv])
                            elif stg == 1:
                                kb.op(dve, lambda h: h.max_index(out=si_u[:, gI, 0:8], in_max=sv[:, gI, 0:8], in_values=seg),
                                      r=[sct, vsv], w=[vsi])
                            elif stg == 2:
                                kb.op(dve, lambda h: h.match_replace(out=seg2, in_to_replace=sv[:, gI, 0:8], in_values=seg,
                                                                     imm_value=-1e30), r=[sct, vsv], w=[vs2])
                            elif stg == 3:
                                kb.op(dve, lambda h: h.max(out=sv[:, gI, 8:16], in_=seg2), r=[vs2], w=[vsv])
                            else:
                                kb.op(dve, lambda h: h.max_index(out=si_u[:, gI, 8:16], in_max=sv[:, gI, 8:16],
                                                                 in_values=seg2), r=[vs2, vsv], w=[vsi])
                    kb.op(dve, lambda h: h.tensor_copy(out=si_f[:], in_=si_u[:]), r=v_si, w=[si_f])
                    sv4 = sv[:].rearrange("p (h c) k -> p h c k", c=2)
                    si4 = si_f[:].rearrange("p (h c) k -> p h c k", c=2)
                    kb.op(dve, lambda h: h.tensor_tensor(out=cand[:].rearrange("p h (a b) -> p h a b", a=16),
                                                         in0=sv4[:, :, 0, :].unsqueeze(3).to_broadcast([128, 8, 16, 16]),
                                                         in1=sv4[:, :, 1, :].unsqueeze(2).to_broadcast([128, 8, 16, 16]),
                                                         op=ALU.add), r=v_sv, w=[cand])
                    for stg in range(5):
                        for hh in range(8):
                            vb, vp, vc2 = v_best[hh], v_pos[hh], v_c2[hh]
                            if stg == 0:
                                kb.op(dve, lambda h: h.max(out=best[:, hh, 0:8], in_=cand[:, hh, :]), r=[cand], w=[vb])
                            elif stg == 1:
                                kb.op(dve, lambda h: h.max_index(out=pos_u[:, hh, 0:8], in_max=best[:, hh, 0:8],
                                                                 in_values=cand[:, hh, :]), r=[cand, vb], w=[vp])
                            elif stg == 2:
                                kb.op(dve, lambda h: h.match_replace(out=cand2[:, hh, :], in_to_replace=best[:, hh, 0:8],
                                                                     in_values=cand[:, hh, :], imm_value=-1e30),
                                      r=[cand, vb], w=[vc2])
                            elif stg == 3:
                                kb.op(dve, lambda h: h.max(out=best[:, hh, 8:16], in_=cand2[:, hh, :]), r=[vc2], w=[vb])
                            else:
                                kb.op(dve, lambda h: h.max_index(out=pos_u[:, hh, 8:16], in_max=best[:, hh, 8:16],
                                                                 in_values=cand2[:, hh, :]), r=[vc2, vb], w=[vp])
                    kb.op(dve, lambda h: h.tensor_single_scalar(out=pa_u[:], in_=pos_u[:], scalar=4,
                                                                op=ALU.logical_shift_right), r=v_pos, w=[pa_u])
                    kb.op(dve, lambda h: h.tensor_single_scalar(out=pb_u[:], in_=pos_u[:], scalar=15, op=ALU.bitwise_and),
                          r=v_pos, w=[pb_u])
                    kb.op(dve, lambda h: h.tensor_copy(out=pa_f[:], in_=pa_u[:]), r=[pa_u], w=[pa_f])
                    kb.op(dve, lambda h: h.tensor_copy(out=pb_f[:], in_=pb_u[:]), r=[pb_u], w=[pb_f])
                    for which, pf in ((0, pa_f), (1, pb_f)):
                        kb.op(dve, lambda h: h.tensor_tensor(out=oh[:], in0=pf[:].unsqueeze(3).to_broadcast([128, 8, 16, 16]),
                                                             in1=iota[:, 0:16].unsqueeze(1).unsqueeze(1).to_broadcast([128, 8, 16, 16]),
                                                             op=ALU.is_equal), r=[pf, iota], w=[oh])
                        kb.op(dve, lambda h: h.tensor_tensor(out=oh[:], in0=oh[:],
                                                             in1=si4[:, :, which, :].unsqueeze(2).to_broadcast([128, 8, 16, 16]),
                                                             op=ALU.mult), r=[oh, si_f], w=[oh])
                        kb.op(dve, lambda h: h.tensor_reduce(out=ot[:, which, :].rearrange("p (h k) -> p h k", h=8),
                                                             in_=oh[:], axis=AX.X, op=ALU.add), r=[oh], w=[ot])
                    gv = ot[:, 2, :].rearrange("p (h k) -> p h k", h=8)
                    kb.op(dve, lambda h: h.tensor_tensor(out=gv, in0=best[:], in1=best[:, :, 0:1].to_broadcast([128, 8, 16]),
                                                         op=ALU.subtract), r=v_best, w=[ot])
                    kb.op(act, lambda h: h.activation(out=gv, in_=gv, func=AF.Exp), r=[ot], w=[ot])
                    kb.op(dve, lambda h: h.tensor_reduce(out=gs[:], in_=gv, axis=AX.X, op=ALU.add), r=[ot], w=[gs])
                    kb.op(dve, lambda h: h.reciprocal(out=gs[:], in_=gs[:]), r=[gs], w=[gs])
                    kb.op(dve, lambda h: h.tensor_tensor(out=gv, in0=gv, in1=gs[:].unsqueeze(2).to_broadcast([128, 8, 16]),
                                                         op=ALU.mult), r=[ot, gs], w=[ot])
                    pp = kb.ps()
                    for w3 in range(3):
                        kb.transpose(pp, pp[:, w3 * 128:(w3 + 1) * 128], ot[:, w3, :], identf[:], r=[ot, identf])
                    for w3, dstb in enumerate((IDX1T, IDX2T, GT)):
                        kb.op(act, lambda h: h.activation(out=dstb[:, rows], in_=pp[:, w3 * 128:(w3 + 1) * 128], func=AF.Copy),
                              r=[pp], w=[dstb])
                kb.barrier()
                kb.release([wq, sk] + hTs)

        def peer_wgen(l, IDX1T, IDX2T, GT):
            with ExitStack() as ph:
                Ps = [kb.sb(ph, f"e_P{i}", [128, 128, 128], BF16) for i in range(1)]
                Qs = [kb.sb(ph, f"e_Q{i}", [128, 128, 128], BF16) for i in range(1)]
                Ws = [kb.sb(ph, f"e_W{i}", [128, 128, 128], BF16) for i in range(2)]
                for t in range(NT):
                    P, Q, W = Ps[0], Qs[0], Ws[t % 2]
                    rows = slice(t * 128, (t + 1) * 128)
                    io3 = iotab[:].unsqueeze(1).to_broadcast([128, 128, 128])
                    kb.op(dve, lambda h: h.tensor_tensor(out=Q[:], in0=io3,
                                                         in1=IDX2T[:, rows].unsqueeze(2).to_broadcast([128, 128, 128]),
                                                         op=ALU.is_equal), r=[iotab, IDX2T], w=[Q])
                    kb.op(dve, lambda h: h.tensor_tensor(out=P[:], in0=io3,
                                                         in1=IDX1T[:, rows].unsqueeze(2).to_broadcast([128, 128, 128]),
                                                         op=ALU.is_equal), r=[iotab, IDX1T], w=[P])
                    kb.op(pool, lambda h: h.tensor_tensor(out=P[:], in0=P[:],
                                                          in1=GT[:, rows].unsqueeze(2).to_broadcast([128, 128, 128]),
                                                          op=ALU.mult), r=[P, GT], w=[P])
                    for t4 in range(0, 128, 4):
                        pp = kb.ps()
                        kb.mm_multi([pp], [[(pp[:, j * 128:(j + 1) * 128], Q[:, t4 + j, :], P[:, t4 + j, :])] for j in range(4)],
                                    r=[P, Q])
                        src = pp[:, :].rearrange("p (t i) -> p i t", t=4)
                        if (t4 // 4) % 2 == 0:
                            kb.op(act, lambda h: h.activation(out=W[:, :, t4:t4 + 4], in_=src, func=AF.Copy), r=[pp], w=[W])
                        else:
                            kb.op(dve, lambda h: h.tensor_copy(out=W[:, :, t4:t4 + 4], in_=src), r=[pp], w=[W])
                    for c0 in range(0, 128, 32):
                        kb.dma(sp, WT[t, :, c0:c0 + 32, :], W[:, c0:c0 + 32, :], W, r=[W], ndesc=128)
                kb.barrier()
                kb.release(Ps + Qs + Ws)

        def peer_dense(l, dst_tok, final):
            SC = 4
            nsc = 128 // SC
            utv = ut[l].rearrange("(k p) e -> p k e", p=128)
            vvv = vv[l].rearrange("(c p) d -> p c d", p=128)
            maxpass = 6
            passes = []
            t = 0
            while t < NT:
                n = min(maxpass, NT - t)
                passes.append((t, n))
                t += n
            with ExitStack() as ph:
                rr = [0]
                g, b = load_ln_params(ph, l * 2 + 1)
                acc = kb.sb(ph, "f_acc", [128, maxpass, D], F32)
                hTp = kb.sb(ph, "f_hT", [128, KD, maxpass * 128], BF16)
                us = [kb.sb(ph, f"f_u{i}", [128, KD, SC * 128], BF16) for i in range(2)]
                vs_ = [kb.sb(ph, f"f_v{i}", [128, SC, D], BF16) for i in range(2)]
                gsb = [kb.sb(ph, f"f_g{i}", [128, SC, 512], BF16) for i in range(2)]
                wts = [kb.sb(ph, f"f_w{i}", [128, SC, 512], BF16) for i in range(2)]
                xs = [kb.sb(ph, f"f_x{i}", [128, D], F32) for i in range(1)]
                ys = [kb.sb(ph, f"f_y{i}", [128, D], F32) for i in range(1)]
                stat = [kb.sb(ph, f"f_st{i}", [128, 8], F32) for i in range(2)]
                if os.environ.get('MK_DEBUG'):
                    print('dense sbuf remaining', nc.sbuf_bytes_remaining)
                ui = 0
                gi = 0
                for (pt0, pn) in passes:
                    ntp = pn * 128
                    tok0 = pt0 * 128
                    kb.dma(sp, hTp[:, :, 0:ntp], H1T[:, tok0:tok0 + ntp].rearrange("(c p) t -> p c t", p=128), hTp, w=[hTp],
                           ndesc=128 * KD)
                    for sc_i in range(nsc):
                        u, v_ = us[ui % 2], vs_[ui % 2]
                        ui += 1
                        e0 = sc_i * SC * 128
                        kq = max(1, KD // 4)
                        for k0 in range(0, KD, kq):
                            kb.dma(pool, u[:, k0:k0 + kq, :], utv[:, k0:k0 + kq, e0:e0 + SC * 128], u, w=[u], ndesc=128 * kq)
                        for c in range(SC):
                            kb.dma(pool, v_[:, c, :], vvv[:, sc_i * SC + c, :], v_, w=[v_], ndesc=128)
                        for g0 in range(0, pn, 4):
                            ng = min(4, pn - g0)
                            ntk = ng * 128
                            G, Wt = gsb[gi % 2], wts[gi % 2]
                            H = G
                            gi += 1
                            for tt in range(ng):
                                kb.dma(sp, Wt[:, :, tt * 128:(tt + 1) * 128], WT[pt0 + g0 + tt, :, sc_i * SC:(sc_i + 1) * SC, :],
                                       Wt, w=[Wt], ndesc=128)
                            for c in range(SC):
                                pp = kb.ps()
                                kb.mm(pp, [(pp[:, 0:ntk], u[:, k, c * 128:(c + 1) * 128], hTp[:, k, g0 * 128:g0 * 128 + ntk])
                                           for k in range(KD)], r=[u, hTp])
                                kb.op(act, lambda h: h.activation(out=G[:, c, 0:ntk], in_=pp[:, 0:ntk], func=AF.Gelu), r=[pp],
                                      w=[G])
                            kb.op(dve, lambda h: h.tensor_tensor(out=H[:, :, 0:ntk], in0=G[:, :, 0:ntk], in1=Wt[:, :, 0:ntk],
                                                                 op=ALU.mult), r=[G, Wt], w=[H])
                            for tt in range(ng):
                                ti = g0 + tt
                                for f0 in range(0, D, 512):
                                    pp = kb.ps()
                                    kb.mm(pp, [(pp[:, :], H[:, c, tt * 128:(tt + 1) * 128], v_[:, c, f0:f0 + 512])
                                               for c in range(SC)], r=[H, v_])
                                    if sc_i == 0:
                                        kb.op(dve, lambda h: h.tensor_copy(out=acc[:, ti, f0:f0 + 512], in_=pp[:, :]), r=[pp],
                                              w=[acc])
                                    else:
                                        kb.op(dve, lambda h: h.tensor_tensor(out=acc[:, ti, f0:f0 + 512],
                                                                             in0=acc[:, ti, f0:f0 + 512], in1=pp[:, :],
                                                                             op=ALU.add), r=[pp, acc], w=[acc])
                    for ti in range(pn):
                        x, yb, stt = xs[0], ys[0], stat[ti % 2]
                        rows = slice(tok0 + ti * 128, tok0 + (ti + 1) * 128)
                        kb.dma(sp, x[:], H1[rows, :], x, w=[x], ndesc=128)
                        kb.op(dve, lambda h: h.scalar_tensor_tensor(out=x[:], in0=x[:], scalar=cfg.alpha, in1=acc[:, ti, :],
                                                                    op0=ALU.mult, op1=ALU.add), r=[x, acc], w=[x])
                        layer_norm(x, yb, g, b, stt)
                        kb.dma(sp, dst_tok[rows, :], yb[:], yb, r=[yb], ndesc=128)
                kb.barrier()
                kb.release([g, b, acc, hTp] + us + vs_ + gsb + wts + xs + ys + stat)

        stop = cfg.stop
        for l in range(L if stop != ("setup", 0) else 0):
            src_tok = xin if l == 0 else H2
            dst_tok = y if l == L - 1 else H2
            fm = [(0, 12, QT), (SW, 12, KT), (3 * SW, 4, MQT)]
            tm = [(SW, SW, nk[l], None), (2 * SW, SW, nv[l], VB)]
            project(src_tok, NT, w_in[l], 5120, fm, tm, KD, "pa")
            project(memp, NMT // 128, w_mkv[l], 1024, [(0, 4, MKT)], [(0, 512, nmk[l], None), (512, 512, nmv[l], MVB)], KD,
                    "pm")
            if stop == ("proj", l):
                break
            cache_transposes(l)
            attention(l)
            if stop == ("attn", l):
                break
            out_proj(l, src_tok)
            if stop == ("oproj", l):
                break
            with ExitStack() as pph:
                IDX1T = kb.sb(pph, "IDX1T", [128, NTOK], F32)
                IDX2T = kb.sb(pph, "IDX2T", [128, NTOK], F32)
                GT = kb.sb(pph, "GT", [128, NTOK], F32)
                peer_retrieve(l, IDX1T, IDX2T, GT)
                peer_wgen(l, IDX1T, IDX2T, GT)
            if stop == ("wgen", l):
                break
            peer_dense(l, dst_tok, l == L - 1)
            if stop == ("dense", l):
                break
        kb.barrier()
    return nc


def shard_inputs(cfg, inputs, n_cores):
    c = make_consts()
    maps = []
    L = cfg.DEPTH
    u_t = np.ascontiguousarray(np.transpose(inputs["peer_u"], (0, 2, 1)))
    subk = np.ascontiguousarray(np.transpose(inputs["peer_sub_keys"], (0, 1, 2, 4, 3))).reshape(L, 16, 128, 128)
    w_mkv = np.ascontiguousarray(np.concatenate([inputs["w_mem_k"], inputs["w_mem_v"]], axis=2))
    for ci in range(n_cores):
        ps = slice(ci * cfg.NSEQ, (ci + 1) * cfg.NSEQ)
        ss = slice(ci * cfg.NSMP, (ci + 1) * cfg.NSMP)
        xin = np.concatenate([inputs["x_prompt"][ps].reshape(-1, cfg.D), inputs["x_sample"][ss].reshape(-1, cfg.D)], axis=0)
        m = {
            "xin": np.ascontiguousarray(xin),
            "w_in": inputs["w_in"], "w_o": inputs["w_o"], "w_mkv": w_mkv, "w_q": inputs["peer_w_q"],
            "subk": subk, "ut": u_t, "vv": inputs["peer_v"],
            "ck": np.ascontiguousarray(inputs["cache_self_k"][:, ss]),
            "cv": np.ascontiguousarray(inputs["cache_self_v"][:, ss]),
            "cmk": np.ascontiguousarray(inputs["cache_mem_k"][:, ss]).reshape(L, cfg.NSMP, NMEM, 512),
            "cmv": np.ascontiguousarray(inputs["cache_mem_v"][:, ss]).reshape(L, cfg.NSMP, NMEM, 512),
            "memp": np.ascontiguousarray(inputs["mem_prompt"][ps]).reshape(-1, cfg.D),
            "relb": inputs["rel_bias_table"],
            "lamv": inputs["diff_lambda"].reshape(1, 512),
            "subg": inputs["diff_subln_g"].reshape(1, 256),
            "lng": inputs["ln_g"].reshape(L * 2, cfg.D), "lnb": inputs["ln_b"].reshape(L * 2, cfg.D),
        }
        for k, v in c.items():
            m["c_" + k] = v
        maps.append({k: np.ascontiguousarray(v, dtype=np.float32) for k, v in m.items()})
    return maps


def assemble(cfg, results, n_cores):
    L = cfg.DEPTH
    NP = cfg.NP
    yp = np.concatenate([r["y"][:NP].reshape(cfg.NSEQ, cfg.SEQ, cfg.D) for r in results], axis=0)
    ys = np.concatenate([r["y"][NP:].reshape(cfg.NSMP, DEC_SEQ, cfg.D) for r in results], axis=0)
    nkp = np.concatenate([r["nk"][:, :NP].reshape(L, cfg.NSEQ, cfg.SEQ, SW) for r in results], axis=1)
    nvp = np.concatenate([r["nv"][:, :NP].reshape(L, cfg.NSEQ, cfg.SEQ, SW) for r in results], axis=1)
    nks = np.concatenate([r["nk"][:, NP:].reshape(L, cfg.NSMP, DEC_SEQ, SW) for r in results], axis=1)
    nvs = np.concatenate([r["nv"][:, NP:].reshape(L, cfg.NSMP, DEC_SEQ, SW) for r in results], axis=1)
    nmk = np.concatenate([r["nmk"].reshape(L, cfg.NSEQ, NMEM, 4, 128) for r in results], axis=1)
    nmv = np.concatenate([r["nmv"].reshape(L, cfg.NSEQ, NMEM, 4, 128) for r in results], axis=1)
    return (yp, ys, nkp, nvp, nmk, nmv, nks, nvs)


def kernel(**inputs):
    n_cores = 8
    inputs = {k: np.asarray(v) for k, v in inputs.items()}
    B, SEQ, D = inputs["x_prompt"].shape
    DB = inputs["x_sample"].shape[0]
    PAST = inputs["cache_self_k"].shape[2]
    cfg = Cfg(D=D, SEQ=SEQ, NSEQ=B // n_cores, NSMP=DB // n_cores, PAST=PAST, DEPTH=inputs["w_in"].shape[0])
    nc = build(cfg)
    maps = shard_inputs(cfg, inputs, n_cores)
    res = run_bass_kernel_spmd(nc, maps, core_ids=list(range(n_cores)))
    return assemble(cfg, res.results, n_cores)
```

```python
import math
import os
from contextlib import ExitStack
import numpy as np
import concourse.bass as bass
import concourse.mybir as mybir
from concourse.bass_utils import run_bass_kernel_spmd

F32 = mybir.dt.float32
BF16 = mybir.dt.bfloat16
U32 = mybir.dt.uint32
AF = mybir.ActivationFunctionType
ALU = mybir.AluOpType
AX = mybir.AxisListType

NEG = -30000.0
LN_EPS = 1e-5
SW = 1536
NMEM = 256
DEC_SEQ = 32
NEXP = 16384


class Cfg:
    def __init__(self, D=2048, SEQ=2048, NSEQ=2, NSMP=4, PAST=2048, DEPTH=2, debug=False, stop=None):
        self.D, self.SEQ, self.NSEQ, self.NSMP, self.PAST, self.DEPTH = D, SEQ, NSEQ, NSMP, PAST, DEPTH
        self.KD = D // 128
        self.NP = NSEQ * SEQ
        self.NTOK = self.NP + NSMP * DEC_SEQ
        assert self.NTOK % 128 == 0 and SEQ % 256 == 0 and PAST % 128 == 0
        self.NT = self.NTOK // 128
        self.NMT = NSEQ * NMEM
        self.debug = debug
        self.stop = stop
        self.alpha = (2 * DEPTH) ** 0.25


class Sem:
    def __init__(self, h):
        self.h = h
        self.count = 0


class Buf:
    def __init__(self, t, name=""):
        self.t = t
        self.name = name
        self.w = None
        self.r = {}
        self.dsem = None
        self.dsem_sw = None
        self.excl = False

    def __getitem__(self, k):
        return self.t[k]


class Eng:
    def __init__(self, h, sem, name):
        self.h = h
        self.sem = sem
        self.name = name
        self.seen = {}
        self.outst = []


class KB:
    def __init__(self, nc, st):
        self.nc = nc
        self.st = st
        mk = lambda n: Sem(st.enter_context(nc.semaphore(n)))
        self.pe = Eng(nc.tensor, mk("s_pe"), "pe")
        self.act = Eng(nc.scalar, mk("s_act"), "act")
        self.dve = Eng(nc.vector, mk("s_dve"), "dve")
        self.pool = Eng(nc.gpsimd, mk("s_pool"), "pool")
        self.sp = Eng(nc.sync, mk("s_sp"), "sp")
        self.engs = [self.pe, self.act, self.dve, self.pool, self.sp]
        self.dsems = [mk(f"s_d{i}") for i in range(48)]
        self.free_dsems = list(self.dsems)
        self.dsems_sw = [mk(f"s_w{i}") for i in range(28)]
        self.free_dsems_sw = list(self.dsems_sw)
        self.psum = []
        for i in range(8):
            t = st.enter_context(nc.psum_tensor(f"ps{i}", [128, 512], F32))
            self.psum.append(Buf(t, f"ps{i}"))
            self.psum[-1].excl = True
        self.ps_rr = 0
        self.reserved = set()

    def sb(self, ph, name, shape, dt):
        self.uid = getattr(self, "uid", 0) + 1
        name = f"{name}_{self.uid}"
        return Buf(ph.enter_context(self.nc.sbuf_tensor(name, list(shape), dt)), name)

    def release(self, bufs):
        for b in bufs:
            if b.dsem is not None:
                self.free_dsems.append(b.dsem)
                b.dsem = None
            if b.dsem_sw is not None:
                self.free_dsems_sw.append(b.dsem_sw)
                b.dsem_sw = None

    def ps(self, reserve=False):
        while True:
            b = self.psum[self.ps_rr % 8]
            self.ps_rr += 1
            if id(b) not in self.reserved:
                break
        if reserve:
            self.reserved.add(id(b))
        return b

    def unreserve(self, b):
        self.reserved.discard(id(b))

    def _wait(self, e, evs):
        best = {}
        for (s, v) in evs:
            if v > best.get(id(s), (s, 0))[1]:
                best[id(s)] = (s, v)
        for (s, v) in best.values():
            if e is self.pe and s is e.sem:
                continue
            if e.seen.get(id(s), 0) < v:
                e.h.wait_ge(s.h, v)
                e.seen[id(s)] = v

    def _deps(self, r, w):
        evs = []
        for b in r:
            if b.w is not None:
                evs.append(b.w)
            if b.excl:
                evs.extend(b.r.values())
        for b in w:
            if b.w is not None:
                evs.append(b.w)
            evs.extend(b.r.values())
        return evs

    def _record(self, ev, r, w):
        for b in r:
            b.r[id(ev[0])] = ev
        for b in w:
            b.w = ev
            b.r = {}

    def op(self, e, fn, r=(), w=()):
        self._wait(e, self._deps(r, w))
        ins = fn(e.h)
        e.sem.count += 1
        ins.then_inc(e.sem.h, 1)
        ev = (e.sem, e.sem.count)
        self._record(ev, r, w)
        return ev

    def mm(self, out_buf, mms, r=(), extra_w=()):
        e = self.pe
        self._wait(e, self._deps(r, [out_buf] + list(extra_w)))
        n = len(mms)
        ins = None
        for i, (o, l, rh) in enumerate(mms):
            ins = e.h.matmul(o, l, rh, start=(i == 0), stop=(i == n - 1))
        e.sem.count += 1
        ins.then_inc(e.sem.h, 1)
        ev = (e.sem, e.sem.count)
        self._record(ev, r, [out_buf] + list(extra_w))
        return ev

    def mm_multi(self, out_bufs, groups, r=()):
        e = self.pe
        self._wait(e, self._deps(r, out_bufs))
        ins = None
        for g in groups:
            n = len(g)
            for i, (o, l, rh) in enumerate(g):
                ins = e.h.matmul(o, l, rh, start=(i == 0), stop=(i == n - 1))
        e.sem.count += 1
        ins.then_inc(e.sem.h, 1)
        ev = (e.sem, e.sem.count)
        self._record(ev, r, out_bufs)
        return ev

    def transpose(self, out_buf, out_ap, in_ap, ident_ap, r=()):
        e = self.pe
        self._wait(e, self._deps(r, [out_buf]))
        ins = e.h.transpose(out_ap, in_ap, ident_ap)
        e.sem.count += 1
        ins.then_inc(e.sem.h, 1)
        ev = (e.sem, e.sem.count)
        self._record(ev, r, [out_buf])
        return ev

    def dma(self, e, out, in_, sbuf, r=(), w=(), ndesc=256):
        if e is self.pool:
            if sbuf.dsem_sw is None:
                sbuf.dsem_sw = self.free_dsems_sw.pop(0)
            ds = sbuf.dsem_sw
        else:
            if sbuf.dsem is None:
                sbuf.dsem = self.free_dsems.pop(0)
            ds = sbuf.dsem
        lim = 1536 if e is self.pool else 6144
        tot = sum(n for _, n in e.outst) + ndesc
        evs = self._deps(r, w)
        while e.outst and tot > lim:
            ev0, n0 = e.outst.pop(0)
            evs.append((ev0[0], ev0[0].count))
            tot -= n0
        self._wait(e, evs)
        ins = e.h.dma_start(out=out, in_=in_)
        ds.count += 16
        ins.then_inc(ds.h, 16)
        ev = (ds, ds.count)
        e.outst.append((ev, ndesc))
        self._record(ev, r, w)
        return ev

    def barrier(self):
        evs = [(e.sem, e.sem.count) for e in self.engs if e.sem.count > 0]
        evs += [(d, d.count) for d in self.dsems + self.dsems_sw if d.count > 0]
        for e in self.engs:
            self._wait(e, evs)
            e.outst = []


def t5_bucket_np(rel):
    nb = 16
    max_exact = 8
    ret = np.where(rel > 0, nb, 0)
    n = np.abs(rel)
    nf = np.maximum(n, 1).astype(np.float32)
    large = max_exact + (np.log(nf / max_exact) / math.log(128 / max_exact) * (nb - max_exact)).astype(np.int32)
    large = np.minimum(large, nb - 1)
    return ret + np.where(n < max_exact, n, large)


def make_consts():
    c = {}
    c["ident"] = np.eye(128, dtype=np.float32)
    c["iota"] = np.tile(np.arange(128, dtype=np.float32)[None, :], (128, 1))
    k = np.arange(128)[:, None]
    q = np.arange(128)[None, :]
    c["tri"] = (k >= q).astype(np.float32)
    c["ones"] = np.ones((128, 128), np.float32)
    c["jx"] = np.eye(128, dtype=np.float32)[::-1].copy()
    c["maskd"] = np.where((k // 64) <= (q // 64), 0.0, NEG).astype(np.float32)
    c["masksb"] = (k < q).astype(np.float32)
    rel = np.arange(384) - 255
    bk = t5_bucket_np(rel.astype(np.int32))
    oh = np.zeros((32, 384), np.float32)
    oh[bk, np.arange(384)] = 1.0
    c["ohr"] = oh[:, ::-1].copy()
    return c


CONST_SHAPES = {"ident": [128, 128], "iota": [128, 128], "tri": [128, 128], "ones": [128, 128],
                "jx": [128, 128], "maskd": [128, 128], "masksb": [128, 128], "ohr": [32, 384]}


def build(cfg):
    nc = bass.Bass("TRN2", target_bir_lowering=False)
    D, KD, NTOK, NT, L = cfg.D, cfg.KD, cfg.NTOK, cfg.NT, cfg.DEPTH
    NSEQ, NSMP, SEQ, PAST, NMT = cfg.NSEQ, cfg.NSMP, cfg.SEQ, cfg.PAST, cfg.NMT

    def din(name, shape, dt=F32):
        return nc.dram_tensor(name, list(shape), dt, kind="ExternalInput").ap()

    def dout(name, shape, dt=F32):
        return nc.dram_tensor(name, list(shape), dt, kind="ExternalOutput").ap()

    def dscr(name, shape, dt):
        return nc.dram_tensor(name, list(shape), dt, kind="ExternalOutput" if cfg.debug else "Internal").ap()

    xin = din("xin", [NTOK, D])
    w_in = din("w_in", [L, D, 5120])
    w_o = din("w_o", [L, 2048, D])
    w_mkv = din("w_mkv", [L, D, 1024])
    w_q = din("w_q", [L, D, 2048])
    subk = din("subk", [L, 16, 128, 128])
    ut = din("ut", [L, D, NEXP])
    vv = din("vv", [L, NEXP, D])
    ck = din("ck", [L, NSMP, PAST, SW])
    cv = din("cv", [L, NSMP, PAST, SW])
    cmk = din("cmk", [L, NSMP, NMEM, 512])
    cmv = din("cmv", [L, NSMP, NMEM, 512])
    memp = din("memp", [NMT, D])
    relb = din("relb", [32, 6])
    lamv = din("lamv", [1, 512])
    subg = din("subg", [1, 256])
    lng = din("lng", [L * 2, D])
    lnb = din("lnb", [L * 2, D])
    cst = {k: din("c_" + k, s) for k, s in CONST_SHAPES.items()}

    y = dout("y", [NTOK, D])
    nk = dout("nk", [L, NTOK, SW])
    nv = dout("nv", [L, NTOK, SW])
    nmk = dout("nmk", [L, NMT, 512])
    nmv = dout("nmv", [L, NMT, 512])

    QT = dscr("QT", [SW, NTOK], BF16)
    KT = dscr("KT", [SW, NTOK], BF16)
    MQT = dscr("MQT", [512, NTOK], BF16)
    VB = dscr("VB", [NTOK, SW], BF16)
    MKT = dscr("MKT", [512, NMT], BF16)
    MVB = dscr("MVB", [NMT, 512], BF16)
    KTS = dscr("KTS", [NSMP, SW, PAST], BF16)
    MKTS = dscr("MKTS", [NSMP, 512, NMEM], BF16)
    MIXT = dscr("MIXT", [2048, NTOK], BF16)
    H1 = dscr("H1", [NTOK, D], F32)
    H1T = dscr("H1T", [D, NTOK], BF16)
    H2 = dscr("H2", [NTOK, D], F32)
    WT = dscr("WT", [NT, 128, 128, 128], BF16)
    BIASV = dscr("BIASV", [6, 384], F32)

    with ExitStack() as st:
        kb = KB(nc, st)
        pe, act, dve, pool, sp = kb.pe, kb.act, kb.dve, kb.pool, kb.sp

        identf = kb.sb(st, "identf", [128, 128], F32)
        identb = kb.sb(st, "identb", [128, 128], BF16)
        iota = kb.sb(st, "iota", [128, 128], F32)
        iotab = kb.sb(st, "iotab", [128, 128], BF16)
        trib = kb.sb(st, "trib", [128, 128], F32)
        tribb = kb.sb(st, "tribb", [128, 128], BF16)
        onesb = kb.sb(st, "onesb", [128, 128], BF16)
        onesf = kb.sb(st, "onesf", [128, 128], F32)
        jx = kb.sb(st, "jx", [128, 128], F32)
        maskd = kb.sb(st, "maskd", [128, 128], F32)
        masksb = kb.sb(st, "masksb", [128, 128], F32)
        lam_t = kb.sb(st, "lam_t", [128, 4], F32)
        subg_bc = kb.sb(st, "subg_bc", [128, 256], F32)
        cfar = kb.sb(st, "cfar", [128, 6], F32)
        zero_c = kb.sb(st, "zero_c", [128, 1], F32)
        bias_t = kb.sb(st, "bias_t", [128, 6, 2, 128], F32)

        kb.dma(sp, identf[:], cst["ident"], identf, w=[identf])
        kb.dma(pool, identb[:], cst["ident"], identb, w=[identb])
        kb.dma(sp, iota[:], cst["iota"], iota, w=[iota])
        kb.dma(pool, iotab[:], cst["iota"], iotab, w=[iotab])
        kb.dma(sp, trib[:], cst["tri"], trib, w=[trib])
        kb.dma(pool, tribb[:], cst["tri"], tribb, w=[tribb])
        kb.dma(pool, onesb[:], cst["ones"], onesb, w=[onesb])
        kb.dma(sp, onesf[:], cst["ones"], onesf, w=[onesf])
        kb.dma(sp, jx[:], cst["jx"], jx, w=[jx])
        kb.dma(sp, maskd[:], cst["maskd"], maskd, w=[maskd])
        kb.dma(sp, masksb[:], cst["masksb"], masksb, w=[masksb])
        kb.dma(sp, subg_bc[:], subg.partition_broadcast(128), subg_bc, w=[subg_bc])
        kb.dma(sp, cfar[:], relb[15:16, :].partition_broadcast(128), cfar, w=[cfar])
        kb.op(dve, lambda h: h.memset(zero_c[:], 0.0), w=[zero_c])

        with ExitStack() as ph:
            lbc = kb.sb(ph, "lbc", [128, 512], F32)
            lpr = kb.sb(ph, "lpr", [128, 256], F32)
            lsm = kb.sb(ph, "lsm", [128, 2], F32)
            kb.dma(sp, lbc[:], lamv.partition_broadcast(128), lbc, w=[lbc])
            l4 = lbc[:].rearrange("p (a b) -> p a b", a=4)
            kb.op(dve, lambda h: h.tensor_tensor(out=lpr[:].rearrange("p (a b) -> p a b", a=2), in0=l4[:, 0:4:2, :],
                                                 in1=l4[:, 1:4:2, :], op=ALU.mult), r=[lbc], w=[lpr])
            kb.op(dve, lambda h: h.reduce_sum(out=lsm[:], in_=lpr[:].rearrange("p (a b) -> p a b", a=2), axis=AX.X),
                  r=[lpr], w=[lsm])
            kb.op(act, lambda h: h.activation(out=lsm[:], in_=lsm[:], func=AF.Exp), r=[lsm], w=[lsm])
            lam_init0 = 0.8 - 0.6 * math.exp(-0.3 * 0)
            kb.op(dve, lambda h: h.tensor_tensor(out=lam_t[:, 0:1], in0=lsm[:, 0:1], in1=lsm[:, 1:2], op=ALU.subtract),
                  r=[lsm], w=[lam_t])
            kb.op(dve, lambda h: h.tensor_scalar(out=lam_t[:, 0:1], in0=lam_t[:, 0:1], scalar1=lam_init0, scalar2=None,
                                                 op0=ALU.add), r=[lam_t], w=[lam_t])
            kb.op(dve, lambda h: h.tensor_scalar(out=lam_t[:, 1:2], in0=lam_t[:, 0:1], scalar1=-1.0, scalar2=None,
                                                 op0=ALU.mult), r=[lam_t], w=[lam_t])
            kb.op(dve, lambda h: h.tensor_scalar(out=subg_bc[:], in0=subg_bc[:], scalar1=(1.0 - lam_init0), scalar2=None,
                                                 op0=ALU.mult), r=[subg_bc], w=[subg_bc])
            relb_sb = kb.sb(ph, "relb_sb", [32, 6], F32)
            ohr_sb = kb.sb(ph, "ohr_sb", [32, 384], F32)
            fv_sb = kb.sb(ph, "fv_sb", [6, 384], F32)
            bp = kb.sb(ph, "bp", [128, 6, 2, 128], F32)
            kb.dma(sp, relb_sb[:], relb, relb_sb, w=[relb_sb])
            kb.dma(sp, ohr_sb[:], cst["ohr"], ohr_sb, w=[ohr_sb])
            p0 = kb.ps()
            kb.mm(p0, [(p0[0:6, 0:384], relb_sb[:], ohr_sb[:])], r=[relb_sb, ohr_sb])
            kb.op(dve, lambda h: h.tensor_copy(out=fv_sb[:], in_=p0[0:6, 0:384]), r=[p0], w=[fv_sb])
            kb.dma(sp, BIASV, fv_sb[:], fv_sb, r=[fv_sb])
            kb.barrier()
            for hh in range(6):
                for ti, off in enumerate((1, 129)):
                    src = bass.AP(tensor=BIASV.tensor, offset=hh * 384 + off, ap=[[1, 128], [1, 128]])
                    kb.dma(sp, bp[:, hh, ti, :], src, bp, w=[bp])
            for hh in range(6):
                pp = kb.ps()
                kb.mm(pp, [(pp[:, 0:256], jx[:], bp[:, hh, :, :].rearrange("p a b -> p (a b)"))], r=[jx, bp])
                kb.op(dve, lambda h: h.tensor_copy(out=bias_t[:, hh, :, :].rearrange("p a b -> p (a b)"), in_=pp[:, 0:256]),
                      r=[pp], w=[bias_t])
            for hh in range(6):
                kb.op(dve, lambda h: h.tensor_tensor(out=bias_t[:, hh, 0, :], in0=bias_t[:, hh, 0, :], in1=maskd[:],
                                                     op=ALU.add), r=[bias_t, maskd], w=[bias_t])
            kb.barrier()
            kb.release([lbc, relb_sb, ohr_sb, fv_sb, bp, identf, identb, iota, iotab, trib, tribb, onesb, onesf, jx, maskd, masksb, subg_bc, cfar])

        def load_ln_params(ph, idx):
            g = kb.sb(ph, "ln_g", [128, D], F32)
            b = kb.sb(ph, "ln_b", [128, D], F32)
            kb.dma(sp, g[:], lng[idx:idx + 1, :].partition_broadcast(128), g, w=[g])
            kb.dma(sp, b[:], lnb[idx:idx + 1, :].partition_broadcast(128), b, w=[b])
            return g, b

        def layer_norm(r_buf, out_buf, g, b, stat):
            junk = out_buf
            kb.op(act, lambda h: h.activation(out=junk[:], in_=r_buf[:], func=AF.Copy, accum_out=stat[:, 0:1]),
                  r=[r_buf], w=[junk, stat])
            kb.op(dve, lambda h: h.tensor_scalar(out=stat[:, 1:2], in0=stat[:, 0:1], scalar1=-1.0 / D, scalar2=None,
                                                 op0=ALU.mult), r=[stat], w=[stat])
            kb.op(act, lambda h: h.activation(out=junk[:], in_=r_buf[:], func=AF.Square, bias=stat[:, 1:2], scale=1.0,
                                              accum_out=stat[:, 2:3]), r=[r_buf, stat], w=[junk, stat])
            kb.op(dve, lambda h: h.tensor_scalar(out=stat[:, 3:4], in0=stat[:, 2:3], scalar1=1.0 / D, scalar2=LN_EPS,
                                                 op0=ALU.mult, op1=ALU.add), r=[stat], w=[stat])
            kb.op(act, lambda h: h.activation(out=stat[:, 3:4], in_=stat[:, 3:4], func=AF.Sqrt), r=[stat], w=[stat])
            kb.op(dve, lambda h: h.reciprocal(out=stat[:, 4:5], in_=stat[:, 3:4]), r=[stat], w=[stat])
            kb.op(dve, lambda h: h.tensor_tensor(out=stat[:, 5:6], in0=stat[:, 1:2], in1=stat[:, 4:5], op=ALU.mult),
                  r=[stat], w=[stat])
            kb.op(act, lambda h: h.activation(out=junk[:], in_=r_buf[:], func=AF.Identity, bias=stat[:, 5:6],
                                              scale=stat[:, 4:5]), r=[r_buf, stat], w=[junk])
            kb.op(dve, lambda h: h.tensor_tensor(out=junk[:], in0=junk[:], in1=g[:], op=ALU.mult), r=[junk, g], w=[junk])
            kb.op(dve, lambda h: h.tensor_tensor(out=out_buf[:], in0=junk[:], in1=b[:], op=ALU.add), r=[junk, b],
                  w=[out_buf])

        def transpose_to_T(src_bf, dstT, col0, ncol_chunks, evict_rr):
            for c0 in range(0, ncol_chunks, 4):
                n = min(4, ncol_chunks - c0)
                pp = kb.ps()
                ppb = pp[:, 0:256].bitcast(BF16)
                for c in range(n):
                    kb.transpose(pp, ppb[:, c * 128:(c + 1) * 128], src_bf[:, (c0 + c) * 128:(c0 + c + 1) * 128],
                                 identb[:], r=[src_bf, identb])
                e = act if (evict_rr[0] % 2 == 0) else dve
                evict_rr[0] += 1
                if e is act:
                    kb.op(act, lambda h: h.activation(out=dstT[:, c0:c0 + n, col0:col0 + 128],
                                                      in_=ppb[:, 0:n * 128].rearrange("p (c t) -> p c t", c=n),
                                                      func=AF.Copy), r=[pp], w=[dstT])
                else:
                    kb.op(dve, lambda h: h.tensor_copy(out=dstT[:, c0:c0 + n, col0:col0 + 128],
                                                       in_=ppb[:, 0:n * 128].rearrange("p (c t) -> p c t", c=n)),
                          r=[pp], w=[dstT])

        def project(src_tok, ntiles, w_dram, ncols, fm_outs, tm_outs, kd, tag):
            nblk = ncols // 512
            with ExitStack() as ph:
                rr = [0]
                xts = [kb.sb(ph, f"{tag}_xT{i}", [128, kd, 512], BF16) for i in range(2)]
                xbs = [kb.sb(ph, f"{tag}_xb{i}", [128, kd * 128], BF16) for i in range(2)]
                wbs = [kb.sb(ph, f"{tag}_wb{i}", [128, kd, 512], BF16) for i in range(3)]
                ofm = [kb.sb(ph, f"{tag}_ofm{i}", [128, 512], BF16) for i in range(3)]
                otf = [kb.sb(ph, f"{tag}_otf{i}", [128, 512], F32) for i in range(3)]
                otb = [kb.sb(ph, f"{tag}_otb{i}", [128, 512], BF16) for i in range(3)]
                wv = w_dram.rearrange("(k p) f -> p k f", p=128)
                gi = 0
                xi = 0
                wi = 0
                oi = 0
                for t0 in range(0, ntiles, 4):
                    nt_g = min(4, ntiles - t0)
                    ntk = nt_g * 128
                    xT = xts[gi % 2]
                    gi += 1
                    for tt in range(nt_g):
                        xb = xbs[xi % 2]
                        xi += 1
                        kb.dma(pool, xb[:], src_tok[(t0 + tt) * 128:(t0 + tt + 1) * 128, :], xb, w=[xb], ndesc=128)
                        transpose_to_T(xb, xT, tt * 128, kd, rr)
                    CUT = 0
                    for b in range(nblk if CUT != 1 else 0):
                        wb = wbs[wi % 3]
                        wi += 1
                        kq = max(1, kd // 4)
                        for k0 in range(0, kd, kq):
                            kb.dma(pool, wb[:, k0:k0 + kq, :], wv[:, k0:k0 + kq, b * 512:(b + 1) * 512], wb, w=[wb],
                                   ndesc=128 * kq)
                        for (c0, nch, dstT) in (fm_outs if CUT != 2 else []):
                            for ch in range(nch):
                                col = c0 + ch * 128
                                if col // 512 != b:
                                    continue
                                lo = col - b * 512
                                pp = kb.ps()
                                kb.mm(pp, [(pp[:, 0:ntk], wb[:, k, lo:lo + 128], xT[:, k, 0:ntk]) for k in range(kd)],
                                      r=[wb, xT])
                                ob = ofm[oi % 3]
                                oi += 1
                                e = act if oi % 2 == 0 else dve
                                if e is act:
                                    kb.op(act, lambda h: h.activation(out=ob[:, 0:ntk], in_=pp[:, 0:ntk], func=AF.Copy),
                                          r=[pp], w=[ob])
                                else:
                                    kb.op(dve, lambda h: h.tensor_copy(out=ob[:, 0:ntk], in_=pp[:, 0:ntk]), r=[pp], w=[ob])
                                kb.dma(sp, dstT[ch * 128:(ch + 1) * 128, t0 * 128:t0 * 128 + ntk], ob[:, 0:ntk], ob, r=[ob],
                                       ndesc=128)
                        for (c0, ncl, dst32, dst16) in (tm_outs if CUT not in (2, 3) else []):
                            if not (c0 <= b * 512 < c0 + ncl):
                                continue
                            dc = b * 512 - c0
                            for tt in range(nt_g):
                                pp = kb.ps()
                                kb.mm(pp, [(pp[:, :], xT[:, k, tt * 128:(tt + 1) * 128], wb[:, k, :]) for k in range(kd)],
                                      r=[wb, xT])
                                rows = slice((t0 + tt) * 128, (t0 + tt + 1) * 128)
                                if dst32 is not None:
                                    o32 = otf[oi % 3]
                                    kb.op(act, lambda h: h.activation(out=o32[:], in_=pp[:, :], func=AF.Copy), r=[pp],
                                          w=[o32])
                                    kb.dma(sp, dst32[rows, dc:dc + 512], o32[:], o32, r=[o32], ndesc=128)
                                if dst16 is not None:
                                    o16 = otb[oi % 3]
                                    kb.op(dve, lambda h: h.tensor_copy(out=o16[:], in_=pp[:, :]), r=[pp], w=[o16])
                                    kb.dma(sp, dst16[rows, dc:dc + 512], o16[:], o16, r=[o16], ndesc=128)
                                oi += 1
                kb.barrier()
                kb.release(xts + xbs + wbs + ofm + otf + otb)

        def cache_transposes(l):
            with ExitStack() as ph:
                rr = [0]
                kin = [kb.sb(ph, f"ckin{i}", [128, SW], BF16) for i in range(2)]
                kout = [kb.sb(ph, f"ckout{i}", [128, 12, 128], BF16) for i in range(2)]
                i = 0
                for s in range(NSMP):
                    for blk in range(PAST // 128):
                        a = kin[i % 2]
                        o = kout[i % 2]
                        i += 1
                        kb.dma(pool, a[:], ck[l, s, blk * 128:(blk + 1) * 128, :], a, w=[a], ndesc=128)
                        transpose_to_T(a, o, 0, 12, rr)
                        kb.dma(sp, KTS[s, :, blk * 128:(blk + 1) * 128].rearrange("(c p) t -> p c t", p=128), o[:], o,
                               r=[o], ndesc=128 * 12)
                    for blk in range(NMEM // 128):
                        a = kin[i % 2]
                        o = kout[i % 2]
                        i += 1
                        kb.dma(pool, a[:, 0:512], cmk[l, s, blk * 128:(blk + 1) * 128, :], a, w=[a], ndesc=128)
                        transpose_to_T(a, o, 0, 4, rr)
                        kb.dma(sp, MKTS[s, :, blk * 128:(blk + 1) * 128].rearrange("(c p) t -> p c t", p=128),
                               o[:, 0:4, :], o, r=[o], ndesc=128 * 4)
                kb.barrier()
                kb.release(kin + kout)

        def attention(l):
            kind = l % 2
            scale = 128 ** -0.5
            with ExitStack() as ph:
                rr = [0]
                NKMAX = max(SEQ, PAST + 128) // 128
                qTs = [kb.sb(ph, f"a_qT{i}", [128, 2, SEQ], BF16) for i in range(2)]
                kTs = [kb.sb(ph, f"a_kT{i}", [128, 2, NKMAX * 128], BF16) for i in range(2)]
                vss = [kb.sb(ph, f"a_v{i}", [128, NKMAX, 260], BF16) for i in range(2)]
                for vsb in vss:
                    kb.op(dve, lambda h: h.memset(vsb[:], 1.0), w=[vsb])
                NR = 5
                tf = [kb.sb(ph, f"a_tf{i}", [128, 512], F32) for i in range(NR)]
                tg = [kb.sb(ph, f"a_tg{i}", [128, 512], F32) for i in range(NR)]
                tx = [kb.sb(ph, f"a_tx{i}", [128, 256], F32) for i in range(NR)]
                pb = [kb.sb(ph, f"a_pb{i}", [128, 512], BF16) for i in range(NR)]
                raccs = [kb.sb(ph, f"a_racc{i}", [128, 256], F32) for i in range(4)]
                raccb = [kb.sb(ph, f"a_raccb{i}", [128, 256], BF16) for i in range(4)]
                tgb = [kb.sb(ph, f"a_tgb{i}", [128, 256], BF16) for i in range(NR)]
                osb = [kb.sb(ph, f"a_o{i}", [128, 256], F32) for i in range(2)]
                osq = kb.sb(ph, "a_osq", [128, 256], F32)
                obf = [kb.sb(ph, f"a_ob{i}", [128, 256], BF16) for i in range(2)]
                oT = [kb.sb(ph, f"a_oT{i}", [128, 2, 128], BF16) for i in range(2)]
                sm = [kb.sb(ph, f"a_sm{i}", [128, 8], F32) for i in range(2)]
                cnt = {"h": 0, "w": 0, "o": 0}

                def run_head(ncomp, dv, nq_tot, qchunk, kss, cls, mode, bias_h, mix_row0, tok0, loads):
                    hi = cnt["h"]
                    cnt["h"] += 1
                    qT, kT, vs = qTs[hi % 2], kTs[hi % 2], vss[hi % 2]

                    def do_loads():
                        for (cast, which, dstf, src, nd) in loads:
                            buf = {"q": qT, "k": kT, "v": vs}[which]
                            kb.dma(pool if cast else sp, dstf(buf), src, buf, w=[buf], ndesc=nd)

                    def do_compute():
                        nkb = len(kss)
                        for q0 in range(0, nq_tot, qchunk):
                            nq = min(qchunk, nq_tot - q0)
                            nsub = (nq + 127) // 128
                            subs = [(q0 // 128 + i, min(128, nq - i * 128)) for i in range(nsub)]
                            accs = {}
                            for c in range(ncomp):
                                for si in range(nsub):
                                    accs[(c, si)] = kb.ps(reserve=True)
                            order = list(range(nkb))
                            if mode == "sb":
                                order = order[::-1]
                            started = set()
                            live = [j for j in order if any(cls(j, qb) != "skip" for qb, _ in subs)]
                            lastj = {}
                            for si, (qb, _) in enumerate(subs):
                                for j in live:
                                    if cls(j, qb) != "skip":
                                        lastj[si] = j
                            npairs = len(live)
                            st_ = {}

                            def stageA(j):
                                ks = kss[j]
                                types = [cls(j, qb) for qb, _ in subs]
                                sp_ps = kb.ps()
                                kb.mm_multi([sp_ps], [[(sp_ps[0:ks, c * 256:c * 256 + nq], kT[:, c, j * 128:j * 128 + ks],
                                                       qT[:, c, q0:q0 + nq])] for c in range(ncomp)], r=[kT, qT])
                                wi = cnt["w"]
                                cnt["w"] += 1
                                P = pb[wi % NR]
                                T = tf[wi % NR]
                                G = tg[wi % NR]
                                st_[j] = dict(ks=ks, types=types, P=P, T=T, G=G, wi=wi)
                                if mode in ("diff", "mem"):
                                    if all(t == "far" for t in types):
                                        bias_ap = cfar[0:ks, bias_h:bias_h + 1] if mode == "diff" else zero_c[0:ks, :]
                                        kb.op(act, lambda h: h.activation(
                                            out=P[0:ks, :].rearrange("p (c q) -> p c q", c=2)[:, 0:ncomp, 0:nq],
                                            in_=sp_ps[0:ks, :].rearrange("p (c q) -> p c q", c=2)[:, 0:ncomp, 0:nq],
                                            func=AF.Exp, bias=bias_ap, scale=scale), r=[sp_ps, cfar, zero_c], w=[P])
                                    else:
                                        for si, (qb, nqs) in enumerate(subs):
                                            t = types[si]
                                            if t == "skip":
                                                continue
                                            for c in range(ncomp):
                                                src = sp_ps[0:ks, c * 256 + si * 128:c * 256 + si * 128 + nqs]
                                                dst = P[0:ks, c * 256 + si * 128:c * 256 + si * 128 + nqs]
                                                if t == "far":
                                                    kb.op(act, lambda h: h.activation(out=dst, in_=src, func=AF.Exp,
                                                                                      bias=cfar[0:ks, bias_h:bias_h + 1],
                                                                                      scale=scale), r=[sp_ps, cfar], w=[P])
                                                else:
                                                    bt = bias_t[0:ks, bias_h, 0 if t == "diag" else 1, 0:nqs]
                                                    tmp = T[0:ks, c * 256 + si * 128:c * 256 + si * 128 + nqs]
                                                    kb.op(dve, lambda h: h.scalar_tensor_tensor(out=tmp, in0=src, scalar=scale,
                                                                                                in1=bt, op0=ALU.mult,
                                                                                                op1=ALU.add),
                                                          r=[sp_ps, bias_t], w=[T])
                                                    kb.op(act, lambda h: h.activation(out=dst, in_=tmp, func=AF.Exp), r=[T],
                                                          w=[P])
                                else:
                                    E = T
                                    SPt = tgb[wi % NR]
                                    st_[j]["G"] = SPt
                                    kb.op(act, lambda h: h.activation(out=E[0:ks, 0:nq], in_=sp_ps[0:ks, 0:nq], func=AF.Exp,
                                                                      scale=scale), r=[sp_ps], w=[E])
                                    kb.op(act, lambda h: h.activation(out=SPt[0:ks, 0:nq], in_=E[0:ks, 0:nq], func=AF.Ln,
                                                                      bias=1.0, scale=1.0), r=[E], w=[SPt])
                                    for si, (qb, nqs) in enumerate(subs):
                                        t = types[si]
                                        sl = slice(si * 128, si * 128 + nqs)
                                        if t == "diag":
                                            kb.op(dve, lambda h: h.tensor_tensor(out=SPt[0:ks, sl], in0=SPt[0:ks, sl],
                                                                                 in1=masksb[0:ks, 0:nqs], op=ALU.mult),
                                                  r=[SPt, masksb], w=[SPt])
                                            kb.op(dve, lambda h: h.tensor_tensor(out=E[0:ks, sl], in0=E[0:ks, sl],
                                                                                 in1=masksb[0:ks, 0:nqs], op=ALU.mult),
                                                  r=[E, masksb], w=[E])
                                        elif t == "skip":
                                            kb.op(dve, lambda h: h.memset(SPt[0:ks, sl], 0.0), w=[SPt])
                                            kb.op(dve, lambda h: h.memset(E[0:ks, sl], 0.0), w=[E])

                            rstate = {"prev": None, "n": 0}

                            def stageB(j):
                                if mode != "sb":
                                    return
                                d_ = st_[j]
                                ks, P, E, SPt, wi = d_["ks"], d_["P"], d_["T"], d_["G"], d_["wi"]
                                cp = kb.ps()
                                mmsl = [(cp[0:ks, 0:nq], tribb[0:ks, 0:ks], SPt[0:ks, 0:nq])]
                                rprev = rstate["prev"]
                                rd = [tribb, onesb, SPt]
                                if rprev is not None:
                                    mmsl.append((cp[0:ks, 0:nq], onesb[:, 0:ks], rprev[1][:, 0:nq]))
                                    rd.append(rprev[1])
                                kb.mm(cp, mmsl, r=rd)
                                X = tx[wi % NR]
                                kb.op(act, lambda h: h.activation(out=X[0:ks, 0:nq], in_=cp[0:ks, 0:nq], func=AF.Exp,
                                                                  scale=-1.0), r=[cp], w=[X])
                                kb.op(dve, lambda h: h.tensor_tensor(out=P[0:ks, 0:nq], in0=E[0:ks, 0:nq], in1=X[0:ks, 0:nq],
                                                                     op=ALU.mult), r=[E, X], w=[P])
                                rn = raccs[rstate["n"] % len(raccs)]
                                rnb = raccb[rstate["n"] % len(raccb)]
                                rstate["n"] += 1
                                if rprev is None:
                                    if ks < 128:
                                        kb.op(dve, lambda h: h.memset(rn[:], 0.0), w=[rn])
                                    kb.op(dve, lambda h: h.tensor_copy(out=rn[0:ks, 0:nq], in_=SPt[0:ks, 0:nq]), r=[SPt], w=[rn])
                                else:
                                    kb.op(dve, lambda h: h.tensor_tensor(out=rn[:, 0:nq], in0=rprev[0][:, 0:nq], in1=SPt[:, 0:nq],
                                                                         op=ALU.add), r=[rprev[0], SPt], w=[rn])
                                kb.op(pool, lambda h: h.tensor_copy(out=rnb[:, 0:nq], in_=rn[:, 0:nq]), r=[rn], w=[rnb])
                                rstate["prev"] = (rn, rnb)

                            def stageC(j):
                                d_ = st_[j]
                                ks, types, P = d_["ks"], d_["types"], d_["P"]
                                ncol = dv + (0 if mode == "sb" else 1)
                                for si, (qb, nqs) in enumerate(subs):
                                    if types[si] == "skip":
                                        continue
                                    for c in range(ncomp):
                                        a = accs[(c, si)]
                                        first = (c, si) not in started
                                        started.add((c, si))
                                        last = (lastj[si] == j)
                                        kb._wait(pe, kb._deps([P, vs], [a] if first else []))
                                        ins = pe.h.matmul(a[0:nqs, 0:ncol], P[0:ks, c * 256 + si * 128:c * 256 + si * 128 + nqs],
                                                          vs[0:ks, j, 256 - dv:256 - dv + ncol], start=first, stop=last)
                                        pe.sem.count += 1
                                        ins.then_inc(pe.sem.h, 1)
                                        ev = (pe.sem, pe.sem.count)
                                        kb._record(ev, [P, vs], [a])

                            for idx in range(npairs + 2):
                                if idx < npairs:
                                    stageA(live[idx])
                                if 0 <= idx - 1 < npairs:
                                    stageB(live[idx - 1])
                                if 0 <= idx - 2 < npairs:
                                    stageC(live[idx - 2])
                            for si, (qb, nqs) in enumerate(subs):
                                oi = cnt["o"]
                                cnt["o"] += 1
                                o = osb[oi % 2]
                                s_ = sm[oi % 2]
                                ob = obf[oi % 2]
                                ot = oT[oi % 2]
                                if mode == "sb":
                                    kb.op(act, lambda h: h.activation(out=ob[0:nqs, 0:dv], in_=accs[(0, si)][0:nqs, 0:dv],
                                                                      func=AF.Copy), r=[accs[(0, si)]], w=[ob])
                                elif mode == "mem":
                                    a0 = accs[(0, si)]
                                    kb.op(dve, lambda h: h.reciprocal(out=s_[0:nqs, 0:1], in_=a0[0:nqs, dv:dv + 1]), r=[a0],
                                          w=[s_])
                                    kb.op(act, lambda h: h.activation(out=ob[0:nqs, 0:dv], in_=a0[0:nqs, 0:dv], func=AF.Copy,
                                                                      scale=s_[0:nqs, 0:1]), r=[a0, s_], w=[ob])
                                else:
                                    a0, a1 = accs[(0, si)], accs[(1, si)]
                                    kb.op(dve, lambda h: h.reciprocal(out=s_[0:nqs, 0:1], in_=a0[0:nqs, dv:dv + 1]), r=[a0],
                                          w=[s_])
                                    kb.op(dve, lambda h: h.reciprocal(out=s_[0:nqs, 1:2], in_=a1[0:nqs, dv:dv + 1]), r=[a1],
                                          w=[s_])
                                    kb.op(dve, lambda h: h.tensor_tensor(out=s_[0:nqs, 1:2], in0=s_[0:nqs, 1:2],
                                                                         in1=lam_t[0:nqs, 1:2], op=ALU.mult), r=[s_, lam_t],
                                          w=[s_])
                                    kb.op(act, lambda h: h.activation(out=o[0:nqs, 0:dv], in_=a0[0:nqs, 0:dv], func=AF.Copy,
                                                                      scale=s_[0:nqs, 0:1]), r=[a0, s_], w=[o])
                                    kb.op(dve, lambda h: h.scalar_tensor_tensor(out=o[0:nqs, 0:dv], in0=a1[0:nqs, 0:dv],
                                                                                scalar=s_[0:nqs, 1:2], in1=o[0:nqs, 0:dv],
                                                                                op0=ALU.mult, op1=ALU.add), r=[a1, s_, o], w=[o])
                                    kb.op(act, lambda h: h.activation(out=osq[0:nqs, 0:dv], in_=o[0:nqs, 0:dv], func=AF.Square,
                                                                      accum_out=s_[0:nqs, 2:3]), r=[o], w=[osq, s_])
                                    kb.op(dve, lambda h: h.tensor_scalar(out=s_[0:nqs, 3:4], in0=s_[0:nqs, 2:3],
                                                                         scalar1=1.0 / dv, scalar2=LN_EPS, op0=ALU.mult,
                                                                         op1=ALU.add), r=[s_], w=[s_])
                                    kb.op(act, lambda h: h.activation(out=s_[0:nqs, 3:4], in_=s_[0:nqs, 3:4], func=AF.Sqrt),
                                          r=[s_], w=[s_])
                                    kb.op(dve, lambda h: h.reciprocal(out=s_[0:nqs, 4:5], in_=s_[0:nqs, 3:4]), r=[s_], w=[s_])
                                    kb.op(dve, lambda h: h.scalar_tensor_tensor(out=ob[0:nqs, 0:dv], in0=o[0:nqs, 0:dv],
                                                                                scalar=s_[0:nqs, 4:5], in1=subg_bc[0:nqs, 0:dv],
                                                                                op0=ALU.mult, op1=ALU.mult),
                                          r=[o, s_, subg_bc], w=[ob])
                                nch = dv // 128
                                pp = kb.ps()
                                ppb = pp[:, 0:256].bitcast(BF16)
                                for c in range(nch):
                                    kb.transpose(pp, ppb[:, c * 128:c * 128 + nqs], ob[0:nqs, c * 128:(c + 1) * 128],
                                                 identb[0:nqs, 0:nqs], r=[ob, identb])
                                kb.op(dve, lambda h: h.tensor_copy(out=ot[:, 0:nch, 0:nqs],
                                                                   in_=ppb[:, 0:nch * 128].rearrange("p (c t) -> p c t", c=nch)[:, :, 0:nqs]),
                                      r=[pp], w=[ot])
                                tq = tok0 + qb * 128 if nq_tot > 128 else tok0
                                kb.dma(sp, MIXT[mix_row0:mix_row0 + dv, tq:tq + nqs].rearrange("(c p) t -> p c t", p=128),
                                       ot[:, 0:nch, 0:nqs], ot, r=[ot], ndesc=128 * nch)
                            for a_ in accs.values():
                                kb.unreserve(a_)

                    return do_loads, do_compute

                jobs = []
                nblk = SEQ // 128
                npb = PAST // 128
                tokm = lambda ap: ap.rearrange("(j p) d -> p j d", p=128)

                def cls_diff_p(j, qb):
                    return "skip" if j > qb else ("diag" if j == qb else ("prev" if j == qb - 1 else "far"))

                def cls_sb_p(j, qb):
                    return "skip" if j > qb else ("diag" if j == qb else "far")

                def cls_diff_s(j, qb):
                    return "diag" if j == npb else ("prev" if j == npb - 1 else "far")

                def cls_sb_s(j, qb):
                    return "diag" if j == npb else "far"

                cls_far = lambda j, qb: "far"
                for s in range(NSEQ):
                    tok0 = s * SEQ
                    ts_ = slice(tok0, tok0 + SEQ)
                    if kind == 0:
                        for hh in range(6):
                            lds = []
                            for c in range(2):
                                r0 = hh * 256 + c * 128
                                lds.append((False, "q", lambda b, c=c: b[:, c, 0:SEQ], QT[r0:r0 + 128, ts_], 128))
                                lds.append((False, "k", lambda b, c=c: b[:, c, 0:SEQ], KT[r0:r0 + 128, ts_], 128))
                            lds.append((False, "v", lambda b: b[:, 0:nblk, 0:256], tokm(VB[ts_, hh * 256:(hh + 1) * 256]),
                                        128 * nblk))
                            jobs.append(run_head(2, 256, SEQ, 256, [128] * nblk, cls_diff_p, "diff", hh, hh * 256, tok0, lds))
                    else:
                        for hh in range(12):
                            r0 = hh * 128
                            lds = [(False, "q", lambda b: b[:, 0, 0:SEQ], QT[r0:r0 + 128, ts_], 128),
                                   (False, "k", lambda b: b[:, 0, 0:SEQ], KT[r0:r0 + 128, ts_], 128),
                                   (False, "v", lambda b: b[:, 0:nblk, 128:256], tokm(VB[ts_, r0:r0 + 128]), 128 * nblk)]
                            jobs.append(run_head(1, 128, SEQ, 256, [128] * nblk, cls_sb_p, "sb", 0, r0, tok0, lds))
                    for hh in range(4):
                        r0 = hh * 128
                        ms_ = slice(s * NMEM, (s + 1) * NMEM)
                        lds = [(False, "q", lambda b: b[:, 0, 0:SEQ], MQT[r0:r0 + 128, ts_], 128),
                               (False, "k", lambda b: b[:, 0, 0:NMEM], MKT[r0:r0 + 128, ms_], 128),
                               (False, "v", lambda b: b[:, 0:NMEM // 128, 128:256], tokm(MVB[ms_, r0:r0 + 128]), 256)]
                        jobs.append(run_head(1, 128, SEQ, 256, [128] * (NMEM // 128), cls_far, "mem", 0, SW + r0, tok0, lds))
                for s in range(NSMP):
                    tok0 = cfg.NP + s * DEC_SEQ
                    ts_ = slice(tok0, tok0 + DEC_SEQ)
                    kss_s = [128] * npb + [DEC_SEQ]
                    if kind == 0:
                        for hh in range(6):
                            lds = []
                            for c in range(2):
                                r0 = hh * 256 + c * 128
                                lds.append((False, "q", lambda b, c=c: b[:, c, 0:DEC_SEQ], QT[r0:r0 + 128, ts_], 128))
                                lds.append((False, "k", lambda b, c=c: b[:, c, 0:PAST], KTS[s, r0:r0 + 128, :], 128))
                                lds.append((False, "k", lambda b, c=c: b[:, c, PAST:PAST + DEC_SEQ], KT[r0:r0 + 128, ts_], 128))
                            lds.append((True, "v", lambda b: b[:, 0:npb, 0:256], tokm(cv[l, s, :, hh * 256:(hh + 1) * 256]),
                                        128 * npb))
                            lds.append((False, "v", lambda b: b[0:DEC_SEQ, npb, 0:256], VB[ts_, hh * 256:(hh + 1) * 256], 32))
                            jobs.append(run_head(2, 256, DEC_SEQ, 256, kss_s, cls_diff_s, "diff", hh, hh * 256, tok0, lds))
                    else:
                        for hh in range(12):
                            r0 = hh * 128
                            lds = [(False, "q", lambda b: b[:, 0, 0:DEC_SEQ], QT[r0:r0 + 128, ts_], 128),
                                   (False, "k", lambda b: b[:, 0, 0:PAST], KTS[s, r0:r0 + 128, :], 128),
                                   (False, "k", lambda b: b[:, 0, PAST:PAST + DEC_SEQ], KT[r0:r0 + 128, ts_], 128),
                                   (True, "v", lambda b: b[:, 0:npb, 128:256], tokm(cv[l, s, :, r0:r0 + 128]), 128 * npb),
                                   (False, "v", lambda b: b[0:DEC_SEQ, npb, 128:256], VB[ts_, r0:r0 + 128], 32)]
                            jobs.append(run_head(1, 128, DEC_SEQ, 256, kss_s, cls_sb_s, "sb", 0, r0, tok0, lds))
                    for hh in range(4):
                        r0 = hh * 128
                        lds = [(False, "q", lambda b: b[:, 0, 0:DEC_SEQ], MQT[r0:r0 + 128, ts_], 128),
                               (False, "k", lambda b: b[:, 0, 0:NMEM], MKTS[s, r0:r0 + 128, :], 128),
                               (True, "v", lambda b: b[:, 0:NMEM // 128, 128:256], tokm(cmv[l, s, :, r0:r0 + 128]), 256)]
                        jobs.append(run_head(1, 128, DEC_SEQ, 256, [128] * (NMEM // 128), cls_far, "mem", 0, SW + r0, tok0, lds))
                jobs[0][0]()
                for ji, (ld_, cp_) in enumerate(jobs):
                    if ji + 1 < len(jobs):
                        jobs[ji + 1][0]()
                    cp_()
                kb.barrier()
                kb.release(qTs + kTs + vss + tf + tg + tx + pb + osb + obf + oT + sm + raccs + raccb + tgb + [osq])

        def out_proj(l, src_tok):
            with ExitStack() as ph:
                rr = [0]
                wo = kb.sb(ph, "wo", [128, 16, D], BF16)
                wov = w_o[l].rearrange("(k p) f -> p k f", p=128)
                for k0 in range(0, 16, 2):
                    kb.dma(pool, wo[:, k0:k0 + 2, :], wov[:, k0:k0 + 2, :], wo, w=[wo], ndesc=256)
                g, b = load_ln_params(ph, l * 2 + 0)
                mts = [kb.sb(ph, f"c_mT{i}", [128, 16, 128], BF16) for i in range(2)]
                xs = [kb.sb(ph, f"c_x{i}", [128, D], F32) for i in range(2)]
                rs = [kb.sb(ph, f"c_r{i}", [128, D], F32) for i in range(2)]
                hs = [kb.sb(ph, f"c_h{i}", [128, D], F32) for i in range(2)]
                hb = [kb.sb(ph, f"c_hb{i}", [128, D], BF16) for i in range(2)]
                hT = [kb.sb(ph, f"c_hT{i}", [128, KD, 128], BF16) for i in range(2)]
                stat = [kb.sb(ph, f"c_st{i}", [128, 8], F32) for i in range(2)]
                for t in range(NT):
                    mT, x, r, hsb, hbb, hTt, stt = mts[t % 2], xs[t % 2], rs[t % 2], hs[t % 2], hb[t % 2], hT[t % 2], stat[t % 2]
                    rows = slice(t * 128, (t + 1) * 128)
                    kb.dma(sp, mT[:], MIXT[:, rows].rearrange("(c p) t -> p c t", p=128), mT, w=[mT], ndesc=2048)
                    kb.dma(sp, x[:], src_tok[rows, :], x, w=[x], ndesc=128)
                    for f0 in range(0, D, 512):
                        pp = kb.ps()
                        kb.mm(pp, [(pp[:, :], mT[:, k, :], wo[:, k, f0:f0 + 512]) for k in range(16)], r=[mT, wo])
                        kb.op(dve, lambda h: h.scalar_tensor_tensor(out=r[:, f0:f0 + 512], in0=x[:, f0:f0 + 512],
                                                                    scalar=cfg.alpha, in1=pp[:, :], op0=ALU.mult,
                                                                    op1=ALU.add), r=[x, pp], w=[r])
                    layer_norm(r, hsb, g, b, stt)
                    kb.dma(pool, H1[rows, :], hsb[:], hsb, r=[hsb], ndesc=128)
                    kb.op(act, lambda h: h.activation(out=hbb[:], in_=hsb[:], func=AF.Copy), r=[hsb], w=[hbb])
                    transpose_to_T(hbb, hTt, 0, KD, rr)
                    kb.dma(pool, H1T[:, rows].rearrange("(c p) t -> p c t", p=128), hTt[:], hTt, r=[hTt], ndesc=128 * KD)
                kb.barrier()
                kb.release([wo, g, b] + mts + xs + rs + hs + hb + hT + stat)

        def peer_retrieve(l, IDX1T, IDX2T, GT):
            with ExitStack() as ph:
                wq = kb.sb(ph, "wq", [128, KD, 2048], BF16)
                wqv = w_q[l].rearrange("(k p) f -> p k f", p=128)
                for k0 in range(0, KD, 2):
                    kb.dma(pool, wq[:, k0:k0 + 2, :], wqv[:, k0:k0 + 2, :], wq, w=[wq], ndesc=256)
                sk = kb.sb(ph, "sk", [128, 16, 128], BF16)
                kb.dma(pool, sk[:], subk[l].rearrange("g d n -> d g n"), sk, w=[sk], ndesc=2048)
                hTs = [kb.sb(ph, f"d_hT{i}", [128, KD, 128], BF16) for i in range(2)]
                qp = [kb.sb(ph, f"d_qp{i}", [128, 16, 128], BF16) for i in range(2)]
                sc = [kb.sb(ph, f"d_sc{i}", [128, 2048], F32) for i in range(2)]
                sc2 = kb.sb(ph, "d_sc2", [128, 2048], F32)
                sv = kb.sb(ph, "d_sv", [128, 16, 16], F32)
                si_u = kb.sb(ph, "d_siu", [128, 16, 16], U32)
                si_f = kb.sb(ph, "d_sif", [128, 16, 16], F32)
                cand = kb.sb(ph, "d_cand", [128, 8, 256], F32)
                cand2 = kb.sb(ph, "d_cand2", [128, 8, 256], F32)
                best = kb.sb(ph, "d_best", [128, 8, 16], F32)
                pos_u = kb.sb(ph, "d_posu", [128, 8, 16], U32)
                pa_u = kb.sb(ph, "d_pau", [128, 8, 16], U32)
                pb_u = kb.sb(ph, "d_pbu", [128, 8, 16], U32)
                pa_f = kb.sb(ph, "d_paf", [128, 8, 16], F32)
                pb_f = kb.sb(ph, "d_pbf", [128, 8, 16], F32)
                oh = kb.sb(ph, "d_oh", [128, 8, 16, 16], F32)
                outs = [kb.sb(ph, f"d_out{i}", [128, 3, 128], F32) for i in range(2)]
                gs = kb.sb(ph, "d_gs", [128, 8], F32)
                v_sv = [Buf(None, f"vsv{i}") for i in range(16)]
                v_si = [Buf(None, f"vsi{i}") for i in range(16)]
                v_sc2 = [Buf(None, f"vsc2{i}") for i in range(16)]
                v_best = [Buf(None, f"vb{i}") for i in range(8)]
                v_pos = [Buf(None, f"vp{i}") for i in range(8)]
                v_c2 = [Buf(None, f"vc2{i}") for i in range(8)]
                for t in range(NT):
                    hTt, qpt, sct, ot = hTs[t % 2], qp[t % 2], sc[t % 2], outs[t % 2]
                    rows = slice(t * 128, (t + 1) * 128)
                    kb.dma(sp, hTt[:], H1T[:, rows].rearrange("(c p) t -> p c t", p=128), hTt, w=[hTt], ndesc=128 * KD)
                    for g0 in range(0, 16, 4):
                        pp = kb.ps()
                        kb.mm_multi([pp], [[(pp[:, gg * 128:(gg + 1) * 128], wq[:, k, (g0 + gg) * 128:(g0 + gg + 1) * 128],
                                             hTt[:, k, :]) for k in range(KD)] for gg in range(4)], r=[wq, hTt])
                        kb.op(act, lambda h: h.activation(out=qpt[:, g0:g0 + 4, :],
                                                          in_=pp[:, :].rearrange("p (g t) -> p g t", g=4), func=AF.Copy),
                              r=[pp], w=[qpt])
                    for g0 in range(0, 16, 4):
                        pp = kb.ps()
                        kb.mm_multi([pp], [[(pp[:, gg * 128:(gg + 1) * 128], qpt[:, g0 + gg, :], sk[:, g0 + gg, :])]
                                           for gg in range(4)], r=[qpt, sk])
                        kb.op(act, lambda h: h.activation(out=sct[:, g0 * 128:(g0 + 4) * 128], in_=pp[:, :], func=AF.Copy),
                              r=[pp], w=[sct])
                    for stg in range(5):
                        for gI in range(16):
                            seg = sct[:, gI * 128:(gI + 1) * 128]
                            seg2 = sc2[:, gI * 128:(gI + 1) * 128]
                            vsv, vsi, vs2 = v_sv[gI], v_si[gI], v_sc2[gI]
                            if stg == 0:
                                kb.op(dve, lambda h: h.max(out=sv[:, gI, 0:8], in_=seg), r=[sct], w=[vsv])
                            elif stg == 1:
                                kb.op(dve, lambda h: h.max_index(out=si_u[:, gI, 0:8], in_max=sv[:, gI, 0:8], in_values=seg),
                                      r=[sct, vsv], w=[vsi])
                            elif stg == 2:
                                kb.op(dve, lambda h: h.match_replace(out=seg2, in_to_replace=sv[:, gI, 0:8], in_values=seg,
                                                                     imm_value=-1e30), r=[sct, vsv], w=[vs2])
                            elif stg == 3:
                                kb.op(dve, lambda h: h.max(out=sv[:, gI, 8:16], in_=seg2), r=[vs2], w=[vsv])
                            else:
                                kb.op(dve, lambda h: h.max_index(out=si_u[:, gI, 8:16], in_max=sv[:, gI, 8:16],
                                                                 in_values=seg2), r=[vs2, vsv], w=[vsi])
                    kb.op(dve, lambda h: h.tensor_copy(out=si_f[:], in_=si_u[:]), r=v_si, w=[si_f])
                    sv4 = sv[:].rearrange("p (h c) k -> p h c k", c=2)
                    si4 = si_f[:].rearrange("p (h c) k -> p h c k", c=2)
                    kb.op(dve, lambda h: h.tensor_tensor(out=cand[:].rearrange("p h (a b) -> p h a b", a=16),
                                                         in0=sv4[:, :, 0, :].unsqueeze(3).to_broadcast([128, 8, 16, 16]),
                                                         in1=sv4[:, :, 1, :].unsqueeze(2).to_broadcast([128, 8, 16, 16]),
                                                         op=ALU.add), r=v_sv, w=[cand])
                    for stg in range(5):
                        for hh in range(8):
                            vb, vp, vc2 = v_best[hh], v_pos[hh], v_c2[hh]
                            if stg == 0:
                                kb.op(dve, lambda h: h.max(out=best[:, hh, 0:8], in_=cand[:, hh, :]), r=[cand], w=[vb])
                            elif stg == 1:
                                kb.op(dve, lambda h: h.max_index(out=pos_u[:, hh, 0:8], in_max=best[:, hh, 0:8],
                                                                 in_values=cand[:, hh, :]), r=[cand, vb], w=[vp])
                            elif stg == 2:
                                kb.op(dve, lambda h: h.match_replace(out=cand2[:, hh, :], in_to_replace=best[:, hh, 0:8],
                                                                     in_values=cand[:, hh, :], imm_value=-1e30),
                                      r=[cand, vb], w=[vc2])
                            elif stg == 3:
                                kb.op(dve, lambda h: h.max(out=best[:, hh, 8:16], in_=cand2[:, hh, :]), r=[vc2], w=[vb])
                            else:
                                kb.op(dve, lambda h: h.max_index(out=pos_u[:, hh, 8:16], in_max=best[:, hh, 8:16],
                                                                 in_values=cand2[:, hh, :]), r=[vc2, vb], w=[vp])
                    kb.op(dve, lambda h: h.tensor_single_scalar(out=pa_u[:], in_=pos_u[:], scalar=4,
                                                                op=ALU.logical_shift_right), r=v_pos, w=[pa_u])
                    kb.op(dve, lambda h: h.tensor_single_scalar(out=pb_u[:], in_=pos_u[:], scalar=15, op=ALU.bitwise_and),
                          r=v_pos, w=[pb_u])
                    kb.op(dve, lambda h: h.tensor_copy(out=pa_f[:], in_=pa_u[:]), r=[pa_u], w=[pa_f])
                    kb.op(dve, lambda h: h.tensor_copy(out=pb_f[:], in_=pb_u[:]), r=[pb_u], w=[pb_f])
                    for which, pf in ((0, pa_f), (1, pb_f)):
                        kb.op(dve, lambda h: h.tensor_tensor(out=oh[:], in0=pf[:].unsqueeze(3).to_broadcast([128, 8, 16, 16]),
                                                             in1=iota[:, 0:16].unsqueeze(1).unsqueeze(1).to_broadcast([128, 8, 16, 16]),
                                                             op=ALU.is_equal), r=[pf, iota], w=[oh])
                        kb.op(dve, lambda h: h.tensor_tensor(out=oh[:], in0=oh[:],
                                                             in1=si4[:, :, which, :].unsqueeze(2).to_broadcast([128, 8, 16, 16]),
                                                             op=ALU.mult), r=[oh, si_f], w=[oh])
                        kb.op(dve, lambda h: h.tensor_reduce(out=ot[:, which, :].rearrange("p (h k) -> p h k", h=8),
                                                             in_=oh[:], axis=AX.X, op=ALU.add), r=[oh], w=[ot])
                    gv = ot[:, 2, :].rearrange("p (h k) -> p h k", h=8)
                    kb.op(dve, lambda h: h.tensor_tensor(out=gv, in0=best[:], in1=best[:, :, 0:1].to_broadcast([128, 8, 16]),
                                                         op=ALU.subtract), r=v_best, w=[ot])
                    kb.op(act, lambda h: h.activation(out=gv, in_=gv, func=AF.Exp), r=[ot], w=[ot])
                    kb.op(dve, lambda h: h.tensor_reduce(out=gs[:], in_=gv, axis=AX.X, op=ALU.add), r=[ot], w=[gs])
                    kb.op(dve, lambda h: h.reciprocal(out=gs[:], in_=gs[:]), r=[gs], w=[gs])
                    kb.op(dve, lambda h: h.tensor_tensor(out=gv, in0=gv, in1=gs[:].unsqueeze(2).to_broadcast([128, 8, 16]),
                                                         op=ALU.mult), r=[ot, gs], w=[ot])
                    pp = kb.ps()
                    for w3 in range(3):
                        kb.transpose(pp, pp[:, w3 * 128:(w3 + 1) * 128], ot[:, w3, :], identf[:], r=[ot, identf])
                    for w3, dstb in enumerate((IDX1T, IDX2T, GT)):
                        kb.op(act, lambda h: h.activation(out=dstb[:, rows], in_=pp[:, w3 * 128:(w3 + 1) * 128], func=AF.Copy),
                              r=[pp], w=[dstb])
                kb.barrier()
                kb.release([wq, sk] + hTs)

        def peer_wgen(l, IDX1T, IDX2T, GT):
            with ExitStack() as ph:
                Ps = [kb.sb(ph, f"e_P{i}", [128, 64, 128], BF16) for i in range(2)]
                Qs = [kb.sb(ph, f"e_Q{i}", [128, 64, 128], BF16) for i in range(2)]
                Ws = [kb.sb(ph, f"e_W{i}", [128, 128, 128], BF16) for i in range(2)]
                hcnt = 0
                for t in range(NT):
                    W = Ws[t % 2]
                    for hf in range(2):
                        P, Q = Ps[hcnt % 2], Qs[hcnt % 2]
                        hcnt += 1
                        rows = slice(t * 128 + hf * 64, t * 128 + hf * 64 + 64)
                        io3 = iotab[:].unsqueeze(1).to_broadcast([128, 64, 128])
                        kb.op(dve, lambda h: h.tensor_tensor(out=Q[:], in0=io3,
                                                             in1=IDX2T[:, rows].unsqueeze(2).to_broadcast([128, 64, 128]),
                                                             op=ALU.is_equal), r=[iotab, IDX2T], w=[Q])
                        kb.op(dve, lambda h: h.tensor_tensor(out=P[:], in0=io3,
                                                             in1=IDX1T[:, rows].unsqueeze(2).to_broadcast([128, 64, 128]),
                                                             op=ALU.is_equal), r=[iotab, IDX1T], w=[P])
                        kb.op(pool, lambda h: h.tensor_tensor(out=P[:], in0=P[:],
                                                              in1=GT[:, rows].unsqueeze(2).to_broadcast([128, 64, 128]),
                                                              op=ALU.mult), r=[P, GT], w=[P])
                        for t4 in range(0, 64, 4):
                            pp = kb.ps()
                            kb.mm_multi([pp], [[(pp[:, j * 128:(j + 1) * 128], Q[:, t4 + j, :], P[:, t4 + j, :])] for j in range(4)],
                                        r=[P, Q])
                            src = pp[:, :].rearrange("p (t i) -> p i t", t=4)
                            tw = hf * 64 + t4
                            if (t4 // 4) % 2 == 0:
                                kb.op(act, lambda h: h.activation(out=W[:, :, tw:tw + 4], in_=src, func=AF.Copy), r=[pp], w=[W])
                            else:
                                kb.op(dve, lambda h: h.tensor_copy(out=W[:, :, tw:tw + 4], in_=src), r=[pp], w=[W])
                    for c0 in range(0, 128, 32):
                        kb.dma(sp, WT[t, :, c0:c0 + 32, :], W[:, c0:c0 + 32, :], W, r=[W], ndesc=128)
                kb.barrier()
                kb.release(Ps + Qs + Ws)

        def peer_dense(l, dst_tok, final):
            SC = 4
            nsc = 128 // SC
            utv = ut[l].rearrange("(k p) e -> p k e", p=128)
            vvv = vv[l].rearrange("(c p) d -> p c d", p=128)
            maxpass = 6
            passes = []
            t = 0
            while t < NT:
                n = min(maxpass, NT - t)
                passes.append((t, n))
                t += n
            with ExitStack() as ph:
                rr = [0]
                g, b = load_ln_params(ph, l * 2 + 1)
                acc = kb.sb(ph, "f_acc", [128, maxpass, D], F32)
                hTp = kb.sb(ph, "f_hT", [128, KD, maxpass * 128], BF16)
                us = [kb.sb(ph, f"f_u{i}", [128, KD, SC * 128], BF16) for i in range(2)]
                vs_ = [kb.sb(ph, f"f_v{i}", [128, SC, D], BF16) for i in range(2)]
                gsb = [kb.sb(ph, f"f_g{i}", [128, SC, 512], BF16) for i in range(3)]
                wts = [kb.sb(ph, f"f_w{i}", [128, SC, 512], BF16) for i in range(3)]
                xs = [kb.sb(ph, f"f_x{i}", [128, D], F32) for i in range(1)]
                ys = [kb.sb(ph, f"f_y{i}", [128, D], F32) for i in range(1)]
                stat = [kb.sb(ph, f"f_st{i}", [128, 8], F32) for i in range(2)]
                if os.environ.get('MK_DEBUG'):
                    print('dense sbuf remaining', nc.sbuf_bytes_remaining)
                ui = 0
                gi = 0
                for (pt0, pn) in passes:
                    ntp = pn * 128
                    tok0 = pt0 * 128
                    kb.dma(sp, hTp[:, :, 0:ntp], H1T[:, tok0:tok0 + ntp].rearrange("(c p) t -> p c t", p=128), hTp, w=[hTp],
                           ndesc=128 * KD)
                    steps = [(sc_i, g0) for sc_i in range(nsc) for g0 in range(0, pn, 4)]
                    uv = {}
                    ctx = {}

                    def stA(step):
                        nonlocal ui, gi
                        sc_i, g0 = step
                        if g0 == 0:
                            u, v_ = us[ui % 2], vs_[ui % 2]
                            ui += 1
                            uv[sc_i] = (u, v_)
                            e0 = sc_i * SC * 128
                            kq = max(1, KD // 4)
                            for k0 in range(0, KD, kq):
                                kb.dma(pool, u[:, k0:k0 + kq, :], utv[:, k0:k0 + kq, e0:e0 + SC * 128], u, w=[u], ndesc=128 * kq)
                            for c in range(SC):
                                kb.dma(pool, v_[:, c, :], vvv[:, sc_i * SC + c, :], v_, w=[v_], ndesc=128)
                        u, v_ = uv[sc_i]
                        ng = min(4, pn - g0)
                        ntk = ng * 128
                        G, Wt = gsb[gi % 3], wts[gi % 3]
                        gi += 1
                        for tt in range(ng):
                            kb.dma(sp, Wt[:, :, tt * 128:(tt + 1) * 128], WT[pt0 + g0 + tt, :, sc_i * SC:(sc_i + 1) * SC, :],
                                   Wt, w=[Wt], ndesc=128)
                        for c in range(SC):
                            pp = kb.ps()
                            kb.mm(pp, [(pp[:, 0:ntk], u[:, k, c * 128:(c + 1) * 128], hTp[:, k, g0 * 128:g0 * 128 + ntk])
                                       for k in range(KD)], r=[u, hTp])
                            kb.op(act, lambda h: h.activation(out=G[:, c, 0:ntk], in_=pp[:, 0:ntk], func=AF.Gelu), r=[pp],
                                  w=[G])
                        kb.op(dve, lambda h: h.tensor_tensor(out=G[:, :, 0:ntk], in0=G[:, :, 0:ntk], in1=Wt[:, :, 0:ntk],
                                                             op=ALU.mult), r=[G, Wt], w=[G])
                        ctx[step] = (G, ng, v_)

                    def stB(step):
                        sc_i, g0 = step
                        H, ng, v_ = ctx.pop(step)
                        for tt in range(ng):
                            ti = g0 + tt
                            for f0 in range(0, D, 512):
                                pp = kb.ps()
                                kb.mm(pp, [(pp[:, :], H[:, c, tt * 128:(tt + 1) * 128], v_[:, c, f0:f0 + 512])
                                           for c in range(SC)], r=[H, v_])
                                if sc_i == 0:
                                    kb.op(dve, lambda h: h.tensor_copy(out=acc[:, ti, f0:f0 + 512], in_=pp[:, :]), r=[pp],
                                          w=[acc])
                                else:
                                    kb.op(dve, lambda h: h.tensor_tensor(out=acc[:, ti, f0:f0 + 512],
                                                                         in0=acc[:, ti, f0:f0 + 512], in1=pp[:, :],
                                                                         op=ALU.add), r=[pp, acc], w=[acc])

                    for i_ in range(len(steps) + 1):
                        if i_ < len(steps):
                            stA(steps[i_])
                        if i_ >= 1:
                            stB(steps[i_ - 1])
                    for ti in range(pn):
                        x, yb, stt = xs[0], ys[0], stat[ti % 2]
                        rows = slice(tok0 + ti * 128, tok0 + (ti + 1) * 128)
                        kb.dma(sp, x[:], H1[rows, :], x, w=[x], ndesc=128)
                        kb.op(dve, lambda h: h.scalar_tensor_tensor(out=x[:], in0=x[:], scalar=cfg.alpha, in1=acc[:, ti, :],
                                                                    op0=ALU.mult, op1=ALU.add), r=[x, acc], w=[x])
                        layer_norm(x, yb, g, b, stt)
                        kb.dma(sp, dst_tok[rows, :], yb[:], yb, r=[yb], ndesc=128)
                kb.barrier()
                kb.release([g, b, acc, hTp] + us + vs_ + gsb + wts + xs + ys + stat)

        stop = cfg.stop
        for l in range(L if stop != ("setup", 0) else 0):
            src_tok = xin if l == 0 else H2
            dst_tok = y if l == L - 1 else H2
            fm = [(0, 12, QT), (SW, 12, KT), (3 * SW, 4, MQT)]
            tm = [(SW, SW, nk[l], None), (2 * SW, SW, nv[l], VB)]
            project(src_tok, NT, w_in[l], 5120, fm, tm, KD, "pa")
            project(memp, NMT // 128, w_mkv[l], 1024, [(0, 4, MKT)], [(0, 512, nmk[l], None), (512, 512, nmv[l], MVB)], KD,
                    "pm")
            if stop == ("proj", l):
                break
            cache_transposes(l)
            attention(l)
            if stop == ("attn", l):
                break
            out_proj(l, src_tok)
            if stop == ("oproj", l):
                break
            with ExitStack() as pph:
                IDX1T = kb.sb(pph, "IDX1T", [128, NTOK], BF16)
                IDX2T = kb.sb(pph, "IDX2T", [128, NTOK], BF16)
                GT = kb.sb(pph, "GT", [128, NTOK], BF16)
                peer_retrieve(l, IDX1T, IDX2T, GT)
                peer_wgen(l, IDX1T, IDX2T, GT)
            if stop == ("wgen", l):
                break
            peer_dense(l, dst_tok, l == L - 1)
            if stop == ("dense", l):
                break
        kb.barrier()
    return nc


def shard_inputs(cfg, inputs, n_cores):
    c = make_consts()
    maps = []
    L = cfg.DEPTH
    u_t = np.ascontiguousarray(np.transpose(inputs["peer_u"], (0, 2, 1)))
    subk = np.ascontiguousarray(np.transpose(inputs["peer_sub_keys"], (0, 1, 2, 4, 3))).reshape(L, 16, 128, 128)
    w_mkv = np.ascontiguousarray(np.concatenate([inputs["w_mem_k"], inputs["w_mem_v"]], axis=2))
    for ci in range(n_cores):
        ps = slice(ci * cfg.NSEQ, (ci + 1) * cfg.NSEQ)
        ss = slice(ci * cfg.NSMP, (ci + 1) * cfg.NSMP)
        xin = np.concatenate([inputs["x_prompt"][ps].reshape(-1, cfg.D), inputs["x_sample"][ss].reshape(-1, cfg.D)], axis=0)
        m = {
            "xin": np.ascontiguousarray(xin),
            "w_in": inputs["w_in"], "w_o": inputs["w_o"], "w_mkv": w_mkv, "w_q": inputs["peer_w_q"],
            "subk": subk, "ut": u_t, "vv": inputs["peer_v"],
            "ck": np.ascontiguousarray(inputs["cache_self_k"][:, ss]),
            "cv": np.ascontiguousarray(inputs["cache_self_v"][:, ss]),
            "cmk": np.ascontiguousarray(inputs["cache_mem_k"][:, ss]).reshape(L, cfg.NSMP, NMEM, 512),
            "cmv": np.ascontiguousarray(inputs["cache_mem_v"][:, ss]).reshape(L, cfg.NSMP, NMEM, 512),
            "memp": np.ascontiguousarray(inputs["mem_prompt"][ps]).reshape(-1, cfg.D),
            "relb": inputs["rel_bias_table"],
            "lamv": inputs["diff_lambda"].reshape(1, 512),
            "subg": inputs["diff_subln_g"].reshape(1, 256),
            "lng": inputs["ln_g"].reshape(L * 2, cfg.D), "lnb": inputs["ln_b"].reshape(L * 2, cfg.D),
        }
        for k, v in c.items():
            m["c_" + k] = v
        maps.append({k: np.ascontiguousarray(v, dtype=np.float32) for k, v in m.items()})
    return maps


def assemble(cfg, results, n_cores):
    L = cfg.DEPTH
    NP = cfg.NP
    yp = np.concatenate([r["y"][:NP].reshape(cfg.NSEQ, cfg.SEQ, cfg.D) for r in results], axis=0)
    ys = np.concatenate([r["y"][NP:].reshape(cfg.NSMP, DEC_SEQ, cfg.D) for r in results], axis=0)
    nkp = np.concatenate([r["nk"][:, :NP].reshape(L, cfg.NSEQ, cfg.SEQ, SW) for r in results], axis=1)
    nvp = np.concatenate([r["nv"][:, :NP].reshape(L, cfg.NSEQ, cfg.SEQ, SW) for r in results], axis=1)
    nks = np.concatenate([r["nk"][:, NP:].reshape(L, cfg.NSMP, DEC_SEQ, SW) for r in results], axis=1)
    nvs = np.concatenate([r["nv"][:, NP:].reshape(L, cfg.NSMP, DEC_SEQ, SW) for r in results], axis=1)
    nmk = np.concatenate([r["nmk"].reshape(L, cfg.NSEQ, NMEM, 4, 128) for r in results], axis=1)
    nmv = np.concatenate([r["nmv"].reshape(L, cfg.NSEQ, NMEM, 4, 128) for r in results], axis=1)
    return (yp, ys, nkp, nvp, nmk, nmv, nks, nvs)


def kernel(**inputs):
    n_cores = 8
    inputs = {k: np.asarray(v) for k, v in inputs.items()}
    B, SEQ, D = inputs["x_prompt"].shape
    DB = inputs["x_sample"].shape[0]
    PAST = inputs["cache_self_k"].shape[2]
    cfg = Cfg(D=D, SEQ=SEQ, NSEQ=B // n_cores, NSMP=DB // n_cores, PAST=PAST, DEPTH=inputs["w_in"].shape[0])
    nc = build(cfg)
    maps = shard_inputs(cfg, inputs, n_cores)
    res = run_bass_kernel_spmd(nc, maps, core_ids=list(range(n_cores)))
    return assemble(cfg, res.results, n_cores)
```

```python
import math
import os
from contextlib import ExitStack
import numpy as np
import concourse.bass as bass
import concourse.mybir as mybir
from concourse.bass_utils import run_bass_kernel_spmd

F32 = mybir.dt.float32
BF16 = mybir.dt.bfloat16
U32 = mybir.dt.uint32
AF = mybir.ActivationFunctionType
ALU = mybir.AluOpType
AX = mybir.AxisListType

NEG = -30000.0
LN_EPS = 1e-5
SW = 1536
NMEM = 256
DEC_SEQ = 32
NEXP = 16384


class Cfg:
    def __init__(self, D=2048, SEQ=2048, NSEQ=2, NSMP=4, PAST=2048, DEPTH=2, debug=False, stop=None):
        self.D, self.SEQ, self.NSEQ, self.NSMP, self.PAST, self.DEPTH = D, SEQ, NSEQ, NSMP, PAST, DEPTH
        self.KD = D // 128
        self.NP = NSEQ * SEQ
        self.NTOK = self.NP + NSMP * DEC_SEQ
        assert self.NTOK % 128 == 0 and SEQ % 256 == 0 and PAST % 128 == 0
        self.NT = self.NTOK // 128
        self.NMT = NSEQ * NMEM
        self.debug = debug
        self.stop = stop
        self.alpha = (2 * DEPTH) ** 0.25


class Sem:
    def __init__(self, h):
        self.h = h
        self.count = 0


class Buf:
    def __init__(self, t, name=""):
        self.t = t
        self.name = name
        self.w = None
        self.r = {}
        self.dsem = None
        self.dsem_sw = None
        self.excl = False

    def __getitem__(self, k):
        return self.t[k]


class Eng:
    def __init__(self, h, sem, name):
        self.h = h
        self.sem = sem
        self.name = name
        self.seen = {}
        self.outst = []


class KB:
    def __init__(self, nc, st):
        self.nc = nc
        self.st = st
        mk = lambda n: Sem(st.enter_context(nc.semaphore(n)))
        self.pe = Eng(nc.tensor, mk("s_pe"), "pe")
        self.act = Eng(nc.scalar, mk("s_act"), "act")
        self.dve = Eng(nc.vector, mk("s_dve"), "dve")
        self.pool = Eng(nc.gpsimd, mk("s_pool"), "pool")
        self.sp = Eng(nc.sync, mk("s_sp"), "sp")
        self.engs = [self.pe, self.act, self.dve, self.pool, self.sp]
        self.dsems = [mk(f"s_d{i}") for i in range(48)]
        self.free_dsems = list(self.dsems)
        self.dsems_sw = [mk(f"s_w{i}") for i in range(28)]
        self.free_dsems_sw = list(self.dsems_sw)
        self.psum = []
        for i in range(8):
            t = st.enter_context(nc.psum_tensor(f"ps{i}", [128, 512], F32))
            self.psum.append(Buf(t, f"ps{i}"))
            self.psum[-1].excl = True
        self.ps_rr = 0
        self.reserved = set()

    def sb(self, ph, name, shape, dt):
        self.uid = getattr(self, "uid", 0) + 1
        name = f"{name}_{self.uid}"
        return Buf(ph.enter_context(self.nc.sbuf_tensor(name, list(shape), dt)), name)

    def release(self, bufs):
        for b in bufs:
            if b.dsem is not None:
                self.free_dsems.append(b.dsem)
                b.dsem = None
            if b.dsem_sw is not None:
                self.free_dsems_sw.append(b.dsem_sw)
                b.dsem_sw = None

    def ps(self, reserve=False):
        while True:
            b = self.psum[self.ps_rr % 8]
            self.ps_rr += 1
            if id(b) not in self.reserved:
                break
        if reserve:
            self.reserved.add(id(b))
        return b

    def unreserve(self, b):
        self.reserved.discard(id(b))

    def _wait(self, e, evs):
        best = {}
        for (s, v) in evs:
            if v > best.get(id(s), (s, 0))[1]:
                best[id(s)] = (s, v)
        for (s, v) in best.values():
            if e is self.pe and s is e.sem:
                continue
            if e.seen.get(id(s), 0) < v:
                e.h.wait_ge(s.h, v)
                e.seen[id(s)] = v

    def _deps(self, r, w):
        evs = []
        for b in r:
            if b.w is not None:
                evs.append(b.w)
            if b.excl:
                evs.extend(b.r.values())
        for b in w:
            if b.w is not None:
                evs.append(b.w)
            evs.extend(b.r.values())
        return evs

    def _record(self, ev, r, w):
        for b in r:
            b.r[id(ev[0])] = ev
        for b in w:
            b.w = ev
            b.r = {}

    def op(self, e, fn, r=(), w=()):
        self._wait(e, self._deps(r, w))
        ins = fn(e.h)
        e.sem.count += 1
        ins.then_inc(e.sem.h, 1)
        ev = (e.sem, e.sem.count)
        self._record(ev, r, w)
        return ev

    def mm(self, out_buf, mms, r=(), extra_w=()):
        e = self.pe
        self._wait(e, self._deps(r, [out_buf] + list(extra_w)))
        n = len(mms)
        ins = None
        for i, (o, l, rh) in enumerate(mms):
            ins = e.h.matmul(o, l, rh, start=(i == 0), stop=(i == n - 1))
        e.sem.count += 1
        ins.then_inc(e.sem.h, 1)
        ev = (e.sem, e.sem.count)
        self._record(ev, r, [out_buf] + list(extra_w))
        return ev

    def mm_multi(self, out_bufs, groups, r=()):
        e = self.pe
        self._wait(e, self._deps(r, out_bufs))
        ins = None
        for g in groups:
            n = len(g)
            for i, (o, l, rh) in enumerate(g):
                ins = e.h.matmul(o, l, rh, start=(i == 0), stop=(i == n - 1))
        e.sem.count += 1
        ins.then_inc(e.sem.h, 1)
        ev = (e.sem, e.sem.count)
        self._record(ev, r, out_bufs)
        return ev

    def transpose(self, out_buf, out_ap, in_ap, ident_ap, r=()):
        e = self.pe
        self._wait(e, self._deps(r, [out_buf]))
        ins = e.h.transpose(out_ap, in_ap, ident_ap)
        e.sem.count += 1
        ins.then_inc(e.sem.h, 1)
        ev = (e.sem, e.sem.count)
        self._record(ev, r, [out_buf])
        return ev

    def dma(self, e, out, in_, sbuf, r=(), w=(), ndesc=256):
        if e is self.pool:
            if sbuf.dsem_sw is None:
                sbuf.dsem_sw = self.free_dsems_sw.pop(0)
            ds = sbuf.dsem_sw
        else:
            if sbuf.dsem is None:
                sbuf.dsem = self.free_dsems.pop(0)
            ds = sbuf.dsem
        lim = 1536 if e is self.pool else 6144
        tot = sum(n for _, n in e.outst) + ndesc
        evs = self._deps(r, w)
        while e.outst and tot > lim:
            ev0, n0 = e.outst.pop(0)
            evs.append((ev0[0], ev0[0].count))
            tot -= n0
        self._wait(e, evs)
        ins = e.h.dma_start(out=out, in_=in_)
        ds.count += 16
        ins.then_inc(ds.h, 16)
        ev = (ds, ds.count)
        e.outst.append((ev, ndesc))
        self._record(ev, r, w)
        return ev

    def barrier(self):
        evs = [(e.sem, e.sem.count) for e in self.engs if e.sem.count > 0]
        evs += [(d, d.count) for d in self.dsems + self.dsems_sw if d.count > 0]
        for e in self.engs:
            self._wait(e, evs)
            e.outst = []


def t5_bucket_np(rel):
    nb = 16
    max_exact = 8
    ret = np.where(rel > 0, nb, 0)
    n = np.abs(rel)
    nf = np.maximum(n, 1).astype(np.float32)
    large = max_exact + (np.log(nf / max_exact) / math.log(128 / max_exact) * (nb - max_exact)).astype(np.int32)
    large = np.minimum(large, nb - 1)
    return ret + np.where(n < max_exact, n, large)


def make_consts():
    c = {}
    c["ident"] = np.eye(128, dtype=np.float32)
    c["iota"] = np.tile(np.arange(128, dtype=np.float32)[None, :], (128, 1))
    k = np.arange(128)[:, None]
    q = np.arange(128)[None, :]
    c["tri"] = (k >= q).astype(np.float32)
    c["ones"] = np.ones((128, 128), np.float32)
    c["jx"] = np.eye(128, dtype=np.float32)[::-1].copy()
    c["maskd"] = np.where((k // 64) <= (q // 64), 0.0, NEG).astype(np.float32)
    c["masksb"] = (k < q).astype(np.float32)
    rel = np.arange(384) - 255
    bk = t5_bucket_np(rel.astype(np.int32))
    oh = np.zeros((32, 384), np.float32)
    oh[bk, np.arange(384)] = 1.0
    c["ohr"] = oh[:, ::-1].copy()
    return c


CONST_SHAPES = {"ident": [128, 128], "iota": [128, 128], "tri": [128, 128], "ones": [128, 128],
                "jx": [128, 128], "maskd": [128, 128], "masksb": [128, 128], "ohr": [32, 384]}


def build(cfg):
    nc = bass.Bass("TRN2", target_bir_lowering=False)
    D, KD, NTOK, NT, L = cfg.D, cfg.KD, cfg.NTOK, cfg.NT, cfg.DEPTH
    NSEQ, NSMP, SEQ, PAST, NMT = cfg.NSEQ, cfg.NSMP, cfg.SEQ, cfg.PAST, cfg.NMT

    def din(name, shape, dt=F32):
        return nc.dram_tensor(name, list(shape), dt, kind="ExternalInput").ap()

    def dout(name, shape, dt=F32):
        return nc.dram_tensor(name, list(shape), dt, kind="ExternalOutput").ap()

    def dscr(name, shape, dt):
        return nc.dram_tensor(name, list(shape), dt, kind="ExternalOutput" if cfg.debug else "Internal").ap()

    xin = din("xin", [NTOK, D])
    w_in = din("w_in", [L, D, 5120])
    w_o = din("w_o", [L, 2048, D])
    w_mkv = din("w_mkv", [L, D, 1024])
    w_q = din("w_q", [L, D, 2048])
    subk = din("subk", [L, 16, 128, 128])
    ut = din("ut", [L, D, NEXP])
    vv = din("vv", [L, NEXP, D])
    ck = din("ck", [L, NSMP, PAST, SW])
    cv = din("cv", [L, NSMP, PAST, SW])
    cmk = din("cmk", [L, NSMP, NMEM, 512])
    cmv = din("cmv", [L, NSMP, NMEM, 512])
    memp = din("memp", [NMT, D])
    relb = din("relb", [32, 6])
    lamv = din("lamv", [1, 512])
    subg = din("subg", [1, 256])
    lng = din("lng", [L * 2, D])
    lnb = din("lnb", [L * 2, D])
    cst = {k: din("c_" + k, s) for k, s in CONST_SHAPES.items()}

    y = dout("y", [NTOK, D])
    nk = dout("nk", [L, NTOK, SW])
    nv = dout("nv", [L, NTOK, SW])
    nmk = dout("nmk", [L, NMT, 512])
    nmv = dout("nmv", [L, NMT, 512])

    QT = dscr("QT", [SW, NTOK], BF16)
    KT = dscr("KT", [SW, NTOK], BF16)
    MQT = dscr("MQT", [512, NTOK], BF16)
    VB = dscr("VB", [NTOK, SW], BF16)
    MKT = dscr("MKT", [512, NMT], BF16)
    MVB = dscr("MVB", [NMT, 512], BF16)
    KTS = dscr("KTS", [NSMP, SW, PAST], BF16)
    MKTS = dscr("MKTS", [NSMP, 512, NMEM], BF16)
    MIXT = dscr("MIXT", [2048, NTOK], BF16)
    H1 = dscr("H1", [NTOK, D], F32)
    H1T = dscr("H1T", [D, NTOK], BF16)
    H2 = dscr("H2", [NTOK, D], F32)
    WT = dscr("WT", [NT, 128, 128, 128], BF16)
    BIASV = dscr("BIASV", [6, 384], F32)

    with ExitStack() as st:
        kb = KB(nc, st)
        pe, act, dve, pool, sp = kb.pe, kb.act, kb.dve, kb.pool, kb.sp

        identf = kb.sb(st, "identf", [128, 128], F32)
        identb = kb.sb(st, "identb", [128, 128], BF16)
        iota = kb.sb(st, "iota", [128, 128], F32)
        iotab = kb.sb(st, "iotab", [128, 128], BF16)
        trib = kb.sb(st, "trib", [128, 128], F32)
        tribb = kb.sb(st, "tribb", [128, 128], BF16)
        onesb = kb.sb(st, "onesb", [128, 128], BF16)
        onesf = kb.sb(st, "onesf", [128, 128], F32)
        jx = kb.sb(st, "jx", [128, 128], F32)
        maskd = kb.sb(st, "maskd", [128, 128], F32)
        masksb = kb.sb(st, "masksb", [128, 128], F32)
        lam_t = kb.sb(st, "lam_t", [128, 4], F32)
        subg_bc = kb.sb(st, "subg_bc", [128, 256], F32)
        cfar = kb.sb(st, "cfar", [128, 6], F32)
        zero_c = kb.sb(st, "zero_c", [128, 1], F32)
        bias_t = kb.sb(st, "bias_t", [128, 6, 2, 128], F32)

        kb.dma(sp, identf[:], cst["ident"], identf, w=[identf])
        kb.dma(pool, identb[:], cst["ident"], identb, w=[identb])
        kb.dma(sp, iota[:], cst["iota"], iota, w=[iota])
        kb.dma(pool, iotab[:], cst["iota"], iotab, w=[iotab])
        kb.dma(sp, trib[:], cst["tri"], trib, w=[trib])
        kb.dma(pool, tribb[:], cst["tri"], tribb, w=[tribb])
        kb.dma(pool, onesb[:], cst["ones"], onesb, w=[onesb])
        kb.dma(sp, onesf[:], cst["ones"], onesf, w=[onesf])
        kb.dma(sp, jx[:], cst["jx"], jx, w=[jx])
        kb.dma(sp, maskd[:], cst["maskd"], maskd, w=[maskd])
        kb.dma(sp, masksb[:], cst["masksb"], masksb, w=[masksb])
        kb.dma(sp, subg_bc[:], subg.partition_broadcast(128), subg_bc, w=[subg_bc])
        kb.dma(sp, cfar[:], relb[15:16, :].partition_broadcast(128), cfar, w=[cfar])
        kb.op(dve, lambda h: h.memset(zero_c[:], 0.0), w=[zero_c])

        with ExitStack() as ph:
            lbc = kb.sb(ph, "lbc", [128, 512], F32)
            lpr = kb.sb(ph, "lpr", [128, 256], F32)
            lsm = kb.sb(ph, "lsm", [128, 2], F32)
            kb.dma(sp, lbc[:], lamv.partition_broadcast(128), lbc, w=[lbc])
            l4 = lbc[:].rearrange("p (a b) -> p a b", a=4)
            kb.op(dve, lambda h: h.tensor_tensor(out=lpr[:].rearrange("p (a b) -> p a b", a=2), in0=l4[:, 0:4:2, :],
                                                 in1=l4[:, 1:4:2, :], op=ALU.mult), r=[lbc], w=[lpr])
            kb.op(dve, lambda h: h.reduce_sum(out=lsm[:], in_=lpr[:].rearrange("p (a b) -> p a b", a=2), axis=AX.X),
                  r=[lpr], w=[lsm])
            kb.op(act, lambda h: h.activation(out=lsm[:], in_=lsm[:], func=AF.Exp), r=[lsm], w=[lsm])
            lam_init0 = 0.8 - 0.6 * math.exp(-0.3 * 0)
            kb.op(dve, lambda h: h.tensor_tensor(out=lam_t[:, 0:1], in0=lsm[:, 0:1], in1=lsm[:, 1:2], op=ALU.subtract),
                  r=[lsm], w=[lam_t])
            kb.op(dve, lambda h: h.tensor_scalar(out=lam_t[:, 0:1], in0=lam_t[:, 0:1], scalar1=lam_init0, scalar2=None,
                                                 op0=ALU.add), r=[lam_t], w=[lam_t])
            kb.op(dve, lambda h: h.tensor_scalar(out=lam_t[:, 1:2], in0=lam_t[:, 0:1], scalar1=-1.0, scalar2=None,
                                                 op0=ALU.mult), r=[lam_t], w=[lam_t])
            kb.op(dve, lambda h: h.tensor_scalar(out=subg_bc[:], in0=subg_bc[:], scalar1=(1.0 - lam_init0), scalar2=None,
                                                 op0=ALU.mult), r=[subg_bc], w=[subg_bc])
            relb_sb = kb.sb(ph, "relb_sb", [32, 6], F32)
            ohr_sb = kb.sb(ph, "ohr_sb", [32, 384], F32)
            fv_sb = kb.sb(ph, "fv_sb", [6, 384], F32)
            bp = kb.sb(ph, "bp", [128, 6, 2, 128], F32)
            kb.dma(sp, relb_sb[:], relb, relb_sb, w=[relb_sb])
            kb.dma(sp, ohr_sb[:], cst["ohr"], ohr_sb, w=[ohr_sb])
            p0 = kb.ps()
            kb.mm(p0, [(p0[0:6, 0:384], relb_sb[:], ohr_sb[:])], r=[relb_sb, ohr_sb])
            kb.op(dve, lambda h: h.tensor_copy(out=fv_sb[:], in_=p0[0:6, 0:384]), r=[p0], w=[fv_sb])
            kb.dma(sp, BIASV, fv_sb[:], fv_sb, r=[fv_sb])
            kb.barrier()
            for hh in range(6):
                for ti, off in enumerate((1, 129)):
                    src = bass.AP(tensor=BIASV.tensor, offset=hh * 384 + off, ap=[[1, 128], [1, 128]])
                    kb.dma(sp, bp[:, hh, ti, :], src, bp, w=[bp])
            for hh in range(6):
                pp = kb.ps()
                kb.mm(pp, [(pp[:, 0:256], jx[:], bp[:, hh, :, :].rearrange("p a b -> p (a b)"))], r=[jx, bp])
                kb.op(dve, lambda h: h.tensor_copy(out=bias_t[:, hh, :, :].rearrange("p a b -> p (a b)"), in_=pp[:, 0:256]),
                      r=[pp], w=[bias_t])
            for hh in range(6):
                kb.op(dve, lambda h: h.tensor_tensor(out=bias_t[:, hh, 0, :], in0=bias_t[:, hh, 0, :], in1=maskd[:],
                                                     op=ALU.add), r=[bias_t, maskd], w=[bias_t])
            kb.barrier()
            kb.release([lbc, relb_sb, ohr_sb, fv_sb, bp, identf, identb, iota, iotab, trib, tribb, onesb, onesf, jx, maskd, masksb, subg_bc, cfar])

        def load_ln_params(ph, idx):
            g = kb.sb(ph, "ln_g", [128, D], F32)
            b = kb.sb(ph, "ln_b", [128, D], F32)
            kb.dma(sp, g[:], lng[idx:idx + 1, :].partition_broadcast(128), g, w=[g])
            kb.dma(sp, b[:], lnb[idx:idx + 1, :].partition_broadcast(128), b, w=[b])
            return g, b

        def layer_norm(r_buf, out_buf, g, b, stat):
            junk = out_buf
            kb.op(act, lambda h: h.activation(out=junk[:], in_=r_buf[:], func=AF.Copy, accum_out=stat[:, 0:1]),
                  r=[r_buf], w=[junk, stat])
            kb.op(dve, lambda h: h.tensor_scalar(out=stat[:, 1:2], in0=stat[:, 0:1], scalar1=-1.0 / D, scalar2=None,
                                                 op0=ALU.mult), r=[stat], w=[stat])
            kb.op(act, lambda h: h.activation(out=junk[:], in_=r_buf[:], func=AF.Square, bias=stat[:, 1:2], scale=1.0,
                                              accum_out=stat[:, 2:3]), r=[r_buf, stat], w=[junk, stat])
            kb.op(dve, lambda h: h.tensor_scalar(out=stat[:, 3:4], in0=stat[:, 2:3], scalar1=1.0 / D, scalar2=LN_EPS,
                                                 op0=ALU.mult, op1=ALU.add), r=[stat], w=[stat])
            kb.op(act, lambda h: h.activation(out=stat[:, 3:4], in_=stat[:, 3:4], func=AF.Sqrt), r=[stat], w=[stat])
            kb.op(dve, lambda h: h.reciprocal(out=stat[:, 4:5], in_=stat[:, 3:4]), r=[stat], w=[stat])
            kb.op(dve, lambda h: h.tensor_tensor(out=stat[:, 5:6], in0=stat[:, 1:2], in1=stat[:, 4:5], op=ALU.mult),
                  r=[stat], w=[stat])
            kb.op(act, lambda h: h.activation(out=junk[:], in_=r_buf[:], func=AF.Identity, bias=stat[:, 5:6],
                                              scale=stat[:, 4:5]), r=[r_buf, stat], w=[junk])
            kb.op(dve, lambda h: h.tensor_tensor(out=junk[:], in0=junk[:], in1=g[:], op=ALU.mult), r=[junk, g], w=[junk])
            kb.op(dve, lambda h: h.tensor_tensor(out=out_buf[:], in0=junk[:], in1=b[:], op=ALU.add), r=[junk, b],
                  w=[out_buf])

        def transpose_to_T(src_bf, dstT, col0, ncol_chunks, evict_rr):
            for c0 in range(0, ncol_chunks, 4):
                n = min(4, ncol_chunks - c0)
                pp = kb.ps()
                ppb = pp[:, 0:256].bitcast(BF16)
                for c in range(n):
                    kb.transpose(pp, ppb[:, c * 128:(c + 1) * 128], src_bf[:, (c0 + c) * 128:(c0 + c + 1) * 128],
                                 identb[:], r=[src_bf, identb])
                e = act if (evict_rr[0] % 2 == 0) else dve
                evict_rr[0] += 1
                if e is act:
                    kb.op(act, lambda h: h.activation(out=dstT[:, c0:c0 + n, col0:col0 + 128],
                                                      in_=ppb[:, 0:n * 128].rearrange("p (c t) -> p c t", c=n),
                                                      func=AF.Copy), r=[pp], w=[dstT])
                else:
                    kb.op(dve, lambda h: h.tensor_copy(out=dstT[:, c0:c0 + n, col0:col0 + 128],
                                                       in_=ppb[:, 0:n * 128].rearrange("p (c t) -> p c t", c=n)),
                          r=[pp], w=[dstT])

        def project(src_tok, ntiles, w_dram, ncols, fm_outs, tm_outs, kd, tag):
            nblk = ncols // 512
            with ExitStack() as ph:
                rr = [0]
                xts = [kb.sb(ph, f"{tag}_xT{i}", [128, kd, 512], BF16) for i in range(2)]
                xbs = [kb.sb(ph, f"{tag}_xb{i}", [128, kd * 128], BF16) for i in range(2)]
                wbs = [kb.sb(ph, f"{tag}_wb{i}", [128, kd, 512], BF16) for i in range(3)]
                ofm = [kb.sb(ph, f"{tag}_ofm{i}", [128, 512], BF16) for i in range(3)]
                otf = [kb.sb(ph, f"{tag}_otf{i}", [128, 512], F32) for i in range(3)]
                otb = [kb.sb(ph, f"{tag}_otb{i}", [128, 512], BF16) for i in range(3)]
                wv = w_dram.rearrange("(k p) f -> p k f", p=128)
                gi = 0
                xi = 0
                wi = 0
                oi = 0
                for t0 in range(0, ntiles, 4):
                    nt_g = min(4, ntiles - t0)
                    ntk = nt_g * 128
                    xT = xts[gi % 2]
                    gi += 1
                    for tt in range(nt_g):
                        xb = xbs[xi % 2]
                        xi += 1
                        kb.dma(pool, xb[:], src_tok[(t0 + tt) * 128:(t0 + tt + 1) * 128, :], xb, w=[xb], ndesc=128)
                        transpose_to_T(xb, xT, tt * 128, kd, rr)
                    CUT = 0
                    for b in range(nblk if CUT != 1 else 0):
                        wb = wbs[wi % 3]
                        wi += 1
                        kq = max(1, kd // 4)
                        for k0 in range(0, kd, kq):
                            kb.dma(pool, wb[:, k0:k0 + kq, :], wv[:, k0:k0 + kq, b * 512:(b + 1) * 512], wb, w=[wb],
                                   ndesc=128 * kq)
                        for (c0, nch, dstT) in (fm_outs if CUT != 2 else []):
                            for ch in range(nch):
                                col = c0 + ch * 128
                                if col // 512 != b:
                                    continue
                                lo = col - b * 512
                                pp = kb.ps()
                                kb.mm(pp, [(pp[:, 0:ntk], wb[:, k, lo:lo + 128], xT[:, k, 0:ntk]) for k in range(kd)],
                                      r=[wb, xT])
                                ob = ofm[oi % 3]
                                oi += 1
                                e = act if oi % 2 == 0 else dve
                                if e is act:
                                    kb.op(act, lambda h: h.activation(out=ob[:, 0:ntk], in_=pp[:, 0:ntk], func=AF.Copy),
                                          r=[pp], w=[ob])
                                else:
                                    kb.op(dve, lambda h: h.tensor_copy(out=ob[:, 0:ntk], in_=pp[:, 0:ntk]), r=[pp], w=[ob])
                                kb.dma(sp, dstT[ch * 128:(ch + 1) * 128, t0 * 128:t0 * 128 + ntk], ob[:, 0:ntk], ob, r=[ob],
                                       ndesc=128)
                        for (c0, ncl, dst32, dst16) in (tm_outs if CUT not in (2, 3) else []):
                            if not (c0 <= b * 512 < c0 + ncl):
                                continue
                            dc = b * 512 - c0
                            for tt in range(nt_g):
                                pp = kb.ps()
                                kb.mm(pp, [(pp[:, :], xT[:, k, tt * 128:(tt + 1) * 128], wb[:, k, :]) for k in range(kd)],
                                      r=[wb, xT])
                                rows = slice((t0 + tt) * 128, (t0 + tt + 1) * 128)
                                if dst32 is not None:
                                    o32 = otf[oi % 3]
                                    kb.op(act, lambda h: h.activation(out=o32[:], in_=pp[:, :], func=AF.Copy), r=[pp],
                                          w=[o32])
                                    kb.dma(sp, dst32[rows, dc:dc + 512], o32[:], o32, r=[o32], ndesc=128)
                                if dst16 is not None:
                                    o16 = otb[oi % 3]
                                    kb.op(dve, lambda h: h.tensor_copy(out=o16[:], in_=pp[:, :]), r=[pp], w=[o16])
                                    kb.dma(sp, dst16[rows, dc:dc + 512], o16[:], o16, r=[o16], ndesc=128)
                                oi += 1
                kb.barrier()
                kb.release(xts + xbs + wbs + ofm + otf + otb)

        def cache_transposes(l):
            with ExitStack() as ph:
                rr = [0]
                kin = [kb.sb(ph, f"ckin{i}", [128, SW], BF16) for i in range(2)]
                kout = [kb.sb(ph, f"ckout{i}", [128, 12, 128], BF16) for i in range(2)]
                i = 0
                for s in range(NSMP):
                    for blk in range(PAST // 128):
                        a = kin[i % 2]
                        o = kout[i % 2]
                        i += 1
                        kb.dma(pool, a[:], ck[l, s, blk * 128:(blk + 1) * 128, :], a, w=[a], ndesc=128)
                        transpose_to_T(a, o, 0, 12, rr)
                        kb.dma(sp, KTS[s, :, blk * 128:(blk + 1) * 128].rearrange("(c p) t -> p c t", p=128), o[:], o,
                               r=[o], ndesc=128 * 12)
                    for blk in range(NMEM // 128):
                        a = kin[i % 2]
                        o = kout[i % 2]
                        i += 1
                        kb.dma(pool, a[:, 0:512], cmk[l, s, blk * 128:(blk + 1) * 128, :], a, w=[a], ndesc=128)
                        transpose_to_T(a, o, 0, 4, rr)
                        kb.dma(sp, MKTS[s, :, blk * 128:(blk + 1) * 128].rearrange("(c p) t -> p c t", p=128),
                               o[:, 0:4, :], o, r=[o], ndesc=128 * 4)
                kb.barrier()
                kb.release(kin + kout)

        def attention(l):
            kind = l % 2
            scale = 128 ** -0.5
            with ExitStack() as ph:
                rr = [0]
                NKMAX = max(SEQ, PAST + 128) // 128
                qTs = [kb.sb(ph, f"a_qT{i}", [128, 2, SEQ], BF16) for i in range(2)]
                kTs = [kb.sb(ph, f"a_kT{i}", [128, 2, NKMAX * 128], BF16) for i in range(2)]
                vss = [kb.sb(ph, f"a_v{i}", [128, NKMAX, 260], BF16) for i in range(2)]
                for vsb in vss:
                    kb.op(dve, lambda h: h.memset(vsb[:], 1.0), w=[vsb])
                NR = 5
                tf = [kb.sb(ph, f"a_tf{i}", [128, 512], F32) for i in range(NR)]
                tg = [kb.sb(ph, f"a_tg{i}", [128, 512], F32) for i in range(NR)]
                tx = [kb.sb(ph, f"a_tx{i}", [128, 256], F32) for i in range(NR)]
                pb = [kb.sb(ph, f"a_pb{i}", [128, 512], BF16) for i in range(NR)]
                raccs = [kb.sb(ph, f"a_racc{i}", [128, 256], F32) for i in range(4)]
                raccb = [kb.sb(ph, f"a_raccb{i}", [128, 256], BF16) for i in range(4)]
                tgb = [kb.sb(ph, f"a_tgb{i}", [128, 256], BF16) for i in range(NR)]
                osb = [kb.sb(ph, f"a_o{i}", [128, 256], F32) for i in range(2)]
                osq = kb.sb(ph, "a_osq", [128, 256], F32)
                obf = [kb.sb(ph, f"a_ob{i}", [128, 256], BF16) for i in range(2)]
                oT = [kb.sb(ph, f"a_oT{i}", [128, 2, 128], BF16) for i in range(2)]
                sm = [kb.sb(ph, f"a_sm{i}", [128, 8], F32) for i in range(2)]
                cnt = {"h": 0, "w": 0, "o": 0}

                def run_head(ncomp, dv, nq_tot, qchunk, kss, cls, mode, bias_h, mix_row0, tok0, loads):
                    hi = cnt["h"]
                    cnt["h"] += 1
                    qT, kT, vs = qTs[hi % 2], kTs[hi % 2], vss[hi % 2]

                    def do_loads():
                        for (cast, which, dstf, src, nd) in loads:
                            buf = {"q": qT, "k": kT, "v": vs}[which]
                            kb.dma(pool if cast else sp, dstf(buf), src, buf, w=[buf], ndesc=nd)

                    def do_compute():
                        nkb = len(kss)
                        for q0 in range(0, nq_tot, qchunk):
                            nq = min(qchunk, nq_tot - q0)
                            nsub = (nq + 127) // 128
                            subs = [(q0 // 128 + i, min(128, nq - i * 128)) for i in range(nsub)]
                            accs = {}
                            for c in range(ncomp):
                                for si in range(nsub):
                                    accs[(c, si)] = kb.ps(reserve=True)
                            order = list(range(nkb))
                            if mode == "sb":
                                order = order[::-1]
                            started = set()
                            live = [j for j in order if any(cls(j, qb) != "skip" for qb, _ in subs)]
                            lastj = {}
                            for si, (qb, _) in enumerate(subs):
                                for j in live:
                                    if cls(j, qb) != "skip":
                                        lastj[si] = j
                            npairs = len(live)
                            st_ = {}

                            def stageA(j):
                                ks = kss[j]
                                types = [cls(j, qb) for qb, _ in subs]
                                sp_ps = kb.ps()
                                kb.mm_multi([sp_ps], [[(sp_ps[0:ks, c * 256:c * 256 + nq], kT[:, c, j * 128:j * 128 + ks],
                                                       qT[:, c, q0:q0 + nq])] for c in range(ncomp)], r=[kT, qT])
                                wi = cnt["w"]
                                cnt["w"] += 1
                                P = pb[wi % NR]
                                T = tf[wi % NR]
                                G = tg[wi % NR]
                                st_[j] = dict(ks=ks, types=types, P=P, T=T, G=G, wi=wi)
                                if mode in ("diff", "mem"):
                                    if all(t == "far" for t in types):
                                        bias_ap = cfar[0:ks, bias_h:bias_h + 1] if mode == "diff" else zero_c[0:ks, :]
                                        kb.op(act, lambda h: h.activation(
                                            out=P[0:ks, :].rearrange("p (c q) -> p c q", c=2)[:, 0:ncomp, 0:nq],
                                            in_=sp_ps[0:ks, :].rearrange("p (c q) -> p c q", c=2)[:, 0:ncomp, 0:nq],
                                            func=AF.Exp, bias=bias_ap, scale=scale), r=[sp_ps, cfar, zero_c], w=[P])
                                    else:
                                        for si, (qb, nqs) in enumerate(subs):
                                            t = types[si]
                                            if t == "skip":
                                                continue
                                            for c in range(ncomp):
                                                src = sp_ps[0:ks, c * 256 + si * 128:c * 256 + si * 128 + nqs]
                                                dst = P[0:ks, c * 256 + si * 128:c * 256 + si * 128 + nqs]
                                                if t == "far":
                                                    kb.op(act, lambda h: h.activation(out=dst, in_=src, func=AF.Exp,
                                                                                      bias=cfar[0:ks, bias_h:bias_h + 1],
                                                                                      scale=scale), r=[sp_ps, cfar], w=[P])
                                                else:
                                                    bt = bias_t[0:ks, bias_h, 0 if t == "diag" else 1, 0:nqs]
                                                    tmp = T[0:ks, c * 256 + si * 128:c * 256 + si * 128 + nqs]
                                                    kb.op(dve, lambda h: h.scalar_tensor_tensor(out=tmp, in0=src, scalar=scale,
                                                                                                in1=bt, op0=ALU.mult,
                                                                                                op1=ALU.add),
                                                          r=[sp_ps, bias_t], w=[T])
                                                    kb.op(act, lambda h: h.activation(out=dst, in_=tmp, func=AF.Exp), r=[T],
                                                          w=[P])
                                else:
                                    E = T
                                    SPt = tgb[wi % NR]
                                    st_[j]["G"] = SPt
                                    kb.op(act, lambda h: h.activation(out=E[0:ks, 0:nq], in_=sp_ps[0:ks, 0:nq], func=AF.Exp,
                                                                      scale=scale), r=[sp_ps], w=[E])
                                    kb.op(act, lambda h: h.activation(out=SPt[0:ks, 0:nq], in_=E[0:ks, 0:nq], func=AF.Ln,
                                                                      bias=1.0, scale=1.0), r=[E], w=[SPt])
                                    for si, (qb, nqs) in enumerate(subs):
                                        t = types[si]
                                        sl = slice(si * 128, si * 128 + nqs)
                                        if t == "diag":
                                            kb.op(dve, lambda h: h.tensor_tensor(out=SPt[0:ks, sl], in0=SPt[0:ks, sl],
                                                                                 in1=masksb[0:ks, 0:nqs], op=ALU.mult),
                                                  r=[SPt, masksb], w=[SPt])
                                            kb.op(dve, lambda h: h.tensor_tensor(out=E[0:ks, sl], in0=E[0:ks, sl],
                                                                                 in1=masksb[0:ks, 0:nqs], op=ALU.mult),
                                                  r=[E, masksb], w=[E])
                                        elif t == "skip":
                                            kb.op(dve, lambda h: h.memset(SPt[0:ks, sl], 0.0), w=[SPt])
                                            kb.op(dve, lambda h: h.memset(E[0:ks, sl], 0.0), w=[E])

                            rstate = {"prev": None, "n": 0}

                            def stageB(j):
                                if mode != "sb":
                                    return
                                d_ = st_[j]
                                ks, P, E, SPt, wi = d_["ks"], d_["P"], d_["T"], d_["G"], d_["wi"]
                                cp = kb.ps()
                                mmsl = [(cp[0:ks, 0:nq], tribb[0:ks, 0:ks], SPt[0:ks, 0:nq])]
                                rprev = rstate["prev"]
                                rd = [tribb, onesb, SPt]
                                if rprev is not None:
                                    mmsl.append((cp[0:ks, 0:nq], onesb[:, 0:ks], rprev[1][:, 0:nq]))
                                    rd.append(rprev[1])
                                kb.mm(cp, mmsl, r=rd)
                                X = tx[wi % NR]
                                kb.op(act, lambda h: h.activation(out=X[0:ks, 0:nq], in_=cp[0:ks, 0:nq], func=AF.Exp,
                                                                  scale=-1.0), r=[cp], w=[X])
                                kb.op(dve, lambda h: h.tensor_tensor(out=P[0:ks, 0:nq], in0=E[0:ks, 0:nq], in1=X[0:ks, 0:nq],
                                                                     op=ALU.mult), r=[E, X], w=[P])
                                rn = raccs[rstate["n"] % len(raccs)]
                                rnb = raccb[rstate["n"] % len(raccb)]
                                rstate["n"] += 1
                                if rprev is None:
                                    if ks < 128:
                                        kb.op(dve, lambda h: h.memset(rn[:], 0.0), w=[rn])
                                    kb.op(dve, lambda h: h.tensor_copy(out=rn[0:ks, 0:nq], in_=SPt[0:ks, 0:nq]), r=[SPt], w=[rn])
                                else:
                                    kb.op(dve, lambda h: h.tensor_tensor(out=rn[:, 0:nq], in0=rprev[0][:, 0:nq], in1=SPt[:, 0:nq],
                                                                         op=ALU.add), r=[rprev[0], SPt], w=[rn])
                                kb.op(pool, lambda h: h.tensor_copy(out=rnb[:, 0:nq], in_=rn[:, 0:nq]), r=[rn], w=[rnb])
                                rstate["prev"] = (rn, rnb)

                            def stageC(j):
                                d_ = st_[j]
                                ks, types, P = d_["ks"], d_["types"], d_["P"]
                                ncol = dv + (0 if mode == "sb" else 1)
                                for si, (qb, nqs) in enumerate(subs):
                                    if types[si] == "skip":
                                        continue
                                    for c in range(ncomp):
                                        a = accs[(c, si)]
                                        first = (c, si) not in started
                                        started.add((c, si))
                                        last = (lastj[si] == j)
                                        kb._wait(pe, kb._deps([P, vs], [a] if first else []))
                                        ins = pe.h.matmul(a[0:nqs, 0:ncol], P[0:ks, c * 256 + si * 128:c * 256 + si * 128 + nqs],
                                                          vs[0:ks, j, 256 - dv:256 - dv + ncol], start=first, stop=last)
                                        pe.sem.count += 1
                                        ins.then_inc(pe.sem.h, 1)
                                        ev = (pe.sem, pe.sem.count)
                                        kb._record(ev, [P, vs], [a])

                            for idx in range(npairs + 2):
                                if idx < npairs:
                                    stageA(live[idx])
                                if 0 <= idx - 1 < npairs:
                                    stageB(live[idx - 1])
                                if 0 <= idx - 2 < npairs:
                                    stageC(live[idx - 2])
                            for si, (qb, nqs) in enumerate(subs):
                                oi = cnt["o"]
                                cnt["o"] += 1
                                o = osb[oi % 2]
                                s_ = sm[oi % 2]
                                ob = obf[oi % 2]
                                ot = oT[oi % 2]
                                if mode == "sb":
                                    kb.op(act, lambda h: h.activation(out=ob[0:nqs, 0:dv], in_=accs[(0, si)][0:nqs, 0:dv],
                                                                      func=AF.Copy), r=[accs[(0, si)]], w=[ob])
                                elif mode == "mem":
                                    a0 = accs[(0, si)]
                                    kb.op(dve, lambda h: h.reciprocal(out=s_[0:nqs, 0:1], in_=a0[0:nqs, dv:dv + 1]), r=[a0],
                                          w=[s_])
                                    kb.op(act, lambda h: h.activation(out=ob[0:nqs, 0:dv], in_=a0[0:nqs, 0:dv], func=AF.Copy,
                                                                      scale=s_[0:nqs, 0:1]), r=[a0, s_], w=[ob])
                                else:
                                    a0, a1 = accs[(0, si)], accs[(1, si)]
                                    kb.op(dve, lambda h: h.reciprocal(out=s_[0:nqs, 0:1], in_=a0[0:nqs, dv:dv + 1]), r=[a0],
                                          w=[s_])
                                    kb.op(dve, lambda h: h.reciprocal(out=s_[0:nqs, 1:2], in_=a1[0:nqs, dv:dv + 1]), r=[a1],
                                          w=[s_])
                                    kb.op(dve, lambda h: h.tensor_tensor(out=s_[0:nqs, 1:2], in0=s_[0:nqs, 1:2],
                                                                         in1=lam_t[0:nqs, 1:2], op=ALU.mult), r=[s_, lam_t],
                                          w=[s_])
                                    kb.op(act, lambda h: h.activation(out=o[0:nqs, 0:dv], in_=a0[0:nqs, 0:dv], func=AF.Copy,
                                                                      scale=s_[0:nqs, 0:1]), r=[a0, s_], w=[o])
                                    kb.op(dve, lambda h: h.scalar_tensor_tensor(out=o[0:nqs, 0:dv], in0=a1[0:nqs, 0:dv],
                                                                                scalar=s_[0:nqs, 1:2], in1=o[0:nqs, 0:dv],
                                                                                op0=ALU.mult, op1=ALU.add), r=[a1, s_, o], w=[o])
                                    kb.op(act, lambda h: h.activation(out=osq[0:nqs, 0:dv], in_=o[0:nqs, 0:dv], func=AF.Square,
                                                                      accum_out=s_[0:nqs, 2:3]), r=[o], w=[osq, s_])
                                    kb.op(dve, lambda h: h.tensor_scalar(out=s_[0:nqs, 3:4], in0=s_[0:nqs, 2:3],
                                                                         scalar1=1.0 / dv, scalar2=LN_EPS, op0=ALU.mult,
                                                                         op1=ALU.add), r=[s_], w=[s_])
                                    kb.op(act, lambda h: h.activation(out=s_[0:nqs, 3:4], in_=s_[0:nqs, 3:4], func=AF.Sqrt),
                                          r=[s_], w=[s_])
                                    kb.op(dve, lambda h: h.reciprocal(out=s_[0:nqs, 4:5], in_=s_[0:nqs, 3:4]), r=[s_], w=[s_])
                                    kb.op(dve, lambda h: h.scalar_tensor_tensor(out=ob[0:nqs, 0:dv], in0=o[0:nqs, 0:dv],
                                                                                scalar=s_[0:nqs, 4:5], in1=subg_bc[0:nqs, 0:dv],
                                                                                op0=ALU.mult, op1=ALU.mult),
                                          r=[o, s_, subg_bc], w=[ob])
                                nch = dv // 128
                                pp = kb.ps()
                                ppb = pp[:, 0:256].bitcast(BF16)
                                for c in range(nch):
                                    kb.transpose(pp, ppb[:, c * 128:c * 128 + nqs], ob[0:nqs, c * 128:(c + 1) * 128],
                                                 identb[0:nqs, 0:nqs], r=[ob, identb])
                                kb.op(dve, lambda h: h.tensor_copy(out=ot[:, 0:nch, 0:nqs],
                                                                   in_=ppb[:, 0:nch * 128].rearrange("p (c t) -> p c t", c=nch)[:, :, 0:nqs]),
                                      r=[pp], w=[ot])
                                tq = tok0 + qb * 128 if nq_tot > 128 else tok0
                                kb.dma(sp, MIXT[mix_row0:mix_row0 + dv, tq:tq + nqs].rearrange("(c p) t -> p c t", p=128),
                                       ot[:, 0:nch, 0:nqs], ot, r=[ot], ndesc=128 * nch)
                            for a_ in accs.values():
                                kb.unreserve(a_)

                    return do_loads, do_compute

                jobs = []
                nblk = SEQ // 128
                npb = PAST // 128
                tokm = lambda ap: ap.rearrange("(j p) d -> p j d", p=128)

                def cls_diff_p(j, qb):
                    return "skip" if j > qb else ("diag" if j == qb else ("prev" if j == qb - 1 else "far"))

                def cls_sb_p(j, qb):
                    return "skip" if j > qb else ("diag" if j == qb else "far")

                def cls_diff_s(j, qb):
                    return "diag" if j == npb else ("prev" if j == npb - 1 else "far")

                def cls_sb_s(j, qb):
                    return "diag" if j == npb else "far"

                cls_far = lambda j, qb: "far"
                for s in range(NSEQ):
                    tok0 = s * SEQ
                    ts_ = slice(tok0, tok0 + SEQ)
                    if kind == 0:
                        for hh in range(6):
                            lds = []
                            for c in range(2):
                                r0 = hh * 256 + c * 128
                                lds.append((False, "q", lambda b, c=c: b[:, c, 0:SEQ], QT[r0:r0 + 128, ts_], 128))
                                lds.append((False, "k", lambda b, c=c: b[:, c, 0:SEQ], KT[r0:r0 + 128, ts_], 128))
                            lds.append((False, "v", lambda b: b[:, 0:nblk, 0:256], tokm(VB[ts_, hh * 256:(hh + 1) * 256]),
                                        128 * nblk))
                            jobs.append(run_head(2, 256, SEQ, 256, [128] * nblk, cls_diff_p, "diff", hh, hh * 256, tok0, lds))
                    else:
                        for hh in range(12):
                            r0 = hh * 128
                            lds = [(False, "q", lambda b: b[:, 0, 0:SEQ], QT[r0:r0 + 128, ts_], 128),
                                   (False, "k", lambda b: b[:, 0, 0:SEQ], KT[r0:r0 + 128, ts_], 128),
                                   (False, "v", lambda b: b[:, 0:nblk, 128:256], tokm(VB[ts_, r0:r0 + 128]), 128 * nblk)]
                            jobs.append(run_head(1, 128, SEQ, 256, [128] * nblk, cls_sb_p, "sb", 0, r0, tok0, lds))
                    for hh in range(4):
                        r0 = hh * 128
                        ms_ = slice(s * NMEM, (s + 1) * NMEM)
                        lds = [(False, "q", lambda b: b[:, 0, 0:SEQ], MQT[r0:r0 + 128, ts_], 128),
                               (False, "k", lambda b: b[:, 0, 0:NMEM], MKT[r0:r0 + 128, ms_], 128),
                               (False, "v", lambda b: b[:, 0:NMEM // 128, 128:256], tokm(MVB[ms_, r0:r0 + 128]), 256)]
                        jobs.append(run_head(1, 128, SEQ, 256, [128] * (NMEM // 128), cls_far, "mem", 0, SW + r0, tok0, lds))
                for s in range(NSMP):
                    tok0 = cfg.NP + s * DEC_SEQ
                    ts_ = slice(tok0, tok0 + DEC_SEQ)
                    kss_s = [128] * npb + [DEC_SEQ]
                    if kind == 0:
                        for hh in range(6):
                            lds = []
                            for c in range(2):
                                r0 = hh * 256 + c * 128
                                lds.append((False, "q", lambda b, c=c: b[:, c, 0:DEC_SEQ], QT[r0:r0 + 128, ts_], 128))
                                lds.append((False, "k", lambda b, c=c: b[:, c, 0:PAST], KTS[s, r0:r0 + 128, :], 128))
                                lds.append((False, "k", lambda b, c=c: b[:, c, PAST:PAST + DEC_SEQ], KT[r0:r0 + 128, ts_], 128))
                            lds.append((True, "v", lambda b: b[:, 0:npb, 0:256], tokm(cv[l, s, :, hh * 256:(hh + 1) * 256]),
                                        128 * npb))
                            lds.append((False, "v", lambda b: b[0:DEC_SEQ, npb, 0:256], VB[ts_, hh * 256:(hh + 1) * 256], 32))
                            jobs.append(run_head(2, 256, DEC_SEQ, 256, kss_s, cls_diff_s, "diff", hh, hh * 256, tok0, lds))
                    else:
                        for hh in range(12):
                            r0 = hh * 128
                            lds = [(False, "q", lambda b: b[:, 0, 0:DEC_SEQ], QT[r0:r0 + 128, ts_], 128),
                                   (False, "k", lambda b: b[:, 0, 0:PAST], KTS[s, r0:r0 + 128, :], 128),
                                   (False, "k", lambda b: b[:, 0, PAST:PAST + DEC_SEQ], KT[r0:r0 + 128, ts_], 128),
                                   (True, "v", lambda b: b[:, 0:npb, 128:256], tokm(cv[l, s, :, r0:r0 + 128]), 128 * npb),
                                   (False, "v", lambda b: b[0:DEC_SEQ, npb, 128:256], VB[ts_, r0:r0 + 128], 32)]
                            jobs.append(run_head(1, 128, DEC_SEQ, 256, kss_s, cls_sb_s, "sb", 0, r0, tok0, lds))
                    for hh in range(4):
                        r0 = hh * 128
                        lds = [(False, "q", lambda b: b[:, 0, 0:DEC_SEQ], MQT[r0:r0 + 128, ts_], 128),
                               (False, "k", lambda b: b[:, 0, 0:NMEM], MKTS[s, r0:r0 + 128, :], 128),
                               (True, "v", lambda b: b[:, 0:NMEM // 128, 128:256], tokm(cmv[l, s, :, r0:r0 + 128]), 256)]
                        jobs.append(run_head(1, 128, DEC_SEQ, 256, [128] * (NMEM // 128), cls_far, "mem", 0, SW + r0, tok0, lds))
                jobs[0][0]()
                for ji, (ld_, cp_) in enumerate(jobs):
                    if ji + 1 < len(jobs):
                        jobs[ji + 1][0]()
                    cp_()
                kb.barrier()
                kb.release(qTs + kTs + vss + tf + tg + tx + pb + osb + obf + oT + sm + raccs + raccb + tgb + [osq])

        def out_proj(l, src_tok):
            with ExitStack() as ph:
                rr = [0]
                wo = kb.sb(ph, "wo", [128, 16, D], BF16)
                wov = w_o[l].rearrange("(k p) f -> p k f", p=128)
                for k0 in range(0, 16, 2):
                    kb.dma(pool, wo[:, k0:k0 + 2, :], wov[:, k0:k0 + 2, :], wo, w=[wo], ndesc=256)
                g, b = load_ln_params(ph, l * 2 + 0)
                mts = [kb.sb(ph, f"c_mT{i}", [128, 16, 128], BF16) for i in range(2)]
                xs = [kb.sb(ph, f"c_x{i}", [128, D], F32) for i in range(2)]
                rs = [kb.sb(ph, f"c_r{i}", [128, D], F32) for i in range(2)]
                hs = [kb.sb(ph, f"c_h{i}", [128, D], F32) for i in range(2)]
                hb = [kb.sb(ph, f"c_hb{i}", [128, D], BF16) for i in range(2)]
                hT = [kb.sb(ph, f"c_hT{i}", [128, KD, 128], BF16) for i in range(2)]
                stat = [kb.sb(ph, f"c_st{i}", [128, 8], F32) for i in range(2)]
                for t in range(NT):
                    mT, x, r, hsb, hbb, hTt, stt = mts[t % 2], xs[t % 2], rs[t % 2], hs[t % 2], hb[t % 2], hT[t % 2], stat[t % 2]
                    rows = slice(t * 128, (t + 1) * 128)
                    kb.dma(sp, mT[:], MIXT[:, rows].rearrange("(c p) t -> p c t", p=128), mT, w=[mT], ndesc=2048)
                    kb.dma(sp, x[:], src_tok[rows, :], x, w=[x], ndesc=128)
                    for f0 in range(0, D, 512):
                        pp = kb.ps()
                        kb.mm(pp, [(pp[:, :], mT[:, k, :], wo[:, k, f0:f0 + 512]) for k in range(16)], r=[mT, wo])
                        kb.op(dve, lambda h: h.scalar_tensor_tensor(out=r[:, f0:f0 + 512], in0=x[:, f0:f0 + 512],
                                                                    scalar=cfg.alpha, in1=pp[:, :], op0=ALU.mult,
                                                                    op1=ALU.add), r=[x, pp], w=[r])
                    layer_norm(r, hsb, g, b, stt)
                    kb.dma(pool, H1[rows, :], hsb[:], hsb, r=[hsb], ndesc=128)
                    kb.op(act, lambda h: h.activation(out=hbb[:], in_=hsb[:], func=AF.Copy), r=[hsb], w=[hbb])
                    transpose_to_T(hbb, hTt, 0, KD, rr)
                    kb.dma(pool, H1T[:, rows].rearrange("(c p) t -> p c t", p=128), hTt[:], hTt, r=[hTt], ndesc=128 * KD)
                kb.barrier()
                kb.release([wo, g, b] + mts + xs + rs + hs + hb + hT + stat)

        def peer_retrieve(l, IDX1T, IDX2T, GT):
            with ExitStack() as ph:
                wq = kb.sb(ph, "wq", [128, KD, 2048], BF16)
                wqv = w_q[l].rearrange("(k p) f -> p k f", p=128)
                for k0 in range(0, KD, 2):
                    kb.dma(pool, wq[:, k0:k0 + 2, :], wqv[:, k0:k0 + 2, :], wq, w=[wq], ndesc=256)
                sk = kb.sb(ph, "sk", [128, 16, 128], BF16)
                kb.dma(pool, sk[:], subk[l].rearrange("g d n -> d g n"), sk, w=[sk], ndesc=2048)
                hTs = [kb.sb(ph, f"d_hT{i}", [128, KD, 128], BF16) for i in range(2)]
                qp = [kb.sb(ph, f"d_qp{i}", [128, 16, 128], BF16) for i in range(2)]
                sc = [kb.sb(ph, f"d_sc{i}", [128, 2048], F32) for i in range(2)]
                sc2 = kb.sb(ph, "d_sc2", [128, 2048], F32)
                sv = kb.sb(ph, "d_sv", [128, 16, 16], F32)
                si_u = kb.sb(ph, "d_siu", [128, 16, 16], U32)
                si_f = kb.sb(ph, "d_sif", [128, 16, 16], F32)
                cand = kb.sb(ph, "d_cand", [128, 8, 256], F32)
                cand2 = kb.sb(ph, "d_cand2", [128, 8, 256], F32)
                best = kb.sb(ph, "d_best", [128, 8, 16], F32)
                pos_u = kb.sb(ph, "d_posu", [128, 8, 16], U32)
                pa_u = kb.sb(ph, "d_pau", [128, 8, 16], U32)
                pb_u = kb.sb(ph, "d_pbu", [128, 8, 16], U32)
                pa_f = kb.sb(ph, "d_paf", [128, 8, 16], F32)
                pb_f = kb.sb(ph, "d_pbf", [128, 8, 16], F32)
                oh = kb.sb(ph, "d_oh", [128, 8, 16, 16], F32)
                outs = [kb.sb(ph, f"d_out{i}", [128, 3, 128], F32) for i in range(2)]
                gs = kb.sb(ph, "d_gs", [128, 8], F32)
                v_sv = [Buf(None, f"vsv{i}") for i in range(16)]
                v_si = [Buf(None, f"vsi{i}") for i in range(16)]
                v_sc2 = [Buf(None, f"vsc2{i}") for i in range(16)]
                v_best = [Buf(None, f"vb{i}") for i in range(8)]
                v_pos = [Buf(None, f"vp{i}") for i in range(8)]
                v_c2 = [Buf(None, f"vc2{i}") for i in range(8)]
                for t in range(NT):
                    hTt, qpt, sct, ot = hTs[t % 2], qp[t % 2], sc[t % 2], outs[t % 2]
                    rows = slice(t * 128, (t + 1) * 128)
                    kb.dma(sp, hTt[:], H1T[:, rows].rearrange("(c p) t -> p c t", p=128), hTt, w=[hTt], ndesc=128 * KD)
                    for g0 in range(0, 16, 4):
                        pp = kb.ps()
                        kb.mm_multi([pp], [[(pp[:, gg * 128:(gg + 1) * 128], wq[:, k, (g0 + gg) * 128:(g0 + gg + 1) * 128],
                                             hTt[:, k, :]) for k in range(KD)] for gg in range(4)], r=[wq, hTt])
                        kb.op(act, lambda h: h.activation(out=qpt[:, g0:g0 + 4, :],
                                                          in_=pp[:, :].rearrange("p (g t) -> p g t", g=4), func=AF.Copy),
                              r=[pp], w=[qpt])
                    for g0 in range(0, 16, 4):
                        pp = kb.ps()
                        kb.mm_multi([pp], [[(pp[:, gg * 128:(gg + 1) * 128], qpt[:, g0 + gg, :], sk[:, g0 + gg, :])]
                                           for gg in range(4)], r=[qpt, sk])
                        kb.op(act, lambda h: h.activation(out=sct[:, g0 * 128:(g0 + 4) * 128], in_=pp[:, :], func=AF.Copy),
                              r=[pp], w=[sct])
                    for stg in range(5):
                        for gI in range(16):
                            seg = sct[:, gI * 128:(gI + 1) * 128]
                            seg2 = sc2[:, gI * 128:(gI + 1) * 128]
                            vsv, vsi, vs2 = v_sv[gI], v_si[gI], v_sc2[gI]
                            if stg == 0:
                                kb.op(dve, lambda h: h.max(out=sv[:, gI, 0:8], in_=seg), r=[sct], w=[vsv])
                            elif stg == 1:
                                kb.op(dve, lambda h: h.max_index(out=si_u[:, gI, 0:8], in_max=sv[:, gI, 0:8], in_values=seg),
                                      r=[sct, vsv], w=[vsi])
                            elif stg == 2:
                                kb.op(dve, lambda h: h.match_replace(out=seg2, in_to_replace=sv[:, gI, 0:8], in_values=seg,
                                                                     imm_value=-1e30), r=[sct, vsv], w=[vs2])
                            elif stg == 3:
                                kb.op(dve, lambda h: h.max(out=sv[:, gI, 8:16], in_=seg2), r=[vs2], w=[vsv])
                            else:
                                kb.op(dve, lambda h: h.max_index(out=si_u[:, gI, 8:16], in_max=sv[:, gI, 8:16],
                                                                 in_values=seg2), r=[vs2, vsv], w=[vsi])
                    kb.op(dve, lambda h: h.tensor_copy(out=si_f[:], in_=si_u[:]), r=v_si, w=[si_f])
                    sv4 = sv[:].rearrange("p (h c) k -> p h c k", c=2)
                    si4 = si_f[:].rearrange("p (h c) k -> p h c k", c=2)
                    kb.op(dve, lambda h: h.tensor_tensor(out=cand[:].rearrange("p h (a b) -> p h a b", a=16),
                                                         in0=sv4[:, :, 0, :].unsqueeze(3).to_broadcast([128, 8, 16, 16]),
                                                         in1=sv4[:, :, 1, :].unsqueeze(2).to_broadcast([128, 8, 16, 16]),
                                                         op=ALU.add), r=v_sv, w=[cand])
                    for stg in range(5):
                        for hh in range(8):
                            vb, vp, vc2 = v_best[hh], v_pos[hh], v_c2[hh]
                            if stg == 0:
                                kb.op(dve, lambda h: h.max(out=best[:, hh, 0:8], in_=cand[:, hh, :]), r=[cand], w=[vb])
                            elif stg == 1:
                                kb.op(dve, lambda h: h.max_index(out=pos_u[:, hh, 0:8], in_max=best[:, hh, 0:8],
                                                                 in_values=cand[:, hh, :]), r=[cand, vb], w=[vp])
                            elif stg == 2:
                                kb.op(dve, lambda h: h.match_replace(out=cand2[:, hh, :], in_to_replace=best[:, hh, 0:8],
                                                                     in_values=cand[:, hh, :], imm_value=-1e30),
                                      r=[cand, vb], w=[vc2])
                            elif stg == 3:
                                kb.op(dve, lambda h: h.max(out=best[:, hh, 8:16], in_=cand2[:, hh, :]), r=[vc2], w=[vb])
                            else:
                                kb.op(dve, lambda h: h.max_index(out=pos_u[:, hh, 8:16], in_max=best[:, hh, 8:16],
                                                                 in_values=cand2[:, hh, :]), r=[vc2, vb], w=[vp])
                    kb.op(dve, lambda h: h.tensor_single_scalar(out=pa_u[:], in_=pos_u[:], scalar=4,
                                                                op=ALU.logical_shift_right), r=v_pos, w=[pa_u])
                    kb.op(dve, lambda h: h.tensor_single_scalar(out=pb_u[:], in_=pos_u[:], scalar=15, op=ALU.bitwise_and),
                          r=v_pos, w=[pb_u])
                    kb.op(dve, lambda h: h.tensor_copy(out=pa_f[:], in_=pa_u[:]), r=[pa_u], w=[pa_f])
                    kb.op(dve, lambda h: h.tensor_copy(out=pb_f[:], in_=pb_u[:]), r=[pb_u], w=[pb_f])
                    for which, pf in ((0, pa_f), (1, pb_f)):
                        kb.op(dve, lambda h: h.tensor_tensor(out=oh[:], in0=pf[:].unsqueeze(3).to_broadcast([128, 8, 16, 16]),
                                                             in1=iota[:, 0:16].unsqueeze(1).unsqueeze(1).to_broadcast([128, 8, 16, 16]),
                                                             op=ALU.is_equal), r=[pf, iota], w=[oh])
                        kb.op(dve, lambda h: h.tensor_tensor(out=oh[:], in0=oh[:],
                                                             in1=si4[:, :, which, :].unsqueeze(2).to_broadcast([128, 8, 16, 16]),
                                                             op=ALU.mult), r=[oh, si_f], w=[oh])
                        kb.op(dve, lambda h: h.tensor_reduce(out=ot[:, which, :].rearrange("p (h k) -> p h k", h=8),
                                                             in_=oh[:], axis=AX.X, op=ALU.add), r=[oh], w=[ot])
                    gv = ot[:, 2, :].rearrange("p (h k) -> p h k", h=8)
                    kb.op(dve, lambda h: h.tensor_tensor(out=gv, in0=best[:], in1=best[:, :, 0:1].to_broadcast([128, 8, 16]),
                                                         op=ALU.subtract), r=v_best, w=[ot])
                    kb.op(act, lambda h: h.activation(out=gv, in_=gv, func=AF.Exp), r=[ot], w=[ot])
                    kb.op(dve, lambda h: h.tensor_reduce(out=gs[:], in_=gv, axis=AX.X, op=ALU.add), r=[ot], w=[gs])
                    kb.op(dve, lambda h: h.reciprocal(out=gs[:], in_=gs[:]), r=[gs], w=[gs])
                    kb.op(dve, lambda h: h.tensor_tensor(out=gv, in0=gv, in1=gs[:].unsqueeze(2).to_broadcast([128, 8, 16]),
                                                         op=ALU.mult), r=[ot, gs], w=[ot])
                    pp = kb.ps()
                    for w3 in range(3):
                        kb.transpose(pp, pp[:, w3 * 128:(w3 + 1) * 128], ot[:, w3, :], identf[:], r=[ot, identf])
                    for w3, dstb in enumerate((IDX1T, IDX2T, GT)):
                        kb.op(act, lambda h: h.activation(out=dstb[:, rows], in_=pp[:, w3 * 128:(w3 + 1) * 128], func=AF.Copy),
                              r=[pp], w=[dstb])
                kb.barrier()
                kb.release([wq, sk] + hTs)

        def peer_wgen(l, IDX1T, IDX2T, GT):
            with ExitStack() as ph:
                Ps = [kb.sb(ph, f"e_P{i}", [128, 64, 128], BF16) for i in range(2)]
                Qs = [kb.sb(ph, f"e_Q{i}", [128, 64, 128], BF16) for i in range(2)]
                Ws = [kb.sb(ph, f"e_W{i}", [128, 128, 128], BF16) for i in range(2)]
                hcnt = 0
                for t in range(NT):
                    W = Ws[t % 2]
                    for hf in range(2):
                        P, Q = Ps[hcnt % 2], Qs[hcnt % 2]
                        hcnt += 1
                        rows = slice(t * 128 + hf * 64, t * 128 + hf * 64 + 64)
                        io3 = iotab[:].unsqueeze(1).to_broadcast([128, 64, 128])
                        kb.op(dve, lambda h: h.tensor_tensor(out=Q[:], in0=io3,
                                                             in1=IDX2T[:, rows].unsqueeze(2).to_broadcast([128, 64, 128]),
                                                             op=ALU.is_equal), r=[iotab, IDX2T], w=[Q])
                        kb.op(dve, lambda h: h.tensor_tensor(out=P[:], in0=io3,
                                                             in1=IDX1T[:, rows].unsqueeze(2).to_broadcast([128, 64, 128]),
                                                             op=ALU.is_equal), r=[iotab, IDX1T], w=[P])
                        kb.op(pool, lambda h: h.tensor_tensor(out=P[:], in0=P[:],
                                                              in1=GT[:, rows].unsqueeze(2).to_broadcast([128, 64, 128]),
                                                              op=ALU.mult), r=[P, GT], w=[P])
                        for t4 in range(0, 64, 4):
                            pp = kb.ps()
                            kb.mm_multi([pp], [[(pp[:, j * 128:(j + 1) * 128], Q[:, t4 + j, :], P[:, t4 + j, :])] for j in range(4)],
                                        r=[P, Q])
                            src = pp[:, :].rearrange("p (t i) -> p i t", t=4)
                            tw = hf * 64 + t4
                            kb.op(act, lambda h: h.activation(out=W[:, :, tw:tw + 4], in_=src, func=AF.Copy), r=[pp], w=[W])
                    for c0 in range(0, 128, 32):
                        kb.dma(sp, WT[t, :, c0:c0 + 32, :], W[:, c0:c0 + 32, :], W, r=[W], ndesc=128)
                kb.barrier()
                kb.release(Ps + Qs + Ws)

        def peer_dense(l, dst_tok, final):
            SC = 4
            nsc = 128 // SC
            utv = ut[l].rearrange("(k p) e -> p k e", p=128)
            vvv = vv[l].rearrange("(c p) d -> p c d", p=128)
            maxpass = 6
            passes = []
            t = 0
            while t < NT:
                n = min(maxpass, NT - t)
                passes.append((t, n))
                t += n
            with ExitStack() as ph:
                rr = [0]
                g, b = load_ln_params(ph, l * 2 + 1)
                acc = kb.sb(ph, "f_acc", [128, maxpass, D], F32)
                hTp = kb.sb(ph, "f_hT", [128, KD, maxpass * 128], BF16)
                us = [kb.sb(ph, f"f_u{i}", [128, KD, SC * 128], BF16) for i in range(2)]
                vs_ = [kb.sb(ph, f"f_v{i}", [128, SC, D], BF16) for i in range(2)]
                gsb = [kb.sb(ph, f"f_g{i}", [128, SC, 512], BF16) for i in range(3)]
                wts = [kb.sb(ph, f"f_w{i}", [128, SC, 512], BF16) for i in range(3)]
                xs = [kb.sb(ph, f"f_x{i}", [128, D], F32) for i in range(1)]
                ys = [kb.sb(ph, f"f_y{i}", [128, D], F32) for i in range(1)]
                stat = [kb.sb(ph, f"f_st{i}", [128, 8], F32) for i in range(2)]
                if os.environ.get('MK_DEBUG'):
                    print('dense sbuf remaining', nc.sbuf_bytes_remaining)
                ui = 0
                gi = 0
                for (pt0, pn) in passes:
                    ntp = pn * 128
                    tok0 = pt0 * 128
                    kb.dma(sp, hTp[:, :, 0:ntp], H1T[:, tok0:tok0 + ntp].rearrange("(c p) t -> p c t", p=128), hTp, w=[hTp],
                           ndesc=128 * KD)
                    steps = [(sc_i, g0) for sc_i in range(nsc) for g0 in range(0, pn, 4)]
                    uv = {}
                    ctx = {}

                    def stA(step):
                        nonlocal ui, gi
                        sc_i, g0 = step
                        if g0 == 0:
                            u, v_ = us[ui % 2], vs_[ui % 2]
                            ui += 1
                            uv[sc_i] = (u, v_)
                            e0 = sc_i * SC * 128
                            kq = max(1, KD // 4)
                            for k0 in range(0, KD, kq):
                                kb.dma(pool, u[:, k0:k0 + kq, :], utv[:, k0:k0 + kq, e0:e0 + SC * 128], u, w=[u], ndesc=128 * kq)
                            for c in range(SC):
                                kb.dma(pool, v_[:, c, :], vvv[:, sc_i * SC + c, :], v_, w=[v_], ndesc=128)
                        u, v_ = uv[sc_i]
                        ng = min(4, pn - g0)
                        ntk = ng * 128
                        G, Wt = gsb[gi % 3], wts[gi % 3]
                        gi += 1
                        for tt in range(ng):
                            kb.dma(sp, Wt[:, :, tt * 128:(tt + 1) * 128], WT[pt0 + g0 + tt, :, sc_i * SC:(sc_i + 1) * SC, :],
                                   Wt, w=[Wt], ndesc=128)
                        for c in range(SC):
                            pp = kb.ps()
                            kb.mm(pp, [(pp[:, 0:ntk], u[:, k, c * 128:(c + 1) * 128], hTp[:, k, g0 * 128:g0 * 128 + ntk])
                                       for k in range(KD)], r=[u, hTp])
                            kb.op(act, lambda h: h.activation(out=G[:, c, 0:ntk], in_=pp[:, 0:ntk], func=AF.Gelu), r=[pp],
                                  w=[G])
                        kb.op(dve, lambda h: h.tensor_tensor(out=G[:, :, 0:ntk], in0=G[:, :, 0:ntk], in1=Wt[:, :, 0:ntk],
                                                             op=ALU.mult), r=[G, Wt], w=[G])
                        ctx[step] = (G, ng, v_)

                    def stB(step):
                        sc_i, g0 = step
                        H, ng, v_ = ctx.pop(step)
                        for tt in range(ng):
                            ti = g0 + tt
                            for f0 in range(0, D, 512):
                                pp = kb.ps()
                                kb.mm(pp, [(pp[:, :], H[:, c, tt * 128:(tt + 1) * 128], v_[:, c, f0:f0 + 512])
                                           for c in range(SC)], r=[H, v_])
                                if sc_i == 0:
                                    kb.op(dve, lambda h: h.tensor_copy(out=acc[:, ti, f0:f0 + 512], in_=pp[:, :]), r=[pp],
                                          w=[acc])
                                else:
                                    kb.op(dve, lambda h: h.tensor_tensor(out=acc[:, ti, f0:f0 + 512],
                                                                         in0=acc[:, ti, f0:f0 + 512], in1=pp[:, :],
                                                                         op=ALU.add), r=[pp, acc], w=[acc])

                    for i_ in range(len(steps) + 1):
                        if i_ < len(steps):
                            stA(steps[i_])
                        if i_ >= 1:
                            stB(steps[i_ - 1])
                    for ti in range(pn):
                        x, yb, stt = xs[0], ys[0], stat[ti % 2]
                        rows = slice(tok0 + ti * 128, tok0 + (ti + 1) * 128)
                        kb.dma(sp, x[:], H1[rows, :], x, w=[x], ndesc=128)
                        kb.op(dve, lambda h: h.scalar_tensor_tensor(out=x[:], in0=x[:], scalar=cfg.alpha, in1=acc[:, ti, :],
                                                                    op0=ALU.mult, op1=ALU.add), r=[x, acc], w=[x])
                        layer_norm(x, yb, g, b, stt)
                        kb.dma(sp, dst_tok[rows, :], yb[:], yb, r=[yb], ndesc=128)
                kb.barrier()
                kb.release([g, b, acc, hTp] + us + vs_ + gsb + wts + xs + ys + stat)

        stop = cfg.stop
        for l in range(L if stop != ("setup", 0) else 0):
            src_tok = xin if l == 0 else H2
            dst_tok = y if l == L - 1 else H2
            fm = [(0, 12, QT), (SW, 12, KT), (3 * SW, 4, MQT)]
            tm = [(SW, SW, nk[l], None), (2 * SW, SW, nv[l], VB)]
            project(src_tok, NT, w_in[l], 5120, fm, tm, KD, "pa")
            project(memp, NMT // 128, w_mkv[l], 1024, [(0, 4, MKT)], [(0, 512, nmk[l], None), (512, 512, nmv[l], MVB)], KD,
                    "pm")
            if stop == ("proj", l):
                break
            cache_transposes(l)
            attention(l)
            if stop == ("attn", l):
                break
            out_proj(l, src_tok)
            if stop == ("oproj", l):
                break
            with ExitStack() as pph:
                IDX1T = kb.sb(pph, "IDX1T", [128, NTOK], BF16)
                IDX2T = kb.sb(pph, "IDX2T", [128, NTOK], BF16)
                GT = kb.sb(pph, "GT", [128, NTOK], BF16)
                peer_retrieve(l, IDX1T, IDX2T, GT)
                peer_wgen(l, IDX1T, IDX2T, GT)
            if stop == ("wgen", l):
                break
            peer_dense(l, dst_tok, l == L - 1)
            if stop == ("dense", l):
                break
        kb.barrier()
    return nc


def shard_inputs(cfg, inputs, n_cores):
    c = make_consts()
    maps = []
    L = cfg.DEPTH
    u_t = np.ascontiguousarray(np.transpose(inputs["peer_u"], (0, 2, 1)))
    subk = np.ascontiguousarray(np.transpose(inputs["peer_sub_keys"], (0, 1, 2, 4, 3))).reshape(L, 16, 128, 128)
    w_mkv = np.ascontiguousarray(np.concatenate([inputs["w_mem_k"], inputs["w_mem_v"]], axis=2))
    for ci in range(n_cores):
        ps = slice(ci * cfg.NSEQ, (ci + 1) * cfg.NSEQ)
        ss = slice(ci * cfg.NSMP, (ci + 1) * cfg.NSMP)
        xin = np.concatenate([inputs["x_prompt"][ps].reshape(-1, cfg.D), inputs["x_sample"][ss].reshape(-1, cfg.D)], axis=0)
        m = {
            "xin": np.ascontiguousarray(xin),
            "w_in": inputs["w_in"], "w_o": inputs["w_o"], "w_mkv": w_mkv, "w_q": inputs["peer_w_q"],
            "subk": subk, "ut": u_t, "vv": inputs["peer_v"],
            "ck": np.ascontiguousarray(inputs["cache_self_k"][:, ss]),
            "cv": np.ascontiguousarray(inputs["cache_self_v"][:, ss]),
            "cmk": np.ascontiguousarray(inputs["cache_mem_k"][:, ss]).reshape(L, cfg.NSMP, NMEM, 512),
            "cmv": np.ascontiguousarray(inputs["cache_mem_v"][:, ss]).reshape(L, cfg.NSMP, NMEM, 512),
            "memp": np.ascontiguousarray(inputs["mem_prompt"][ps]).reshape(-1, cfg.D),
            "relb": inputs["rel_bias_table"],
            "lamv": inputs["diff_lambda"].reshape(1, 512),
            "subg": inputs["diff_subln_g"].reshape(1, 256),
            "lng": inputs["ln_g"].reshape(L * 2, cfg.D), "lnb": inputs["ln_b"].reshape(L * 2, cfg.D),
        }
        for k, v in c.items():
            m["c_" + k] = v
        maps.append({k: np.ascontiguousarray(v, dtype=np.float32) for k, v in m.items()})
    return maps


def assemble(cfg, results, n_cores):
    L = cfg.DEPTH
    NP = cfg.NP
    yp = np.concatenate([r["y"][:NP].reshape(cfg.NSEQ, cfg.SEQ, cfg.D) for r in results], axis=0)
    ys = np.concatenate([r["y"][NP:].reshape(cfg.NSMP, DEC_SEQ, cfg.D) for r in results], axis=0)
    nkp = np.concatenate([r["nk"][:, :NP].reshape(L, cfg.NSEQ, cfg.SEQ, SW) for r in results], axis=1)
    nvp = np.concatenate([r["nv"][:, :NP].reshape(L, cfg.NSEQ, cfg.SEQ, SW) for r in results], axis=1)
    nks = np.concatenate([r["nk"][:, NP:].reshape(L, cfg.NSMP, DEC_SEQ, SW) for r in results], axis=1)
    nvs = np.concatenate([r["nv"][:, NP:].reshape(L, cfg.NSMP, DEC_SEQ, SW) for r in results], axis=1)
    nmk = np.concatenate([r["nmk"].reshape(L, cfg.NSEQ, NMEM, 4, 128) for r in results], axis=1)
    nmv = np.concatenate([r["nmv"].reshape(L, cfg.NSEQ, NMEM, 4, 128) for r in results], axis=1)
    return (yp, ys, nkp, nvp, nmk, nmv, nks, nvs)


def kernel(**inputs):
    n_cores = 8
    inputs = {k: np.asarray(v) for k, v in inputs.items()}
    B, SEQ, D = inputs["x_prompt"].shape
    DB = inputs["x_sample"].shape[0]
    PAST = inputs["cache_self_k"].shape[2]
    cfg = Cfg(D=D, SEQ=SEQ, NSEQ=B // n_cores, NSMP=DB // n_cores, PAST=PAST, DEPTH=inputs["w_in"].shape[0])
    nc = build(cfg)
    maps = shard_inputs(cfg, inputs, n_cores)
    res = run_bass_kernel_spmd(nc, maps, core_ids=list(range(n_cores)))
    return assemble(cfg, res.results, n_cores)
```

```python
import math
import os
from contextlib import ExitStack
import numpy as np
import concourse.bass as bass
import concourse.mybir as mybir
from concourse.bass_utils import run_bass_kernel_spmd

F32 = mybir.dt.float32
BF16 = mybir.dt.bfloat16
U32 = mybir.dt.uint32
AF = mybir.ActivationFunctionType
ALU = mybir.AluOpType
AX = mybir.AxisListType

NEG = -30000.0
LN_EPS = 1e-5
SW = 1536
NMEM = 256
DEC_SEQ = 32
NEXP = 16384


class Cfg:
    def __init__(self, D=2048, SEQ=2048, NSEQ=2, NSMP=4, PAST=2048, DEPTH=2, debug=False, stop=None):
        self.D, self.SEQ, self.NSEQ, self.NSMP, self.PAST, self.DEPTH = D, SEQ, NSEQ, NSMP, PAST, DEPTH
        self.KD = D // 128
        self.NP = NSEQ * SEQ
        self.NTOK = self.NP + NSMP * DEC_SEQ
        assert self.NTOK % 128 == 0 and SEQ % 256 == 0 and PAST % 128 == 0
        self.NT = self.NTOK // 128
        self.NMT = NSEQ * NMEM
        self.debug = debug
        self.stop = stop
        self.alpha = (2 * DEPTH) ** 0.25


class Sem:
    def __init__(self, h):
        self.h = h
        self.count = 0


class Buf:
    def __init__(self, t, name=""):
        self.t = t
        self.name = name
        self.w = None
        self.r = {}
        self.dsem = None
        self.dsem_sw = None
        self.excl = False

    def __getitem__(self, k):
        return self.t[k]


class Eng:
    def __init__(self, h, sem, name):
        self.h = h
        self.sem = sem
        self.name = name
        self.seen = {}
        self.outst = []


class KB:
    def __init__(self, nc, st):
        self.nc = nc
        self.st = st
        mk = lambda n: Sem(st.enter_context(nc.semaphore(n)))
        self.pe = Eng(nc.tensor, mk("s_pe"), "pe")
        self.act = Eng(nc.scalar, mk("s_act"), "act")
        self.dve = Eng(nc.vector, mk("s_dve"), "dve")
        self.pool = Eng(nc.gpsimd, mk("s_pool"), "pool")
        self.sp = Eng(nc.sync, mk("s_sp"), "sp")
        self.engs = [self.pe, self.act, self.dve, self.pool, self.sp]
        self.dsems = [mk(f"s_d{i}") for i in range(48)]
        self.free_dsems = list(self.dsems)
        self.dsems_sw = [mk(f"s_w{i}") for i in range(28)]
        self.free_dsems_sw = list(self.dsems_sw)
        self.psum = []
        for i in range(8):
            t = st.enter_context(nc.psum_tensor(f"ps{i}", [128, 512], F32))
            self.psum.append(Buf(t, f"ps{i}"))
            self.psum[-1].excl = True
        self.ps_rr = 0
        self.reserved = set()

    def sb(self, ph, name, shape, dt):
        self.uid = getattr(self, "uid", 0) + 1
        name = f"{name}_{self.uid}"
        return Buf(ph.enter_context(self.nc.sbuf_tensor(name, list(shape), dt)), name)

    def release(self, bufs):
        for b in bufs:
            if b.dsem is not None:
                self.free_dsems.append(b.dsem)
                b.dsem = None
            if b.dsem_sw is not None:
                self.free_dsems_sw.append(b.dsem_sw)
                b.dsem_sw = None

    def ps(self, reserve=False):
        while True:
            b = self.psum[self.ps_rr % 8]
            self.ps_rr += 1
            if id(b) not in self.reserved:
                break
        if reserve:
            self.reserved.add(id(b))
        return b

    def unreserve(self, b):
        self.reserved.discard(id(b))

    def _wait(self, e, evs):
        best = {}
        for (s, v) in evs:
            if v > best.get(id(s), (s, 0))[1]:
                best[id(s)] = (s, v)
        for (s, v) in best.values():
            if e is self.pe and s is e.sem:
                continue
            if e.seen.get(id(s), 0) < v:
                e.h.wait_ge(s.h, v)
                e.seen[id(s)] = v

    def _deps(self, r, w):
        evs = []
        for b in r:
            if b.w is not None:
                evs.append(b.w)
            if b.excl:
                evs.extend(b.r.values())
        for b in w:
            if b.w is not None:
                evs.append(b.w)
            evs.extend(b.r.values())
        return evs

    def _record(self, ev, r, w):
        for b in r:
            b.r[id(ev[0])] = ev
        for b in w:
            b.w = ev
            b.r = {}

    def op(self, e, fn, r=(), w=()):
        self._wait(e, self._deps(r, w))
        ins = fn(e.h)
        e.sem.count += 1
        ins.then_inc(e.sem.h, 1)
        ev = (e.sem, e.sem.count)
        self._record(ev, r, w)
        return ev

    def mm(self, out_buf, mms, r=(), extra_w=()):
        e = self.pe
        self._wait(e, self._deps(r, [out_buf] + list(extra_w)))
        n = len(mms)
        ins = None
        for i, (o, l, rh) in enumerate(mms):
            ins = e.h.matmul(o, l, rh, start=(i == 0), stop=(i == n - 1))
        e.sem.count += 1
        ins.then_inc(e.sem.h, 1)
        ev = (e.sem, e.sem.count)
        self._record(ev, r, [out_buf] + list(extra_w))
        return ev

    def mm_multi(self, out_bufs, groups, r=()):
        e = self.pe
        self._wait(e, self._deps(r, out_bufs))
        ins = None
        for g in groups:
            n = len(g)
            for i, (o, l, rh) in enumerate(g):
                ins = e.h.matmul(o, l, rh, start=(i == 0), stop=(i == n - 1))
        e.sem.count += 1
        ins.then_inc(e.sem.h, 1)
        ev = (e.sem, e.sem.count)
        self._record(ev, r, out_bufs)
        return ev

    def transpose(self, out_buf, out_ap, in_ap, ident_ap, r=()):
        e = self.pe
        self._wait(e, self._deps(r, [out_buf]))
        ins = e.h.transpose(out_ap, in_ap, ident_ap)
        e.sem.count += 1
        ins.then_inc(e.sem.h, 1)
        ev = (e.sem, e.sem.count)
        self._record(ev, r, [out_buf])
        return ev

    def dma(self, e, out, in_, sbuf, r=(), w=(), ndesc=256):
        if e is self.pool:
            if sbuf.dsem_sw is None:
                sbuf.dsem_sw = self.free_dsems_sw.pop(0)
            ds = sbuf.dsem_sw
        else:
            if sbuf.dsem is None:
                sbuf.dsem = self.free_dsems.pop(0)
            ds = sbuf.dsem
        lim = 1536 if e is self.pool else 6144
        tot = sum(n for _, n in e.outst) + ndesc
        evs = self._deps(r, w)
        while e.outst and tot > lim:
            ev0, n0 = e.outst.pop(0)
            evs.append((ev0[0], ev0[0].count))
            tot -= n0
        self._wait(e, evs)
        ins = e.h.dma_start(out=out, in_=in_)
        ds.count += 16
        ins.then_inc(ds.h, 16)
        ev = (ds, ds.count)
        e.outst.append((ev, ndesc))
        self._record(ev, r, w)
        return ev

    def barrier(self):
        evs = [(e.sem, e.sem.count) for e in self.engs if e.sem.count > 0]
        evs += [(d, d.count) for d in self.dsems + self.dsems_sw if d.count > 0]
        for e in self.engs:
            self._wait(e, evs)
            e.outst = []


def t5_bucket_np(rel):
    nb = 16
    max_exact = 8
    ret = np.where(rel > 0, nb, 0)
    n = np.abs(rel)
    nf = np.maximum(n, 1).astype(np.float32)
    large = max_exact + (np.log(nf / max_exact) / math.log(128 / max_exact) * (nb - max_exact)).astype(np.int32)
    large = np.minimum(large, nb - 1)
    return ret + np.where(n < max_exact, n, large)


def make_consts():
    c = {}
    c["ident"] = np.eye(128, dtype=np.float32)
    c["iota"] = np.tile(np.arange(128, dtype=np.float32)[None, :], (128, 1))
    k = np.arange(128)[:, None]
    q = np.arange(128)[None, :]
    c["tri"] = (k >= q).astype(np.float32)
    c["ones"] = np.ones((128, 128), np.float32)
    c["jx"] = np.eye(128, dtype=np.float32)[::-1].copy()
    c["maskd"] = np.where((k // 64) <= (q // 64), 0.0, NEG).astype(np.float32)
    c["masksb"] = (k < q).astype(np.float32)
    rel = np.arange(384) - 255
    bk = t5_bucket_np(rel.astype(np.int32))
    oh = np.zeros((32, 384), np.float32)
    oh[bk, np.arange(384)] = 1.0
    c["ohr"] = oh[:, ::-1].copy()
    return c


CONST_SHAPES = {"ident": [128, 128], "iota": [128, 128], "tri": [128, 128], "ones": [128, 128],
                "jx": [128, 128], "maskd": [128, 128], "masksb": [128, 128], "ohr": [32, 384]}


def build(cfg):
    nc = bass.Bass("TRN2", target_bir_lowering=False)
    D, KD, NTOK, NT, L = cfg.D, cfg.KD, cfg.NTOK, cfg.NT, cfg.DEPTH
    NSEQ, NSMP, SEQ, PAST, NMT = cfg.NSEQ, cfg.NSMP, cfg.SEQ, cfg.PAST, cfg.NMT

    def din(name, shape, dt=F32):
        return nc.dram_tensor(name, list(shape), dt, kind="ExternalInput").ap()

    def dout(name, shape, dt=F32):
        return nc.dram_tensor(name, list(shape), dt, kind="ExternalOutput").ap()

    def dscr(name, shape, dt):
        return nc.dram_tensor(name, list(shape), dt, kind="ExternalOutput" if cfg.debug else "Internal").ap()

    xin = din("xin", [NTOK, D])
    w_in = din("w_in", [L, D, 5120])
    w_o = din("w_o", [L, 2048, D])
    w_mkv = din("w_mkv", [L, D, 1024])
    w_q = din("w_q", [L, D, 2048])
    subk = din("subk", [L, 16, 128, 128])
    ut = din("ut", [L, D, NEXP])
    vv = din("vv", [L, NEXP, D])
    ck = din("ck", [L, NSMP, PAST, SW])
    cv = din("cv", [L, NSMP, PAST, SW])
    cmk = din("cmk", [L, NSMP, NMEM, 512])
    cmv = din("cmv", [L, NSMP, NMEM, 512])
    memp = din("memp", [NMT, D])
    relb = din("relb", [32, 6])
    lamv = din("lamv", [1, 512])
    subg = din("subg", [1, 256])
    lng = din("lng", [L * 2, D])
    lnb = din("lnb", [L * 2, D])
    cst = {k: din("c_" + k, s) for k, s in CONST_SHAPES.items()}

    y = dout("y", [NTOK, D])
    nk = dout("nk", [L, NTOK, SW])
    nv = dout("nv", [L, NTOK, SW])
    nmk = dout("nmk", [L, NMT, 512])
    nmv = dout("nmv", [L, NMT, 512])

    QT = dscr("QT", [SW, NTOK], BF16)
    KT = dscr("KT", [SW, NTOK], BF16)
    MQT = dscr("MQT", [512, NTOK], BF16)
    VB = dscr("VB", [NTOK, SW], BF16)
    MKT = dscr("MKT", [512, NMT], BF16)
    MVB = dscr("MVB", [NMT, 512], BF16)
    KTS = dscr("KTS", [NSMP, SW, PAST], BF16)
    MKTS = dscr("MKTS", [NSMP, 512, NMEM], BF16)
    MIXT = dscr("MIXT", [2048, NTOK], BF16)
    H1 = dscr("H1", [NTOK, D], F32)
    H1T = dscr("H1T", [D, NTOK], BF16)
    H2 = dscr("H2", [NTOK, D], F32)
    WT = dscr("WT", [NT, 128, 128, 128], BF16)
    BIASV = dscr("BIASV", [6, 384], F32)

    with ExitStack() as st:
        kb = KB(nc, st)
        pe, act, dve, pool, sp = kb.pe, kb.act, kb.dve, kb.pool, kb.sp

        identf = kb.sb(st, "identf", [128, 128], F32)
        identb = kb.sb(st, "identb", [128, 128], BF16)
        iota = kb.sb(st, "iota", [128, 128], F32)
        iotab = kb.sb(st, "iotab", [128, 128], BF16)
        trib = kb.sb(st, "trib", [128, 128], F32)
        tribb = kb.sb(st, "tribb", [128, 128], BF16)
        onesb = kb.sb(st, "onesb", [128, 128], BF16)
        onesf = kb.sb(st, "onesf", [128, 128], F32)
        jx = kb.sb(st, "jx", [128, 128], F32)
        maskd = kb.sb(st, "maskd", [128, 128], F32)
        masksb = kb.sb(st, "masksb", [128, 128], F32)
        lam_t = kb.sb(st, "lam_t", [128, 4], F32)
        subg_bc = kb.sb(st, "subg_bc", [128, 256], F32)
        cfar = kb.sb(st, "cfar", [128, 6], F32)
        zero_c = kb.sb(st, "zero_c", [128, 1], F32)
        bias_t = kb.sb(st, "bias_t", [128, 6, 2, 128], F32)

        kb.dma(sp, identf[:], cst["ident"], identf, w=[identf])
        kb.dma(pool, identb[:], cst["ident"], identb, w=[identb])
        kb.dma(sp, iota[:], cst["iota"], iota, w=[iota])
        kb.dma(pool, iotab[:], cst["iota"], iotab, w=[iotab])
        kb.dma(sp, trib[:], cst["tri"], trib, w=[trib])
        kb.dma(pool, tribb[:], cst["tri"], tribb, w=[tribb])
        kb.dma(pool, onesb[:], cst["ones"], onesb, w=[onesb])
        kb.dma(sp, onesf[:], cst["ones"], onesf, w=[onesf])
        kb.dma(sp, jx[:], cst["jx"], jx, w=[jx])
        kb.dma(sp, maskd[:], cst["maskd"], maskd, w=[maskd])
        kb.dma(sp, masksb[:], cst["masksb"], masksb, w=[masksb])
        kb.dma(sp, subg_bc[:], subg.partition_broadcast(128), subg_bc, w=[subg_bc])
        kb.dma(sp, cfar[:], relb[15:16, :].partition_broadcast(128), cfar, w=[cfar])
        kb.op(dve, lambda h: h.memset(zero_c[:], 0.0), w=[zero_c])

        with ExitStack() as ph:
            lbc = kb.sb(ph, "lbc", [128, 512], F32)
            lpr = kb.sb(ph, "lpr", [128, 256], F32)
            lsm = kb.sb(ph, "lsm", [128, 2], F32)
            kb.dma(sp, lbc[:], lamv.partition_broadcast(128), lbc, w=[lbc])
            l4 = lbc[:].rearrange("p (a b) -> p a b", a=4)
            kb.op(dve, lambda h: h.tensor_tensor(out=lpr[:].rearrange("p (a b) -> p a b", a=2), in0=l4[:, 0:4:2, :],
                                                 in1=l4[:, 1:4:2, :], op=ALU.mult), r=[lbc], w=[lpr])
            kb.op(dve, lambda h: h.reduce_sum(out=lsm[:], in_=lpr[:].rearrange("p (a b) -> p a b", a=2), axis=AX.X),
                  r=[lpr], w=[lsm])
            kb.op(act, lambda h: h.activation(out=lsm[:], in_=lsm[:], func=AF.Exp), r=[lsm], w=[lsm])
            lam_init0 = 0.8 - 0.6 * math.exp(-0.3 * 0)
            kb.op(dve, lambda h: h.tensor_tensor(out=lam_t[:, 0:1], in0=lsm[:, 0:1], in1=lsm[:, 1:2], op=ALU.subtract),
                  r=[lsm], w=[lam_t])
            kb.op(dve, lambda h: h.tensor_scalar(out=lam_t[:, 0:1], in0=lam_t[:, 0:1], scalar1=lam_init0, scalar2=None,
                                                 op0=ALU.add), r=[lam_t], w=[lam_t])
            kb.op(dve, lambda h: h.tensor_scalar(out=lam_t[:, 1:2], in0=lam_t[:, 0:1], scalar1=-1.0, scalar2=None,
                                                 op0=ALU.mult), r=[lam_t], w=[lam_t])
            kb.op(dve, lambda h: h.tensor_scalar(out=subg_bc[:], in0=subg_bc[:], scalar1=(1.0 - lam_init0), scalar2=None,
                                                 op0=ALU.mult), r=[subg_bc], w=[subg_bc])
            relb_sb = kb.sb(ph, "relb_sb", [32, 6], F32)
            ohr_sb = kb.sb(ph, "ohr_sb", [32, 384], F32)
            fv_sb = kb.sb(ph, "fv_sb", [6, 384], F32)
            bp = kb.sb(ph, "bp", [128, 6, 2, 128], F32)
            kb.dma(sp, relb_sb[:], relb, relb_sb, w=[relb_sb])
            kb.dma(sp, ohr_sb[:], cst["ohr"], ohr_sb, w=[ohr_sb])
            p0 = kb.ps()
            kb.mm(p0, [(p0[0:6, 0:384], relb_sb[:], ohr_sb[:])], r=[relb_sb, ohr_sb])
            kb.op(dve, lambda h: h.tensor_copy(out=fv_sb[:], in_=p0[0:6, 0:384]), r=[p0], w=[fv_sb])
            kb.dma(sp, BIASV, fv_sb[:], fv_sb, r=[fv_sb])
            kb.barrier()
            for hh in range(6):
                for ti, off in enumerate((1, 129)):
                    src = bass.AP(tensor=BIASV.tensor, offset=hh * 384 + off, ap=[[1, 128], [1, 128]])
                    kb.dma(sp, bp[:, hh, ti, :], src, bp, w=[bp])
            for hh in range(6):
                pp = kb.ps()
                kb.mm(pp, [(pp[:, 0:256], jx[:], bp[:, hh, :, :].rearrange("p a b -> p (a b)"))], r=[jx, bp])
                kb.op(dve, lambda h: h.tensor_copy(out=bias_t[:, hh, :, :].rearrange("p a b -> p (a b)"), in_=pp[:, 0:256]),
                      r=[pp], w=[bias_t])
            for hh in range(6):
                kb.op(dve, lambda h: h.tensor_tensor(out=bias_t[:, hh, 0, :], in0=bias_t[:, hh, 0, :], in1=maskd[:],
                                                     op=ALU.add), r=[bias_t, maskd], w=[bias_t])
            kb.barrier()
            kb.release([lbc, relb_sb, ohr_sb, fv_sb, bp, identf, identb, iota, iotab, trib, tribb, onesb, onesf, jx, maskd, masksb, subg_bc, cfar])

        def load_ln_params(ph, idx):
            g = kb.sb(ph, "ln_g", [128, D], F32)
            b = kb.sb(ph, "ln_b", [128, D], F32)
            kb.dma(sp, g[:], lng[idx:idx + 1, :].partition_broadcast(128), g, w=[g])
            kb.dma(sp, b[:], lnb[idx:idx + 1, :].partition_broadcast(128), b, w=[b])
            return g, b

        def layer_norm(r_buf, out_buf, g, b, stat):
            junk = out_buf
            kb.op(act, lambda h: h.activation(out=junk[:], in_=r_buf[:], func=AF.Copy, accum_out=stat[:, 0:1]),
                  r=[r_buf], w=[junk, stat])
            kb.op(dve, lambda h: h.tensor_scalar(out=stat[:, 1:2], in0=stat[:, 0:1], scalar1=-1.0 / D, scalar2=None,
                                                 op0=ALU.mult), r=[stat], w=[stat])
            kb.op(act, lambda h: h.activation(out=junk[:], in_=r_buf[:], func=AF.Square, bias=stat[:, 1:2], scale=1.0,
                                              accum_out=stat[:, 2:3]), r=[r_buf, stat], w=[junk, stat])
            kb.op(dve, lambda h: h.tensor_scalar(out=stat[:, 3:4], in0=stat[:, 2:3], scalar1=1.0 / D, scalar2=LN_EPS,
                                                 op0=ALU.mult, op1=ALU.add), r=[stat], w=[stat])
            kb.op(act, lambda h: h.activation(out=stat[:, 3:4], in_=stat[:, 3:4], func=AF.Sqrt), r=[stat], w=[stat])
            kb.op(dve, lambda h: h.reciprocal(out=stat[:, 4:5], in_=stat[:, 3:4]), r=[stat], w=[stat])
            kb.op(dve, lambda h: h.tensor_tensor(out=stat[:, 5:6], in0=stat[:, 1:2], in1=stat[:, 4:5], op=ALU.mult),
                  r=[stat], w=[stat])
            kb.op(act, lambda h: h.activation(out=junk[:], in_=r_buf[:], func=AF.Identity, bias=stat[:, 5:6],
                                              scale=stat[:, 4:5]), r=[r_buf, stat], w=[junk])
            kb.op(dve, lambda h: h.tensor_tensor(out=junk[:], in0=junk[:], in1=g[:], op=ALU.mult), r=[junk, g], w=[junk])
            kb.op(dve, lambda h: h.tensor_tensor(out=out_buf[:], in0=junk[:], in1=b[:], op=ALU.add), r=[junk, b],
                  w=[out_buf])

        def transpose_to_T(src_bf, dstT, col0, ncol_chunks, evict_rr):
            for c0 in range(0, ncol_chunks, 4):
                n = min(4, ncol_chunks - c0)
                pp = kb.ps()
                ppb = pp[:, 0:256].bitcast(BF16)
                for c in range(n):
                    kb.transpose(pp, ppb[:, c * 128:(c + 1) * 128], src_bf[:, (c0 + c) * 128:(c0 + c + 1) * 128],
                                 identb[:], r=[src_bf, identb])
                e = act if (evict_rr[0] % 2 == 0) else dve
                evict_rr[0] += 1
                if e is act:
                    kb.op(act, lambda h: h.activation(out=dstT[:, c0:c0 + n, col0:col0 + 128],
                                                      in_=ppb[:, 0:n * 128].rearrange("p (c t) -> p c t", c=n),
                                                      func=AF.Copy), r=[pp], w=[dstT])
                else:
                    kb.op(dve, lambda h: h.tensor_copy(out=dstT[:, c0:c0 + n, col0:col0 + 128],
                                                       in_=ppb[:, 0:n * 128].rearrange("p (c t) -> p c t", c=n)),
                          r=[pp], w=[dstT])

        def project(src_tok, ntiles, w_dram, ncols, fm_outs, tm_outs, kd, tag):
            nblk = ncols // 512
            with ExitStack() as ph:
                rr = [0]
                xts = [kb.sb(ph, f"{tag}_xT{i}", [128, kd, 512], BF16) for i in range(2)]
                xbs = [kb.sb(ph, f"{tag}_xb{i}", [128, kd * 128], BF16) for i in range(2)]
                wbs = [kb.sb(ph, f"{tag}_wb{i}", [128, kd, 512], BF16) for i in range(3)]
                kq_ = max(1, kd // 4)
                wfs = [[kb.sb(ph, f"{tag}_wf{i}_{q}", [128, kq_, 512], F32) for q in range(kd // kq_)] for i in range(2)]
                ofm = [kb.sb(ph, f"{tag}_ofm{i}", [128, 512], BF16) for i in range(3)]
                otf = [kb.sb(ph, f"{tag}_otf{i}", [128, 512], F32) for i in range(3)]
                otb = [kb.sb(ph, f"{tag}_otb{i}", [128, 512], BF16) for i in range(3)]
                wv = w_dram.rearrange("(k p) f -> p k f", p=128)
                gi = 0
                xi = 0
                wi = 0
                oi = 0
                for t0 in range(0, ntiles, 4):
                    nt_g = min(4, ntiles - t0)
                    ntk = nt_g * 128
                    xT = xts[gi % 2]
                    gi += 1
                    for tt in range(nt_g):
                        xb = xbs[xi % 2]
                        xi += 1
                        kb.dma(pool, xb[:], src_tok[(t0 + tt) * 128:(t0 + tt + 1) * 128, :], xb, w=[xb], ndesc=128)
                        transpose_to_T(xb, xT, tt * 128, kd, rr)
                    CUT = 0
                    for b in range(nblk if CUT != 1 else 0):
                        wb = wbs[wi % 3]
                        wi += 1
                        kq = max(1, kd // 4)
                        for qi, k0 in enumerate(range(0, kd, kq)):
                            wf = wfs[(wi - 1) % 2][qi]
                            kb.dma(sp, wf[:], wv[:, k0:k0 + kq, b * 512:(b + 1) * 512], wf, w=[wf], ndesc=128 * kq)
                            kb.op(dve, lambda h: h.tensor_copy(out=wb[:, k0:k0 + kq, :], in_=wf[:]), r=[wf], w=[wb])
                        for (c0, nch, dstT) in (fm_outs if CUT != 2 else []):
                            for ch in range(nch):
                                col = c0 + ch * 128
                                if col // 512 != b:
                                    continue
                                lo = col - b * 512
                                pp = kb.ps()
                                kb.mm(pp, [(pp[:, 0:ntk], wb[:, k, lo:lo + 128], xT[:, k, 0:ntk]) for k in range(kd)],
                                      r=[wb, xT])
                                ob = ofm[oi % 3]
                                oi += 1
                                kb.op(act, lambda h: h.activation(out=ob[:, 0:ntk], in_=pp[:, 0:ntk], func=AF.Copy),
                                      r=[pp], w=[ob])
                                kb.dma(act, dstT[ch * 128:(ch + 1) * 128, t0 * 128:t0 * 128 + ntk], ob[:, 0:ntk], ob, r=[ob],
                                       ndesc=128)
                        for (c0, ncl, dst32, dst16) in (tm_outs if CUT not in (2, 3) else []):
                            if not (c0 <= b * 512 < c0 + ncl):
                                continue
                            dc = b * 512 - c0
                            for tt in range(nt_g):
                                pp = kb.ps()
                                kb.mm(pp, [(pp[:, :], xT[:, k, tt * 128:(tt + 1) * 128], wb[:, k, :]) for k in range(kd)],
                                      r=[wb, xT])
                                rows = slice((t0 + tt) * 128, (t0 + tt + 1) * 128)
                                if dst32 is not None:
                                    o32 = otf[oi % 3]
                                    kb.op(act, lambda h: h.activation(out=o32[:], in_=pp[:, :], func=AF.Copy), r=[pp],
                                          w=[o32])
                                    kb.dma(act, dst32[rows, dc:dc + 512], o32[:], o32, r=[o32], ndesc=128)
                                if dst16 is not None:
                                    o16 = otb[oi % 3]
                                    kb.op(act, lambda h: h.activation(out=o16[:], in_=pp[:, :], func=AF.Copy), r=[pp], w=[o16])
                                    kb.dma(act, dst16[rows, dc:dc + 512], o16[:], o16, r=[o16], ndesc=128)
                                oi += 1
                kb.barrier()
                kb.release(xts + xbs + wbs + [w_ for ws_ in wfs for w_ in ws_] + ofm + otf + otb)

        def cache_transposes(l):
            with ExitStack() as ph:
                rr = [0]
                kin = [kb.sb(ph, f"ckin{i}", [128, SW], BF16) for i in range(2)]
                kout = [kb.sb(ph, f"ckout{i}", [128, 12, 128], BF16) for i in range(2)]
                i = 0
                for s in range(NSMP):
                    for blk in range(PAST // 128):
                        a = kin[i % 2]
                        o = kout[i % 2]
                        i += 1
                        kb.dma(pool, a[:], ck[l, s, blk * 128:(blk + 1) * 128, :], a, w=[a], ndesc=128)
                        transpose_to_T(a, o, 0, 12, rr)
                        kb.dma(sp, KTS[s, :, blk * 128:(blk + 1) * 128].rearrange("(c p) t -> p c t", p=128), o[:], o,
                               r=[o], ndesc=128 * 12)
                    for blk in range(NMEM // 128):
                        a = kin[i % 2]
                        o = kout[i % 2]
                        i += 1
                        kb.dma(pool, a[:, 0:512], cmk[l, s, blk * 128:(blk + 1) * 128, :], a, w=[a], ndesc=128)
                        transpose_to_T(a, o, 0, 4, rr)
                        kb.dma(sp, MKTS[s, :, blk * 128:(blk + 1) * 128].rearrange("(c p) t -> p c t", p=128),
                               o[:, 0:4, :], o, r=[o], ndesc=128 * 4)
                kb.barrier()
                kb.release(kin + kout)

        def attention(l):
            kind = l % 2
            scale = 128 ** -0.5
            with ExitStack() as ph:
                rr = [0]
                NKMAX = max(SEQ, PAST + 128) // 128
                qTs = [kb.sb(ph, f"a_qT{i}", [128, 2, SEQ], BF16) for i in range(2)]
                kTs = [kb.sb(ph, f"a_kT{i}", [128, 2, NKMAX * 128], BF16) for i in range(2)]
                vss = [kb.sb(ph, f"a_v{i}", [128, NKMAX, 260], BF16) for i in range(2)]
                for vsb in vss:
                    kb.op(dve, lambda h: h.memset(vsb[:], 1.0), w=[vsb])
                NR = 5
                tf = [kb.sb(ph, f"a_tf{i}", [128, 512], F32) for i in range(NR)]
                tg = [kb.sb(ph, f"a_tg{i}", [128, 512], F32) for i in range(NR)]
                tx = [kb.sb(ph, f"a_tx{i}", [128, 256], F32) for i in range(NR)]
                pb = [kb.sb(ph, f"a_pb{i}", [128, 512], BF16) for i in range(NR)]
                raccs = [kb.sb(ph, f"a_racc{i}", [128, 256], F32) for i in range(4)]
                raccb = [kb.sb(ph, f"a_raccb{i}", [128, 256], BF16) for i in range(4)]
                tgb = [kb.sb(ph, f"a_tgb{i}", [128, 256], BF16) for i in range(NR)]
                osb = [kb.sb(ph, f"a_o{i}", [128, 256], F32) for i in range(2)]
                osq = kb.sb(ph, "a_osq", [128, 256], F32)
                obf = [kb.sb(ph, f"a_ob{i}", [128, 256], BF16) for i in range(2)]
                oT = [kb.sb(ph, f"a_oT{i}", [128, 2, 128], BF16) for i in range(2)]
                sm = [kb.sb(ph, f"a_sm{i}", [128, 8], F32) for i in range(2)]
                cnt = {"h": 0, "w": 0, "o": 0}

                def run_head(ncomp, dv, nq_tot, qchunk, kss, cls, mode, bias_h, mix_row0, tok0, loads):
                    hi = cnt["h"]
                    cnt["h"] += 1
                    qT, kT, vs = qTs[hi % 2], kTs[hi % 2], vss[hi % 2]

                    def do_loads():
                        for (cast, which, dstf, src, nd) in loads:
                            buf = {"q": qT, "k": kT, "v": vs}[which]
                            kb.dma(pool if cast else sp, dstf(buf), src, buf, w=[buf], ndesc=nd)

                    def do_compute():
                        nkb = len(kss)
                        for q0 in range(0, nq_tot, qchunk):
                            nq = min(qchunk, nq_tot - q0)
                            nsub = (nq + 127) // 128
                            subs = [(q0 // 128 + i, min(128, nq - i * 128)) for i in range(nsub)]
                            accs = {}
                            for c in range(ncomp):
                                for si in range(nsub):
                                    accs[(c, si)] = kb.ps(reserve=True)
                            order = list(range(nkb))
                            if mode == "sb":
                                order = order[::-1]
                            started = set()
                            live = [j for j in order if any(cls(j, qb) != "skip" for qb, _ in subs)]
                            lastj = {}
                            for si, (qb, _) in enumerate(subs):
                                for j in live:
                                    if cls(j, qb) != "skip":
                                        lastj[si] = j
                            npairs = len(live)
                            st_ = {}

                            def stageA(j):
                                ks = kss[j]
                                types = [cls(j, qb) for qb, _ in subs]
                                sp_ps = kb.ps()
                                kb.mm_multi([sp_ps], [[(sp_ps[0:ks, c * 256:c * 256 + nq], kT[:, c, j * 128:j * 128 + ks],
                                                       qT[:, c, q0:q0 + nq])] for c in range(ncomp)], r=[kT, qT])
                                wi = cnt["w"]
                                cnt["w"] += 1
                                P = pb[wi % NR]
                                T = tf[wi % NR]
                                G = tg[wi % NR]
                                st_[j] = dict(ks=ks, types=types, P=P, T=T, G=G, wi=wi)
                                if mode in ("diff", "mem"):
                                    if all(t == "far" for t in types):
                                        bias_ap = cfar[0:ks, bias_h:bias_h + 1] if mode == "diff" else zero_c[0:ks, :]
                                        kb.op(act, lambda h: h.activation(
                                            out=P[0:ks, :].rearrange("p (c q) -> p c q", c=2)[:, 0:ncomp, 0:nq],
                                            in_=sp_ps[0:ks, :].rearrange("p (c q) -> p c q", c=2)[:, 0:ncomp, 0:nq],
                                            func=AF.Exp, bias=bias_ap, scale=scale), r=[sp_ps, cfar, zero_c], w=[P])
                                    else:
                                        for si, (qb, nqs) in enumerate(subs):
                                            t = types[si]
                                            if t == "skip":
                                                continue
                                            for c in range(ncomp):
                                                src = sp_ps[0:ks, c * 256 + si * 128:c * 256 + si * 128 + nqs]
                                                dst = P[0:ks, c * 256 + si * 128:c * 256 + si * 128 + nqs]
                                                if t == "far":
                                                    kb.op(act, lambda h: h.activation(out=dst, in_=src, func=AF.Exp,
                                                                                      bias=cfar[0:ks, bias_h:bias_h + 1],
                                                                                      scale=scale), r=[sp_ps, cfar], w=[P])
                                                else:
                                                    bt = bias_t[0:ks, bias_h, 0 if t == "diag" else 1, 0:nqs]
                                                    tmp = T[0:ks, c * 256 + si * 128:c * 256 + si * 128 + nqs]
                                                    kb.op(dve, lambda h: h.scalar_tensor_tensor(out=tmp, in0=src, scalar=scale,
                                                                                                in1=bt, op0=ALU.mult,
                                                                                                op1=ALU.add),
                                                          r=[sp_ps, bias_t], w=[T])
                                                    kb.op(act, lambda h: h.activation(out=dst, in_=tmp, func=AF.Exp), r=[T],
                                                          w=[P])
                                else:
                                    E = T
                                    SPt = tgb[wi % NR]
                                    st_[j]["G"] = SPt
                                    kb.op(act, lambda h: h.activation(out=E[0:ks, 0:nq], in_=sp_ps[0:ks, 0:nq], func=AF.Exp,
                                                                      scale=scale), r=[sp_ps], w=[E])
                                    kb.op(act, lambda h: h.activation(out=SPt[0:ks, 0:nq], in_=E[0:ks, 0:nq], func=AF.Ln,
                                                                      bias=1.0, scale=1.0), r=[E], w=[SPt])
                                    for si, (qb, nqs) in enumerate(subs):
                                        t = types[si]
                                        sl = slice(si * 128, si * 128 + nqs)
                                        if t == "diag":
                                            kb.op(dve, lambda h: h.tensor_tensor(out=SPt[0:ks, sl], in0=SPt[0:ks, sl],
                                                                                 in1=masksb[0:ks, 0:nqs], op=ALU.mult),
                                                  r=[SPt, masksb], w=[SPt])
                                            kb.op(dve, lambda h: h.tensor_tensor(out=E[0:ks, sl], in0=E[0:ks, sl],
                                                                                 in1=masksb[0:ks, 0:nqs], op=ALU.mult),
                                                  r=[E, masksb], w=[E])
                                        elif t == "skip":
                                            kb.op(dve, lambda h: h.memset(SPt[0:ks, sl], 0.0), w=[SPt])
                                            kb.op(dve, lambda h: h.memset(E[0:ks, sl], 0.0), w=[E])

                            rstate = {"prev": None, "n": 0}

                            def stageB(j):
                                if mode != "sb":
                                    return
                                d_ = st_[j]
                                ks, P, E, SPt, wi = d_["ks"], d_["P"], d_["T"], d_["G"], d_["wi"]
                                cp = kb.ps()
                                mmsl = [(cp[0:ks, 0:nq], tribb[0:ks, 0:ks], SPt[0:ks, 0:nq])]
                                rprev = rstate["prev"]
                                rd = [tribb, onesb, SPt]
                                if rprev is not None:
                                    mmsl.append((cp[0:ks, 0:nq], onesb[:, 0:ks], rprev[1][:, 0:nq]))
                                    rd.append(rprev[1])
                                kb.mm(cp, mmsl, r=rd)
                                X = tx[wi % NR]
                                kb.op(act, lambda h: h.activation(out=X[0:ks, 0:nq], in_=cp[0:ks, 0:nq], func=AF.Exp,
                                                                  scale=-1.0), r=[cp], w=[X])
                                kb.op(dve, lambda h: h.tensor_tensor(out=P[0:ks, 0:nq], in0=E[0:ks, 0:nq], in1=X[0:ks, 0:nq],
                                                                     op=ALU.mult), r=[E, X], w=[P])
                                rn = raccs[rstate["n"] % len(raccs)]
                                rnb = raccb[rstate["n"] % len(raccb)]
                                rstate["n"] += 1
                                if rprev is None:
                                    if ks < 128:
                                        kb.op(dve, lambda h: h.memset(rn[:], 0.0), w=[rn])
                                    kb.op(dve, lambda h: h.tensor_copy(out=rn[0:ks, 0:nq], in_=SPt[0:ks, 0:nq]), r=[SPt], w=[rn])
                                else:
                                    kb.op(dve, lambda h: h.tensor_tensor(out=rn[:, 0:nq], in0=rprev[0][:, 0:nq], in1=SPt[:, 0:nq],
                                                                         op=ALU.add), r=[rprev[0], SPt], w=[rn])
                                kb.op(pool, lambda h: h.tensor_copy(out=rnb[:, 0:nq], in_=rn[:, 0:nq]), r=[rn], w=[rnb])
                                rstate["prev"] = (rn, rnb)

                            def stageC(j):
                                d_ = st_[j]
                                ks, types, P = d_["ks"], d_["types"], d_["P"]
                                ncol = dv + (0 if mode == "sb" else 1)
                                for si, (qb, nqs) in enumerate(subs):
                                    if types[si] == "skip":
                                        continue
                                    for c in range(ncomp):
                                        a = accs[(c, si)]
                                        first = (c, si) not in started
                                        started.add((c, si))
                                        last = (lastj[si] == j)
                                        kb._wait(pe, kb._deps([P, vs], [a] if first else []))
                                        ins = pe.h.matmul(a[0:nqs, 0:ncol], P[0:ks, c * 256 + si * 128:c * 256 + si * 128 + nqs],
                                                          vs[0:ks, j, 256 - dv:256 - dv + ncol], start=first, stop=last)
                                        pe.sem.count += 1
                                        ins.then_inc(pe.sem.h, 1)
                                        ev = (pe.sem, pe.sem.count)
                                        kb._record(ev, [P, vs], [a])

                            for idx in range(npairs + 2):
                                if idx < npairs:
                                    stageA(live[idx])
                                if 0 <= idx - 1 < npairs:
                                    stageB(live[idx - 1])
                                if 0 <= idx - 2 < npairs:
                                    stageC(live[idx - 2])
                            for si, (qb, nqs) in enumerate(subs):
                                oi = cnt["o"]
                                cnt["o"] += 1
                                o = osb[oi % 2]
                                s_ = sm[oi % 2]
                                ob = obf[oi % 2]
                                ot = oT[oi % 2]
                                if mode == "sb":
                                    kb.op(act, lambda h: h.activation(out=ob[0:nqs, 0:dv], in_=accs[(0, si)][0:nqs, 0:dv],
                                                                      func=AF.Copy), r=[accs[(0, si)]], w=[ob])
                                elif mode == "mem":
                                    a0 = accs[(0, si)]
                                    kb.op(dve, lambda h: h.reciprocal(out=s_[0:nqs, 0:1], in_=a0[0:nqs, dv:dv + 1]), r=[a0],
                                          w=[s_])
                                    kb.op(act, lambda h: h.activation(out=ob[0:nqs, 0:dv], in_=a0[0:nqs, 0:dv], func=AF.Copy,
                                                                      scale=s_[0:nqs, 0:1]), r=[a0, s_], w=[ob])
                                else:
                                    a0, a1 = accs[(0, si)], accs[(1, si)]
                                    kb.op(dve, lambda h: h.reciprocal(out=s_[0:nqs, 0:1], in_=a0[0:nqs, dv:dv + 1]), r=[a0],
                                          w=[s_])
                                    kb.op(dve, lambda h: h.reciprocal(out=s_[0:nqs, 1:2], in_=a1[0:nqs, dv:dv + 1]), r=[a1],
                                          w=[s_])
                                    kb.op(dve, lambda h: h.tensor_tensor(out=s_[0:nqs, 1:2], in0=s_[0:nqs, 1:2],
                                                                         in1=lam_t[0:nqs, 1:2], op=ALU.mult), r=[s_, lam_t],
                                          w=[s_])
                                    kb.op(act, lambda h: h.activation(out=o[0:nqs, 0:dv], in_=a0[0:nqs, 0:dv], func=AF.Copy,
                                                                      scale=s_[0:nqs, 0:1]), r=[a0, s_], w=[o])
                                    kb.op(dve, lambda h: h.scalar_tensor_tensor(out=o[0:nqs, 0:dv], in0=a1[0:nqs, 0:dv],
                                                                                scalar=s_[0:nqs, 1:2], in1=o[0:nqs, 0:dv],
                                                                                op0=ALU.mult, op1=ALU.add), r=[a1, s_, o], w=[o])
                                    kb.op(act, lambda h: h.activation(out=osq[0:nqs, 0:dv], in_=o[0:nqs, 0:dv], func=AF.Square,
                                                                      accum_out=s_[0:nqs, 2:3]), r=[o], w=[osq, s_])
                                    kb.op(dve, lambda h: h.tensor_scalar(out=s_[0:nqs, 3:4], in0=s_[0:nqs, 2:3],
                                                                         scalar1=1.0 / dv, scalar2=LN_EPS, op0=ALU.mult,
                                                                         op1=ALU.add), r=[s_], w=[s_])
                                    kb.op(act, lambda h: h.activation(out=s_[0:nqs, 3:4], in_=s_[0:nqs, 3:4], func=AF.Sqrt),
                                          r=[s_], w=[s_])
                                    kb.op(dve, lambda h: h.reciprocal(out=s_[0:nqs, 4:5], in_=s_[0:nqs, 3:4]), r=[s_], w=[s_])
                                    kb.op(dve, lambda h: h.scalar_tensor_tensor(out=ob[0:nqs, 0:dv], in0=o[0:nqs, 0:dv],
                                                                                scalar=s_[0:nqs, 4:5], in1=subg_bc[0:nqs, 0:dv],
                                                                                op0=ALU.mult, op1=ALU.mult),
                                          r=[o, s_, subg_bc], w=[ob])
                                nch = dv // 128
                                pp = kb.ps()
                                ppb = pp[:, 0:256].bitcast(BF16)
                                for c in range(nch):
                                    kb.transpose(pp, ppb[:, c * 128:c * 128 + nqs], ob[0:nqs, c * 128:(c + 1) * 128],
                                                 identb[0:nqs, 0:nqs], r=[ob, identb])
                                kb.op(dve, lambda h: h.tensor_copy(out=ot[:, 0:nch, 0:nqs],
                                                                   in_=ppb[:, 0:nch * 128].rearrange("p (c t) -> p c t", c=nch)[:, :, 0:nqs]),
                                      r=[pp], w=[ot])
                                tq = tok0 + qb * 128 if nq_tot > 128 else tok0
                                kb.dma(sp, MIXT[mix_row0:mix_row0 + dv, tq:tq + nqs].rearrange("(c p) t -> p c t", p=128),
                                       ot[:, 0:nch, 0:nqs], ot, r=[ot], ndesc=128 * nch)
                            for a_ in accs.values():
                                kb.unreserve(a_)

                    return do_loads, do_compute

                jobs = []
                nblk = SEQ // 128
                npb = PAST // 128
                tokm = lambda ap: ap.rearrange("(j p) d -> p j d", p=128)

                def cls_diff_p(j, qb):
                    return "skip" if j > qb else ("diag" if j == qb else ("prev" if j == qb - 1 else "far"))

                def cls_sb_p(j, qb):
                    return "skip" if j > qb else ("diag" if j == qb else "far")

                def cls_diff_s(j, qb):
                    return "diag" if j == npb else ("prev" if j == npb - 1 else "far")

                def cls_sb_s(j, qb):
                    return "diag" if j == npb else "far"

                cls_far = lambda j, qb: "far"
                for s in range(NSEQ):
                    tok0 = s * SEQ
                    ts_ = slice(tok0, tok0 + SEQ)
                    if kind == 0:
                        for hh in range(6):
                            lds = []
                            for c in range(2):
                                r0 = hh * 256 + c * 128
                                lds.append((False, "q", lambda b, c=c: b[:, c, 0:SEQ], QT[r0:r0 + 128, ts_], 128))
                                lds.append((False, "k", lambda b, c=c: b[:, c, 0:SEQ], KT[r0:r0 + 128, ts_], 128))
                            lds.append((False, "v", lambda b: b[:, 0:nblk, 0:256], tokm(VB[ts_, hh * 256:(hh + 1) * 256]),
                                        128 * nblk))
                            jobs.append(run_head(2, 256, SEQ, 256, [128] * nblk, cls_diff_p, "diff", hh, hh * 256, tok0, lds))
                    else:
                        for hh in range(12):
                            r0 = hh * 128
                            lds = [(False, "q", lambda b: b[:, 0, 0:SEQ], QT[r0:r0 + 128, ts_], 128),
                                   (False, "k", lambda b: b[:, 0, 0:SEQ], KT[r0:r0 + 128, ts_], 128),
                                   (False, "v", lambda b: b[:, 0:nblk, 128:256], tokm(VB[ts_, r0:r0 + 128]), 128 * nblk)]
                            jobs.append(run_head(1, 128, SEQ, 256, [128] * nblk, cls_sb_p, "sb", 0, r0, tok0, lds))
                    for hh in range(4):
                        r0 = hh * 128
                        ms_ = slice(s * NMEM, (s + 1) * NMEM)
                        lds = [(False, "q", lambda b: b[:, 0, 0:SEQ], MQT[r0:r0 + 128, ts_], 128),
                               (False, "k", lambda b: b[:, 0, 0:NMEM], MKT[r0:r0 + 128, ms_], 128),
                               (False, "v", lambda b: b[:, 0:NMEM // 128, 128:256], tokm(MVB[ms_, r0:r0 + 128]), 256)]
                        jobs.append(run_head(1, 128, SEQ, 256, [128] * (NMEM // 128), cls_far, "mem", 0, SW + r0, tok0, lds))
                for s in range(NSMP):
                    tok0 = cfg.NP + s * DEC_SEQ
                    ts_ = slice(tok0, tok0 + DEC_SEQ)
                    kss_s = [128] * npb + [DEC_SEQ]
                    if kind == 0:
                        for hh in range(6):
                            lds = []
                            for c in range(2):
                                r0 = hh * 256 + c * 128
                                lds.append((False, "q", lambda b, c=c: b[:, c, 0:DEC_SEQ], QT[r0:r0 + 128, ts_], 128))
                                lds.append((False, "k", lambda b, c=c: b[:, c, 0:PAST], KTS[s, r0:r0 + 128, :], 128))
                                lds.append((False, "k", lambda b, c=c: b[:, c, PAST:PAST + DEC_SEQ], KT[r0:r0 + 128, ts_], 128))
                            lds.append((True, "v", lambda b: b[:, 0:npb, 0:256], tokm(cv[l, s, :, hh * 256:(hh + 1) * 256]),
                                        128 * npb))
                            lds.append((False, "v", lambda b: b[0:DEC_SEQ, npb, 0:256], VB[ts_, hh * 256:(hh + 1) * 256], 32))
                            jobs.append(run_head(2, 256, DEC_SEQ, 256, kss_s, cls_diff_s, "diff", hh, hh * 256, tok0, lds))
                    else:
                        for hh in range(12):
                            r0 = hh * 128
                            lds = [(False, "q", lambda b: b[:, 0, 0:DEC_SEQ], QT[r0:r0 + 128, ts_], 128),
                                   (False, "k", lambda b: b[:, 0, 0:PAST], KTS[s, r0:r0 + 128, :], 128),
                                   (False, "k", lambda b: b[:, 0, PAST:PAST + DEC_SEQ], KT[r0:r0 + 128, ts_], 128),
                                   (True, "v", lambda b: b[:, 0:npb, 128:256], tokm(cv[l, s, :, r0:r0 + 128]), 128 * npb),
                                   (False, "v", lambda b: b[0:DEC_SEQ, npb, 128:256], VB[ts_, r0:r0 + 128], 32)]
                            jobs.append(run_head(1, 128, DEC_SEQ, 256, kss_s, cls_sb_s, "sb", 0, r0, tok0, lds))
                    for hh in range(4):
                        r0 = hh * 128
                        lds = [(False, "q", lambda b: b[:, 0, 0:DEC_SEQ], MQT[r0:r0 + 128, ts_], 128),
                               (False, "k", lambda b: b[:, 0, 0:NMEM], MKTS[s, r0:r0 + 128, :], 128),
                               (True, "v", lambda b: b[:, 0:NMEM // 128, 128:256], tokm(cmv[l, s, :, r0:r0 + 128]), 256)]
                        jobs.append(run_head(1, 128, DEC_SEQ, 256, [128] * (NMEM // 128), cls_far, "mem", 0, SW + r0, tok0, lds))
                jobs[0][0]()
                for ji, (ld_, cp_) in enumerate(jobs):
                    if ji + 1 < len(jobs):
                        jobs[ji + 1][0]()
                    cp_()
                kb.barrier()
                kb.release(qTs + kTs + vss + tf + tg + tx + pb + osb + obf + oT + sm + raccs + raccb + tgb + [osq])

        def out_proj(l, src_tok):
            with ExitStack() as ph:
                rr = [0]
                wo = kb.sb(ph, "wo", [128, 16, D], BF16)
                wov = w_o[l].rearrange("(k p) f -> p k f", p=128)
                for k0 in range(0, 16, 2):
                    kb.dma(pool, wo[:, k0:k0 + 2, :], wov[:, k0:k0 + 2, :], wo, w=[wo], ndesc=256)
                g, b = load_ln_params(ph, l * 2 + 0)
                mts = [kb.sb(ph, f"c_mT{i}", [128, 16, 128], BF16) for i in range(2)]
                xs = [kb.sb(ph, f"c_x{i}", [128, D], F32) for i in range(2)]
                rs = [kb.sb(ph, f"c_r{i}", [128, D], F32) for i in range(2)]
                hs = [kb.sb(ph, f"c_h{i}", [128, D], F32) for i in range(2)]
                hb = [kb.sb(ph, f"c_hb{i}", [128, D], BF16) for i in range(2)]
                hT = [kb.sb(ph, f"c_hT{i}", [128, KD, 128], BF16) for i in range(2)]
                stat = [kb.sb(ph, f"c_st{i}", [128, 8], F32) for i in range(2)]
                for t in range(NT):
                    mT, x, r, hsb, hbb, hTt, stt = mts[t % 2], xs[t % 2], rs[t % 2], hs[t % 2], hb[t % 2], hT[t % 2], stat[t % 2]
                    rows = slice(t * 128, (t + 1) * 128)
                    kb.dma(sp, mT[:], MIXT[:, rows].rearrange("(c p) t -> p c t", p=128), mT, w=[mT], ndesc=2048)
                    kb.dma(sp, x[:], src_tok[rows, :], x, w=[x], ndesc=128)
                    for f0 in range(0, D, 512):
                        pp = kb.ps()
                        kb.mm(pp, [(pp[:, :], mT[:, k, :], wo[:, k, f0:f0 + 512]) for k in range(16)], r=[mT, wo])
                        kb.op(dve, lambda h: h.scalar_tensor_tensor(out=r[:, f0:f0 + 512], in0=x[:, f0:f0 + 512],
                                                                    scalar=cfg.alpha, in1=pp[:, :], op0=ALU.mult,
                                                                    op1=ALU.add), r=[x, pp], w=[r])
                    layer_norm(r, hsb, g, b, stt)
                    kb.dma(pool, H1[rows, :], hsb[:], hsb, r=[hsb], ndesc=128)
                    kb.op(act, lambda h: h.activation(out=hbb[:], in_=hsb[:], func=AF.Copy), r=[hsb], w=[hbb])
                    transpose_to_T(hbb, hTt, 0, KD, rr)
                    kb.dma(pool, H1T[:, rows].rearrange("(c p) t -> p c t", p=128), hTt[:], hTt, r=[hTt], ndesc=128 * KD)
                kb.barrier()
                kb.release([wo, g, b] + mts + xs + rs + hs + hb + hT + stat)

        def peer_retrieve(l, IDX1T, IDX2T, GT):
            with ExitStack() as ph:
                wq = kb.sb(ph, "wq", [128, KD, 2048], BF16)
                wqv = w_q[l].rearrange("(k p) f -> p k f", p=128)
                for k0 in range(0, KD, 2):
                    kb.dma(pool, wq[:, k0:k0 + 2, :], wqv[:, k0:k0 + 2, :], wq, w=[wq], ndesc=256)
                sk = kb.sb(ph, "sk", [128, 16, 128], BF16)
                kb.dma(pool, sk[:], subk[l].rearrange("g d n -> d g n"), sk, w=[sk], ndesc=2048)
                hTs = [kb.sb(ph, f"d_hT{i}", [128, KD, 128], BF16) for i in range(2)]
                qp = [kb.sb(ph, f"d_qp{i}", [128, 16, 128], BF16) for i in range(2)]
                sc = [kb.sb(ph, f"d_sc{i}", [128, 2048], F32) for i in range(2)]
                sc2 = kb.sb(ph, "d_sc2", [128, 2048], F32)
                sv = kb.sb(ph, "d_sv", [128, 16, 16], F32)
                si_u = kb.sb(ph, "d_siu", [128, 16, 16], U32)
                si_f = kb.sb(ph, "d_sif", [128, 16, 16], F32)
                cand = kb.sb(ph, "d_cand", [128, 8, 256], F32)
                cand2 = kb.sb(ph, "d_cand2", [128, 8, 256], F32)
                best = kb.sb(ph, "d_best", [128, 8, 16], F32)
                pos_u = kb.sb(ph, "d_posu", [128, 8, 16], U32)
                pa_u = kb.sb(ph, "d_pau", [128, 8, 16], U32)
                pb_u = kb.sb(ph, "d_pbu", [128, 8, 16], U32)
                pa_f = kb.sb(ph, "d_paf", [128, 8, 16], F32)
                pb_f = kb.sb(ph, "d_pbf", [128, 8, 16], F32)
                oh = kb.sb(ph, "d_oh", [128, 8, 16, 16], F32)
                outs = [kb.sb(ph, f"d_out{i}", [128, 3, 128], F32) for i in range(2)]
                gs = kb.sb(ph, "d_gs", [128, 8], F32)
                v_sv = [Buf(None, f"vsv{i}") for i in range(16)]
                v_si = [Buf(None, f"vsi{i}") for i in range(16)]
                v_sc2 = [Buf(None, f"vsc2{i}") for i in range(16)]
                v_best = [Buf(None, f"vb{i}") for i in range(8)]
                v_pos = [Buf(None, f"vp{i}") for i in range(8)]
                v_c2 = [Buf(None, f"vc2{i}") for i in range(8)]
                for t in range(NT):
                    hTt, qpt, sct, ot = hTs[t % 2], qp[t % 2], sc[t % 2], outs[t % 2]
                    rows = slice(t * 128, (t + 1) * 128)
                    kb.dma(sp, hTt[:], H1T[:, rows].rearrange("(c p) t -> p c t", p=128), hTt, w=[hTt], ndesc=128 * KD)
                    for g0 in range(0, 16, 4):
                        pp = kb.ps()
                        kb.mm_multi([pp], [[(pp[:, gg * 128:(gg + 1) * 128], wq[:, k, (g0 + gg) * 128:(g0 + gg + 1) * 128],
                                             hTt[:, k, :]) for k in range(KD)] for gg in range(4)], r=[wq, hTt])
                        kb.op(act, lambda h: h.activation(out=qpt[:, g0:g0 + 4, :],
                                                          in_=pp[:, :].rearrange("p (g t) -> p g t", g=4), func=AF.Copy),
                              r=[pp], w=[qpt])
                    for g0 in range(0, 16, 4):
                        pp = kb.ps()
                        kb.mm_multi([pp], [[(pp[:, gg * 128:(gg + 1) * 128], qpt[:, g0 + gg, :], sk[:, g0 + gg, :])]
                                           for gg in range(4)], r=[qpt, sk])
                        kb.op(act, lambda h: h.activation(out=sct[:, g0 * 128:(g0 + 4) * 128], in_=pp[:, :], func=AF.Copy),
                              r=[pp], w=[sct])
                    for stg in range(5):
                        for gI in range(16):
                            seg = sct[:, gI * 128:(gI + 1) * 128]
                            seg2 = sc2[:, gI * 128:(gI + 1) * 128]
                            vsv, vsi, vs2 = v_sv[gI], v_si[gI], v_sc2[gI]
                            if stg == 0:
                                kb.op(dve, lambda h: h.max(out=sv[:, gI, 0:8], in_=seg), r=[sct], w=[vsv])
                            elif stg == 1:
                                kb.op(dve, lambda h: h.max_index(out=si_u[:, gI, 0:8], in_max=sv[:, gI, 0:8], in_values=seg),
                                      r=[sct, vsv], w=[vsi])
                            elif stg == 2:
                                kb.op(dve, lambda h: h.match_replace(out=seg2, in_to_replace=sv[:, gI, 0:8], in_values=seg,
                                                                     imm_value=-1e30), r=[sct, vsv], w=[vs2])
                            elif stg == 3:
                                kb.op(dve, lambda h: h.max(out=sv[:, gI, 8:16], in_=seg2), r=[vs2], w=[vsv])
                            else:
                                kb.op(dve, lambda h: h.max_index(out=si_u[:, gI, 8:16], in_max=sv[:, gI, 8:16],
                                                                 in_values=seg2), r=[vs2, vsv], w=[vsi])
                    kb.op(dve, lambda h: h.tensor_copy(out=si_f[:], in_=si_u[:]), r=v_si, w=[si_f])
                    sv4 = sv[:].rearrange("p (h c) k -> p h c k", c=2)
                    si4 = si_f[:].rearrange("p (h c) k -> p h c k", c=2)
                    kb.op(dve, lambda h: h.tensor_tensor(out=cand[:].rearrange("p h (a b) -> p h a b", a=16),
                                                         in0=sv4[:, :, 0, :].unsqueeze(3).to_broadcast([128, 8, 16, 16]),
                                                         in1=sv4[:, :, 1, :].unsqueeze(2).to_broadcast([128, 8, 16, 16]),
                                                         op=ALU.add), r=v_sv, w=[cand])
                    for stg in range(5):
                        for hh in range(8):
                            vb, vp, vc2 = v_best[hh], v_pos[hh], v_c2[hh]
                            if stg == 0:
                                kb.op(dve, lambda h: h.max(out=best[:, hh, 0:8], in_=cand[:, hh, :]), r=[cand], w=[vb])
                            elif stg == 1:
                                kb.op(dve, lambda h: h.max_index(out=pos_u[:, hh, 0:8], in_max=best[:, hh, 0:8],
                                                                 in_values=cand[:, hh, :]), r=[cand, vb], w=[vp])
                            elif stg == 2:
                                kb.op(dve, lambda h: h.match_replace(out=cand2[:, hh, :], in_to_replace=best[:, hh, 0:8],
                                                                     in_values=cand[:, hh, :], imm_value=-1e30),
                                      r=[cand, vb], w=[vc2])
                            elif stg == 3:
                                kb.op(dve, lambda h: h.max(out=best[:, hh, 8:16], in_=cand2[:, hh, :]), r=[vc2], w=[vb])
                            else:
                                kb.op(dve, lambda h: h.max_index(out=pos_u[:, hh, 8:16], in_max=best[:, hh, 8:16],
                                                                 in_values=cand2[:, hh, :]), r=[vc2, vb], w=[vp])
                    kb.op(dve, lambda h: h.tensor_single_scalar(out=pa_u[:], in_=pos_u[:], scalar=4,
                                                                op=ALU.logical_shift_right), r=v_pos, w=[pa_u])
                    kb.op(dve, lambda h: h.tensor_single_scalar(out=pb_u[:], in_=pos_u[:], scalar=15, op=ALU.bitwise_and),
                          r=v_pos, w=[pb_u])
                    kb.op(dve, lambda h: h.tensor_copy(out=pa_f[:], in_=pa_u[:]), r=[pa_u], w=[pa_f])
                    kb.op(dve, lambda h: h.tensor_copy(out=pb_f[:], in_=pb_u[:]), r=[pb_u], w=[pb_f])
                    for which, pf in ((0, pa_f), (1, pb_f)):
                        kb.op(dve, lambda h: h.tensor_tensor(out=oh[:], in0=pf[:].unsqueeze(3).to_broadcast([128, 8, 16, 16]),
                                                             in1=iota[:, 0:16].unsqueeze(1).unsqueeze(1).to_broadcast([128, 8, 16, 16]),
                                                             op=ALU.is_equal), r=[pf, iota], w=[oh])
                        kb.op(dve, lambda h: h.tensor_tensor(out=oh[:], in0=oh[:],
                                                             in1=si4[:, :, which, :].unsqueeze(2).to_broadcast([128, 8, 16, 16]),
                                                             op=ALU.mult), r=[oh, si_f], w=[oh])
                        kb.op(dve, lambda h: h.tensor_reduce(out=ot[:, which, :].rearrange("p (h k) -> p h k", h=8),
                                                             in_=oh[:], axis=AX.X, op=ALU.add), r=[oh], w=[ot])
                    gv = ot[:, 2, :].rearrange("p (h k) -> p h k", h=8)
                    kb.op(dve, lambda h: h.tensor_tensor(out=gv, in0=best[:], in1=best[:, :, 0:1].to_broadcast([128, 8, 16]),
                                                         op=ALU.subtract), r=v_best, w=[ot])
                    kb.op(act, lambda h: h.activation(out=gv, in_=gv, func=AF.Exp), r=[ot], w=[ot])
                    kb.op(dve, lambda h: h.tensor_reduce(out=gs[:], in_=gv, axis=AX.X, op=ALU.add), r=[ot], w=[gs])
                    kb.op(dve, lambda h: h.reciprocal(out=gs[:], in_=gs[:]), r=[gs], w=[gs])
                    kb.op(dve, lambda h: h.tensor_tensor(out=gv, in0=gv, in1=gs[:].unsqueeze(2).to_broadcast([128, 8, 16]),
                                                         op=ALU.mult), r=[ot, gs], w=[ot])
                    pp = kb.ps()
                    for w3 in range(3):
                        kb.transpose(pp, pp[:, w3 * 128:(w3 + 1) * 128], ot[:, w3, :], identf[:], r=[ot, identf])
                    for w3, dstb in enumerate((IDX1T, IDX2T, GT)):
                        kb.op(act, lambda h: h.activation(out=dstb[:, rows], in_=pp[:, w3 * 128:(w3 + 1) * 128], func=AF.Copy),
                              r=[pp], w=[dstb])
                kb.barrier()
                kb.release([wq, sk] + hTs)

        def peer_wgen(l, IDX1T, IDX2T, GT):
            with ExitStack() as ph:
                Ps = [kb.sb(ph, f"e_P{i}", [128, 64, 128], BF16) for i in range(2)]
                Qs = [kb.sb(ph, f"e_Q{i}", [128, 64, 128], BF16) for i in range(2)]
                Ws = [kb.sb(ph, f"e_W{i}", [128, 128, 128], BF16) for i in range(2)]
                hcnt = 0
                for t in range(NT):
                    W = Ws[t % 2]
                    for hf in range(2):
                        P, Q = Ps[hcnt % 2], Qs[hcnt % 2]
                        hcnt += 1
                        rows = slice(t * 128 + hf * 64, t * 128 + hf * 64 + 64)
                        io3 = iotab[:].unsqueeze(1).to_broadcast([128, 64, 128])
                        kb.op(dve, lambda h: h.tensor_tensor(out=Q[:], in0=io3,
                                                             in1=IDX2T[:, rows].unsqueeze(2).to_broadcast([128, 64, 128]),
                                                             op=ALU.is_equal), r=[iotab, IDX2T], w=[Q])
                        kb.op(dve, lambda h: h.tensor_tensor(out=P[:], in0=io3,
                                                             in1=IDX1T[:, rows].unsqueeze(2).to_broadcast([128, 64, 128]),
                                                             op=ALU.is_equal), r=[iotab, IDX1T], w=[P])
                        kb.op(pool, lambda h: h.tensor_tensor(out=P[:], in0=P[:],
                                                              in1=GT[:, rows].unsqueeze(2).to_broadcast([128, 64, 128]),
                                                              op=ALU.mult), r=[P, GT], w=[P])
                        for t4 in range(0, 64, 4):
                            pp = kb.ps()
                            kb.mm_multi([pp], [[(pp[:, j * 128:(j + 1) * 128], Q[:, t4 + j, :], P[:, t4 + j, :])] for j in range(4)],
                                        r=[P, Q])
                            src = pp[:, :].rearrange("p (t i) -> p i t", t=4)
                            tw = hf * 64 + t4
                            kb.op(act, lambda h: h.activation(out=W[:, :, tw:tw + 4], in_=src, func=AF.Copy), r=[pp], w=[W])
                    for c0 in range(0, 128, 32):
                        kb.dma(sp, WT[t, :, c0:c0 + 32, :], W[:, c0:c0 + 32, :], W, r=[W], ndesc=128)
                kb.barrier()
                kb.release(Ps + Qs + Ws)

        def peer_dense(l, dst_tok, final):
            SC = 4
            nsc = 128 // SC
            utv = ut[l].rearrange("(k p) e -> p k e", p=128)
            vvv = vv[l].rearrange("(c p) d -> p c d", p=128)
            maxpass = 6
            passes = []
            t = 0
            while t < NT:
                n = min(maxpass, NT - t)
                passes.append((t, n))
                t += n
            with ExitStack() as ph:
                rr = [0]
                g, b = load_ln_params(ph, l * 2 + 1)
                acc = kb.sb(ph, "f_acc", [128, maxpass, D], F32)
                hTp = kb.sb(ph, "f_hT", [128, KD, maxpass * 128], BF16)
                us = [kb.sb(ph, f"f_u{i}", [128, KD, SC * 128], BF16) for i in range(2)]
                vs_ = [kb.sb(ph, f"f_v{i}", [128, SC, D], BF16) for i in range(2)]
                gsb = [kb.sb(ph, f"f_g{i}", [128, SC, 512], BF16) for i in range(3)]
                wts = [kb.sb(ph, f"f_w{i}", [128, SC, 512], BF16) for i in range(3)]
                xs = [kb.sb(ph, f"f_x{i}", [128, D], F32) for i in range(1)]
                ys = [kb.sb(ph, f"f_y{i}", [128, D], F32) for i in range(1)]
                stat = [kb.sb(ph, f"f_st{i}", [128, 8], F32) for i in range(2)]
                if os.environ.get('MK_DEBUG'):
                    print('dense sbuf remaining', nc.sbuf_bytes_remaining)
                ui = 0
                gi = 0
                for (pt0, pn) in passes:
                    ntp = pn * 128
                    tok0 = pt0 * 128
                    kb.dma(sp, hTp[:, :, 0:ntp], H1T[:, tok0:tok0 + ntp].rearrange("(c p) t -> p c t", p=128), hTp, w=[hTp],
                           ndesc=128 * KD)
                    steps = [(sc_i, g0) for sc_i in range(nsc) for g0 in range(0, pn, 4)]
                    uv = {}
                    ctx = {}

                    def stA(step):
                        nonlocal ui, gi
                        sc_i, g0 = step
                        if g0 == 0:
                            u, v_ = us[ui % 2], vs_[ui % 2]
                            ui += 1
                            uv[sc_i] = (u, v_)
                            e0 = sc_i * SC * 128
                            kq = max(1, KD // 4)
                            for k0 in range(0, KD, kq):
                                kb.dma(pool, u[:, k0:k0 + kq, :], utv[:, k0:k0 + kq, e0:e0 + SC * 128], u, w=[u], ndesc=128 * kq)
                            for c in range(SC):
                                kb.dma(pool, v_[:, c, :], vvv[:, sc_i * SC + c, :], v_, w=[v_], ndesc=128)
                        u, v_ = uv[sc_i]
                        ng = min(4, pn - g0)
                        ntk = ng * 128
                        G, Wt = gsb[gi % 3], wts[gi % 3]
                        gi += 1
                        for tt in range(ng):
                            kb.dma(sp, Wt[:, :, tt * 128:(tt + 1) * 128], WT[pt0 + g0 + tt, :, sc_i * SC:(sc_i + 1) * SC, :],
                                   Wt, w=[Wt], ndesc=128)
                        for c in range(SC):
                            pp = kb.ps()
                            kb.mm(pp, [(pp[:, 0:ntk], u[:, k, c * 128:(c + 1) * 128], hTp[:, k, g0 * 128:g0 * 128 + ntk])
                                       for k in range(KD)], r=[u, hTp])
                            kb.op(act, lambda h: h.activation(out=G[:, c, 0:ntk], in_=pp[:, 0:ntk], func=AF.Gelu), r=[pp],
                                  w=[G])
                        kb.op(dve, lambda h: h.tensor_tensor(out=G[:, :, 0:ntk], in0=G[:, :, 0:ntk], in1=Wt[:, :, 0:ntk],
                                                             op=ALU.mult), r=[G, Wt], w=[G])
                        ctx[step] = (G, ng, v_)

                    def stB(step):
                        sc_i, g0 = step
                        H, ng, v_ = ctx.pop(step)
                        for tt in range(ng):
                            ti = g0 + tt
                            for f0 in range(0, D, 512):
                                pp = kb.ps()
                                kb.mm(pp, [(pp[:, :], H[:, c, tt * 128:(tt + 1) * 128], v_[:, c, f0:f0 + 512])
                                           for c in range(SC)], r=[H, v_])
                                if sc_i == 0:
                                    kb.op(dve, lambda h: h.tensor_copy(out=acc[:, ti, f0:f0 + 512], in_=pp[:, :]), r=[pp],
                                          w=[acc])
                                else:
                                    kb.op(dve, lambda h: h.tensor_tensor(out=acc[:, ti, f0:f0 + 512],
                                                                         in0=acc[:, ti, f0:f0 + 512], in1=pp[:, :],
                                                                         op=ALU.add), r=[pp, acc], w=[acc])

                    for i_ in range(len(steps) + 1):
                        if i_ < len(steps):
                            stA(steps[i_])
                        if i_ >= 1:
                            stB(steps[i_ - 1])
                    for ti in range(pn):
                        x, yb, stt = xs[0], ys[0], stat[ti % 2]
                        rows = slice(tok0 + ti * 128, tok0 + (ti + 1) * 128)
                        kb.dma(sp, x[:], H1[rows, :], x, w=[x], ndesc=128)
                        kb.op(dve, lambda h: h.scalar_tensor_tensor(out=x[:], in0=x[:], scalar=cfg.alpha, in1=acc[:, ti, :],
                                                                    op0=ALU.mult, op1=ALU.add), r=[x, acc], w=[x])
                        layer_norm(x, yb, g, b, stt)
                        kb.dma(sp, dst_tok[rows, :], yb[:], yb, r=[yb], ndesc=128)
                kb.barrier()
                kb.release([g, b, acc, hTp] + us + vs_ + gsb + wts + xs + ys + stat)

        stop = cfg.stop
        for l in range(L if stop != ("setup", 0) else 0):
            src_tok = xin if l == 0 else H2
            dst_tok = y if l == L - 1 else H2
            fm = [(0, 12, QT), (SW, 12, KT), (3 * SW, 4, MQT)]
            tm = [(SW, SW, nk[l], None), (2 * SW, SW, nv[l], VB)]
            project(src_tok, NT, w_in[l], 5120, fm, tm, KD, "pa")
            project(memp, NMT // 128, w_mkv[l], 1024, [(0, 4, MKT)], [(0, 512, nmk[l], None), (512, 512, nmv[l], MVB)], KD,
                    "pm")
            if stop == ("proj", l):
                break
            cache_transposes(l)
            attention(l)
            if stop == ("attn", l):
                break
            out_proj(l, src_tok)
            if stop == ("oproj", l):
                break
            with ExitStack() as pph:
                IDX1T = kb.sb(pph, "IDX1T", [128, NTOK], BF16)
                IDX2T = kb.sb(pph, "IDX2T", [128, NTOK], BF16)
                GT = kb.sb(pph, "GT", [128, NTOK], BF16)
                peer_retrieve(l, IDX1T, IDX2T, GT)
                peer_wgen(l, IDX1T, IDX2T, GT)
            if stop == ("wgen", l):
                break
            peer_dense(l, dst_tok, l == L - 1)
            if stop == ("dense", l):
                break
        kb.barrier()
    return nc


def shard_inputs(cfg, inputs, n_cores):
    c = make_consts()
    maps = []
    L = cfg.DEPTH
    u_t = np.ascontiguousarray(np.transpose(inputs["peer_u"], (0, 2, 1)))
    subk = np.ascontiguousarray(np.transpose(inputs["peer_sub_keys"], (0, 1, 2, 4, 3))).reshape(L, 16, 128, 128)
    w_mkv = np.ascontiguousarray(np.concatenate([inputs["w_mem_k"], inputs["w_mem_v"]], axis=2))
    for ci in range(n_cores):
        ps = slice(ci * cfg.NSEQ, (ci + 1) * cfg.NSEQ)
        ss = slice(ci * cfg.NSMP, (ci + 1) * cfg.NSMP)
        xin = np.concatenate([inputs["x_prompt"][ps].reshape(-1, cfg.D), inputs["x_sample"][ss].reshape(-1, cfg.D)], axis=0)
        m = {
            "xin": np.ascontiguousarray(xin),
            "w_in": inputs["w_in"], "w_o": inputs["w_o"], "w_mkv": w_mkv, "w_q": inputs["peer_w_q"],
            "subk": subk, "ut": u_t, "vv": inputs["peer_v"],
            "ck": np.ascontiguousarray(inputs["cache_self_k"][:, ss]),
            "cv": np.ascontiguousarray(inputs["cache_self_v"][:, ss]),
            "cmk": np.ascontiguousarray(inputs["cache_mem_k"][:, ss]).reshape(L, cfg.NSMP, NMEM, 512),
            "cmv": np.ascontiguousarray(inputs["cache_mem_v"][:, ss]).reshape(L, cfg.NSMP, NMEM, 512),
            "memp": np.ascontiguousarray(inputs["mem_prompt"][ps]).reshape(-1, cfg.D),
            "relb": inputs["rel_bias_table"],
            "lamv": inputs["diff_lambda"].reshape(1, 512),
            "subg": inputs["diff_subln_g"].reshape(1, 256),
            "lng": inputs["ln_g"].reshape(L * 2, cfg.D), "lnb": inputs["ln_b"].reshape(L * 2, cfg.D),
        }
        for k, v in c.items():
            m["c_" + k] = v
        maps.append({k: np.ascontiguousarray(v, dtype=np.float32) for k, v in m.items()})
    return maps


def assemble(cfg, results, n_cores):
    L = cfg.DEPTH
    NP = cfg.NP
    yp = np.concatenate([r["y"][:NP].reshape(cfg.NSEQ, cfg.SEQ, cfg.D) for r in results], axis=0)
    ys = np.concatenate([r["y"][NP:].reshape(cfg.NSMP, DEC_SEQ, cfg.D) for r in results], axis=0)
    nkp = np.concatenate([r["nk"][:, :NP].reshape(L, cfg.NSEQ, cfg.SEQ, SW) for r in results], axis=1)
    nvp = np.concatenate([r["nv"][:, :NP].reshape(L, cfg.NSEQ, cfg.SEQ, SW) for r in results], axis=1)
    nks = np.concatenate([r["nk"][:, NP:].reshape(L, cfg.NSMP, DEC_SEQ, SW) for r in results], axis=1)
    nvs = np.concatenate([r["nv"][:, NP:].reshape(L, cfg.NSMP, DEC_SEQ, SW) for r in results], axis=1)
    nmk = np.concatenate([r["nmk"].reshape(L, cfg.NSEQ, NMEM, 4, 128) for r in results], axis=1)
    nmv = np.concatenate([r["nmv"].reshape(L, cfg.NSEQ, NMEM, 4, 128) for r in results], axis=1)
    return (yp, ys, nkp, nvp, nmk, nmv, nks, nvs)


def kernel(**inputs):
    n_cores = 8
    inputs = {k: np.asarray(v) for k, v in inputs.items()}
    B, SEQ, D = inputs["x_prompt"].shape
    DB = inputs["x_sample"].shape[0]
    PAST = inputs["cache_self_k"].shape[2]
    cfg = Cfg(D=D, SEQ=SEQ, NSEQ=B // n_cores, NSMP=DB // n_cores, PAST=PAST, DEPTH=inputs["w_in"].shape[0])
    nc = build(cfg)
    maps = shard_inputs(cfg, inputs, n_cores)
    res = run_bass_kernel_spmd(nc, maps, core_ids=list(range(n_cores)))
    return assemble(cfg, res.results, n_cores)
```
